# Optimizing a Trainium2 kernel written in Bass

```python
import math
import jax, jax.numpy as jnp
from jax import lax
import numpy as np

D_MODEL = 1024
BATCH = 16
SEQ = 2048
DEPTH = 4
DEC_BATCH = 4
DEC_SEQ = 8192
PAST_LEN = 128

GRID_W = 64
QBLK = 128
EPS = 1e-6
N_EVEN = (DEPTH + 1) // 2
N_ODD = DEPTH // 2
D_FF = 2816
ROPE_THETA = 500000.0
AXIAL_THETA = 10000.0
MLA_HEADS = 8
MLA_NOPE = 64
MLA_ROPE = 32
MLA_V = 64
MLA_Q_RANK = 384
MLA_KV_RANK = 256
POOL_WINDOWS = (2, 4, 8, 16)
POOL_GROUP = 128
POOL_WIDTH = POOL_GROUP * len(POOL_WINDOWS)
GQA_HEADS = 8
GQA_KV_HEADS = 2
GQA_DIM = 64
DIFF_HEADS = 4
DIFF_DIM = 64
DIFF_ROPE = DIFF_DIM // 4

EVEN_SIZES = (MLA_Q_RANK, MLA_KV_RANK, MLA_ROPE, POOL_WIDTH)
ODD_SIZES = (GQA_HEADS * GQA_DIM, GQA_KV_HEADS * GQA_DIM, GQA_KV_HEADS * GQA_DIM,
             DIFF_HEADS * 2 * DIFF_DIM, DIFF_HEADS * 2 * DIFF_DIM, DIFF_HEADS * 2 * DIFF_DIM)
EVEN_IN = sum(EVEN_SIZES)
ODD_IN = sum(ODD_SIZES)
EVEN_MIX = MLA_HEADS * MLA_V + POOL_WIDTH
ODD_MIX = GQA_HEADS * GQA_DIM + DIFF_HEADS * 2 * DIFF_DIM

kernel_name = "hybrid_bidir_encoder_mla_pool_gqa_diff"


def _split(z, sizes):
    out, o = [], 0
    for n in sizes:
        out.append(z[..., o:o + n])
        o += n
    return out


def _rmsnorm(x, g):
    xf = x.astype(jnp.float32)
    y = xf * lax.rsqrt(jnp.mean(xf * xf, axis=-1, keepdims=True) + EPS)
    return y.astype(x.dtype) * g


def _rope(x, pos, theta):
    d = x.shape[-1]
    inv = theta ** (-jnp.arange(0, d // 2, dtype=jnp.float32) * 2.0 / d)
    ang = pos.astype(jnp.float32)[:, None] * inv[None, :]
    cos = jnp.cos(ang)[None, :, None, :].astype(x.dtype)
    sin = jnp.sin(ang)[None, :, None, :].astype(x.dtype)
    x1, x2 = x[..., :d // 2], x[..., d // 2:]
    return jnp.concatenate([x1 * cos - x2 * sin, x1 * sin + x2 * cos], axis=-1)


def _partial_rope(x, pos, rd, theta):
    return jnp.concatenate([_rope(x[..., :rd], pos, theta), x[..., rd:]], axis=-1)


def _axial_rope(x, rows, cols, theta):
    h = x.shape[-1] // 2
    return jnp.concatenate([_rope(x[..., :h], rows, theta), _rope(x[..., h:], cols, theta)], axis=-1)


def _to_blocks(a):
    b, s = a.shape[:2]
    return jnp.swapaxes(a.reshape((b, s // QBLK, QBLK) + a.shape[2:]), 0, 1)


def _from_blocks(a):
    nb, b, q = a.shape[:3]
    return jnp.swapaxes(a, 0, 1).reshape((b, nb * q) + a.shape[3:])


def _gqa_attention(q, k, v, scale):
    def blk(qb):
        s = jnp.einsum('bqkgd,bskd->bkgqs', qb, k, preferred_element_type=jnp.float32) * scale
        p = jax.nn.softmax(s, axis=-1).astype(v.dtype)
        return jnp.einsum('bkgqs,bskd->bqkgd', p, v)
    return _from_blocks(lax.map(blk, _to_blocks(q)))


def _diff_attention(q1, q2, k1, k2, v, lam, scale):
    def blk(qs):
        qb1, qb2 = qs
        s1 = jnp.einsum('bqhd,bshd->bhqs', qb1, k1, preferred_element_type=jnp.float32) * scale
        s2 = jnp.einsum('bqhd,bshd->bhqs', qb2, k2, preferred_element_type=jnp.float32) * scale
        a = jax.nn.softmax(s1, axis=-1) - lam * jax.nn.softmax(s2, axis=-1)
        return jnp.einsum('bhqs,bshd->bqhd', a.astype(v.dtype), v)
    return _from_blocks(lax.map(blk, (_to_blocks(q1), _to_blocks(q2))))


def _multiscale_pool(p, w_pool, scale):
    b, s, _ = p.shape
    pf = p.astype(jnp.float32)
    cs = jnp.concatenate([jnp.zeros((b, 1, POOL_WIDTH), jnp.float32), jnp.cumsum(pf, axis=1)], axis=1)
    t = jnp.arange(s)
    outs = []
    for g, w in enumerate(POOL_WINDOWS):
        lo = jnp.clip(t - w // 2, 0, s)
        hi = jnp.clip(t + w - w // 2, 0, s)
        sl = slice(g * POOL_GROUP, (g + 1) * POOL_GROUP)
        csg = cs[..., sl]
        mean = (csg[:, hi] - csg[:, lo]) / (hi - lo).astype(jnp.float32)[None, :, None]
        d = (mean - pf[..., sl]).astype(p.dtype)
        outs.append(jnp.einsum('bsc,cd->bsd', d, w_pool[g]))
    return jnp.concatenate(outs, axis=-1) * scale


def _swiglu(h, w_in, w_out):
    g, u = _split(h @ w_in, (D_FF, D_FF))
    return (jax.nn.silu(g) * u) @ w_out


def _even_mixer(h, pos, w_in, gq, w_uq, gkv, w_ukv, w_pool, pool_scale, w_out):
    b, s, _ = h.shape
    cq, ckv, kpe, pz = _split(h @ w_in, EVEN_SIZES)
    q = (_rmsnorm(cq, gq) @ w_uq).reshape(b, s, MLA_HEADS, MLA_NOPE + MLA_ROPE)
    kv = (_rmsnorm(ckv, gkv) @ w_ukv).reshape(b, s, MLA_HEADS, MLA_NOPE + MLA_V)
    q = jnp.concatenate([q[..., :MLA_NOPE], _rope(q[..., MLA_NOPE:], pos, ROPE_THETA)], axis=-1)
    kpe = _rope(kpe[:, :, None, :], pos, ROPE_THETA)
    k = jnp.concatenate([kv[..., :MLA_NOPE], jnp.broadcast_to(kpe, (b, s, MLA_HEADS, MLA_ROPE))], axis=-1)
    v = kv[..., MLA_NOPE:]
    oa = _gqa_attention(q[:, :, :, None, :], k, v, (MLA_NOPE + MLA_ROPE) ** -0.5)
    oa = oa.reshape(b, s, MLA_HEADS * MLA_V)
    ob = _multiscale_pool(pz, w_pool, pool_scale)
    return jnp.concatenate([oa, ob], axis=-1) @ w_out


def _odd_mixer(h, pos, rows, cols, layer_idx, w_in, gqa_gq, gqa_gk, lq1, lk1, lq2, lk2, subln_g, w_out):
    b, s, _ = h.shape
    qc, kc, vc, qd, kd, vd = _split(h @ w_in, ODD_SIZES)
    qc = _rmsnorm(qc.reshape(b, s, GQA_HEADS, GQA_DIM), gqa_gq)
    kc = _rmsnorm(kc.reshape(b, s, GQA_KV_HEADS, GQA_DIM), gqa_gk)
    qc = _axial_rope(qc, rows, cols, AXIAL_THETA).reshape(b, s, GQA_KV_HEADS, GQA_HEADS // GQA_KV_HEADS, GQA_DIM)
    kc = _axial_rope(kc, rows, cols, AXIAL_THETA)
    vc = vc.reshape(b, s, GQA_KV_HEADS, GQA_DIM)
    oc = _gqa_attention(qc, kc, vc, GQA_DIM ** -0.5).reshape(b, s, GQA_HEADS * GQA_DIM)
    qd = _partial_rope(qd.reshape(b, s, DIFF_HEADS * 2, DIFF_DIM), pos, DIFF_ROPE, ROPE_THETA)
    kd = _partial_rope(kd.reshape(b, s, DIFF_HEADS * 2, DIFF_DIM), pos, DIFF_ROPE, ROPE_THETA)
    qd = qd.reshape(b, s, DIFF_HEADS, 2, DIFF_DIM)
    kd = kd.reshape(b, s, DIFF_HEADS, 2, DIFF_DIM)
    vd = vd.reshape(b, s, DIFF_HEADS, 2 * DIFF_DIM)
    lam_init = 0.8 - 0.6 * math.exp(-0.3 * layer_idx)
    f32 = jnp.float32
    lam = (jnp.exp(jnp.sum(lq1.astype(f32) * lk1.astype(f32)))
           - jnp.exp(jnp.sum(lq2.astype(f32) * lk2.astype(f32))) + lam_init)
    od = _diff_attention(qd[:, :, :, 0], qd[:, :, :, 1], kd[:, :, :, 0], kd[:, :, :, 1], vd, lam, DIFF_DIM ** -0.5)
    od = _rmsnorm(od, subln_g) * (1.0 - lam_init)
    return jnp.concatenate([oc, od.reshape(b, s, DIFF_HEADS * 2 * DIFF_DIM)], axis=-1) @ w_out


def _forward(x, c, p):
    b, s, _ = x.shape
    n_rows = s // GRID_W
    pos = jnp.arange(s)
    rows = jnp.repeat(jnp.arange(n_rows), GRID_W)
    cols = pos % GRID_W
    sc = jax.nn.silu(c)
    for i in range(DEPTH):
        mod = (sc @ p['w_ada'][i] + p['b_ada'][i])[:, None, :]
        sh0, s0, g0, sh1, s1, g1, sh2, s2, g2 = jnp.split(mod, 9, axis=-1)
        h = _rmsnorm(x, p['norm_g'][i, 0]) * (1 + s0) + sh0
        x = x + 0.5 * g0 * _swiglu(h, p['ffn_w_in'][i, 0], p['ffn_w_out'][i, 0])
        h = _rmsnorm(x, p['norm_g'][i, 1]) * (1 + s1) + sh1
        if i % 2 == 0:
            j = i // 2
            m = _even_mixer(h, pos, p['ev_w_in'][j], p['mla_gq'][j], p['mla_w_uq'][j], p['mla_gkv'][j],
                            p['mla_w_ukv'][j], p['pool_w'][j], p['pool_scale'][j], p['ev_w_out'][j])
        else:
            j = i // 2
            m = _odd_mixer(h, pos, rows, cols, i, p['od_w_in'][j], p['gqa_gq'][j], p['gqa_gk'][j],
                           p['diff_lq1'][j], p['diff_lk1'][j], p['diff_lq2'][j], p['diff_lk2'][j],
                           p['diff_subln_g'][j], p['od_w_out'][j])
        x = x + g1 * m
        h = _rmsnorm(x, p['norm_g'][i, 2]) * (1 + s2) + sh2
        x = x + 0.5 * g2 * _swiglu(h, p['ffn_w_in'][i, 1], p['ffn_w_out'][i, 1])
    return _rmsnorm(x, p['final_g'])


def setup_inputs(seed: int = 0) -> dict:
    key = jax.random.key(seed)
    ks = jax.random.split(key, 32)
    f32 = jnp.float32

    def nrm(k, shape, fan):
        return jax.random.normal(k, shape, f32) * (fan ** -0.5)

    def gain(k, shape):
        return 1.0 + 0.02 * jax.random.normal(k, shape, f32)

    D = D_MODEL
    return {
        'x_prompt': jax.random.normal(ks[0], (BATCH, SEQ, D), f32),
        'x_sample': jax.random.normal(ks[1], (DEC_BATCH, DEC_SEQ, D), f32),
        'c_prompt': jax.random.normal(ks[2], (BATCH, D), f32),
        'c_sample': jax.random.normal(ks[3], (DEC_BATCH, D), f32),
        'w_ada': nrm(ks[4], (DEPTH, D, 9 * D), D),
        'b_ada': 0.01 * jax.random.normal(ks[5], (DEPTH, 9 * D), f32),
        'norm_g': gain(ks[6], (DEPTH, 3, D)),
        'ffn_w_in': nrm(ks[7], (DEPTH, 2, D, 2 * D_FF), D),
        'ffn_w_out': nrm(ks[8], (DEPTH, 2, D_FF, D), D_FF),
        'ev_w_in': nrm(ks[9], (N_EVEN, D, EVEN_IN), D),
        'mla_gq': gain(ks[10], (N_EVEN, MLA_Q_RANK)),
        'mla_w_uq': nrm(ks[11], (N_EVEN, MLA_Q_RANK, MLA_HEADS * (MLA_NOPE + MLA_ROPE)), MLA_Q_RANK),
        'mla_gkv': gain(ks[12], (N_EVEN, MLA_KV_RANK)),
        'mla_w_ukv': nrm(ks[13], (N_EVEN, MLA_KV_RANK, MLA_HEADS * (MLA_NOPE + MLA_V)), MLA_KV_RANK),
        'pool_w': nrm(ks[14], (N_EVEN, len(POOL_WINDOWS), POOL_GROUP, POOL_GROUP), POOL_GROUP),
        'pool_scale': gain(ks[15], (N_EVEN, POOL_WIDTH)),
        'ev_w_out': nrm(ks[16], (N_EVEN, EVEN_MIX, D), EVEN_MIX),
        'od_w_in': nrm(ks[17], (N_ODD, D, ODD_IN), D),
        'gqa_gq': gain(ks[18], (N_ODD, GQA_DIM)),
        'gqa_gk': gain(ks[19], (N_ODD, GQA_DIM)),
        'diff_lq1': 0.1 * jax.random.normal(ks[20], (N_ODD, DIFF_DIM), f32),
        'diff_lk1': 0.1 * jax.random.normal(ks[21], (N_ODD, DIFF_DIM), f32),
        'diff_lq2': 0.1 * jax.random.normal(ks[22], (N_ODD, DIFF_DIM), f32),
        'diff_lk2': 0.1 * jax.random.normal(ks[23], (N_ODD, DIFF_DIM), f32),
        'diff_subln_g': gain(ks[24], (N_ODD, 2 * DIFF_DIM)),
        'od_w_out': nrm(ks[25], (N_ODD, ODD_MIX, D), ODD_MIX),
        'final_g': gain(ks[26], (D,)),
    }


def reference(x_prompt, x_sample, c_prompt, c_sample, w_ada, b_ada, norm_g, ffn_w_in, ffn_w_out,
              ev_w_in, mla_gq, mla_w_uq, mla_gkv, mla_w_ukv, pool_w, pool_scale, ev_w_out,
              od_w_in, gqa_gq, gqa_gk, diff_lq1, diff_lk1, diff_lq2, diff_lk2, diff_subln_g, od_w_out,
              final_g):
    p = {
        'w_ada': w_ada, 'b_ada': b_ada, 'norm_g': norm_g, 'ffn_w_in': ffn_w_in, 'ffn_w_out': ffn_w_out,
        'ev_w_in': ev_w_in, 'mla_gq': mla_gq, 'mla_w_uq': mla_w_uq, 'mla_gkv': mla_gkv,
        'mla_w_ukv': mla_w_ukv, 'pool_w': pool_w, 'pool_scale': pool_scale, 'ev_w_out': ev_w_out,
        'od_w_in': od_w_in, 'gqa_gq': gqa_gq, 'gqa_gk': gqa_gk, 'diff_lq1': diff_lq1,
        'diff_lk1': diff_lk1, 'diff_lq2': diff_lq2, 'diff_lk2': diff_lk2,
        'diff_subln_g': diff_subln_g, 'od_w_out': od_w_out, 'final_g': final_g,
    }
    y_prompt = _forward(x_prompt, c_prompt, p)
    y_sample = _forward(x_sample, c_sample, p)
    return (y_prompt, y_sample)
```

```python
import contextlib
import numpy as np
import concourse.bass as bass
import concourse.mybir as mybir
from concourse.bass_utils import run_bass_kernel_spmd

F32 = mybir.dt.float32
BF16 = mybir.dt.bfloat16
AF = mybir.ActivationFunctionType
ALU = mybir.AluOpType
AX = mybir.AxisListType

D = 1024
DFF = 2816
DEPTH = 4
NTOK = 8192
T = 512
NT = NTOK // T
EPS = 1e-6
NCORES = 8
NS = 420

ENGS = ['pe', 'act', 'dve', 'pool', 'sp']


class Buf:
    __slots__ = ('name', 'last_w', 'readers')

    def __init__(self, name):
        self.name = name
        self.last_w = None
        self.readers = []


class Chan:
    __slots__ = ('name', 'sem', 'count', 'nobar')

    def __init__(self, name, sem):
        self.name = name
        self.sem = sem
        self.count = 0
        self.nobar = False


class Op:
    __slots__ = ('eng', 'fn', 'waits', 'signal', 'chan', 'sigidx')

    def __init__(self, eng, fn):
        self.eng = eng
        self.fn = fn
        self.waits = []
        self.signal = False
        self.chan = None
        self.sigidx = 0


class Prog:
    def __init__(self, nc, sems):
        self.nc = nc
        self.free_sems = list(sems)
        self.ops = {e: [] for e in ENGS}
        self.esem = {e: self.free_sems.pop() for e in ENGS}
        self.waited_op = {e: {t: -1 for t in ENGS} for e in ENGS}
        self.waited_ch = {e: {} for e in ENGS}
        self.chans = []
        self.cuts = []
        self.emitted = {e: 0 for e in ENGS}
        self.nsig = {e: 0 for e in ENGS}

    def chan(self, name):
        for c in self.chans:
            if c.name == name:
                return c
        c = Chan(name, self.free_sems.pop())
        self.chans.append(c)
        return c

    def _add_wait(self, op, ref):
        e = op.eng
        if ref[0] == 'op':
            _, te, idx = ref
            if te == e and e in ('pe', 'sp'):
                return
            if self.waited_op[e][te] >= idx:
                return
            self.waited_op[e][te] = idx
            if idx < self.emitted[te] and not self.ops[te][idx].signal:
                raise AssertionError(f"late signal request on emitted op {te}[{idx}] from {e}")
            self.ops[te][idx].signal = True
            op.waits.append(ref)
        else:
            _, ch, cnt = ref
            if self.waited_ch[e].get(ch, 0) >= cnt:
                return
            self.waited_ch[e][ch] = cnt
            op.waits.append(ref)

    def add(self, eng, fn, reads=(), writes=(), chan=None):
        op = Op(eng, fn)
        for b in reads:
            if b.last_w is not None:
                self._add_wait(op, b.last_w)
        for b in writes:
            if b.last_w is not None:
                self._add_wait(op, b.last_w)
            for r in b.readers:
                self._add_wait(op, r)
        idx = len(self.ops[eng])
        self.ops[eng].append(op)
        if chan is not None:
            chan.count += 16
            op.chan = chan
            ref = ('dma', chan, chan.count)
        else:
            ref = ('op', eng, idx)
        for b in writes:
            b.last_w = ref
            b.readers = []
        for b in reads:
            if b in writes:
                continue
            if ref[0] == 'op':
                b.readers = [r for r in b.readers if not (r[0] == 'op' and r[1] == eng)]
            else:
                b.readers = [r for r in b.readers if not (r[0] == 'dma' and r[1] is chan)]
            b.readers.append(ref)
        return ref

    def wait_refs(self, eng, refs):
        op = Op(eng, None)
        for r in refs:
            self._add_wait(op, r)
        self.ops[eng].append(op)

    def barrier(self):
        refs = []
        for e in ('pe', 'act', 'dve'):
            if self.ops[e]:
                for idx in range(len(self.ops[e]) - 1, -1, -1):
                    if self.ops[e][idx].fn is not None and self.ops[e][idx].chan is None:
                        refs.append(('op', e, idx))
                        break
        for c in self.chans:
            if c.count and not c.nobar:
                refs.append(('dma', c, c.count))
        for e in ENGS:
            self.wait_refs(e, refs)

    def cut(self):
        self.cuts.append({e: len(self.ops[e]) for e in ENGS})

    def emit_pending(self, nc):
        prog = self
        for e in ENGS:
            for op in self.ops[e][self.emitted[e]:]:
                if op.signal:
                    self.nsig[e] += 1
                    op.sigidx = self.nsig[e]
        end = {e: len(self.ops[e]) for e in ENGS}
        bounds = [c for c in self.cuts if all(c[e] >= self.emitted[e] for e in ENGS)] + [end]
        self.cuts = []
        prev = dict(self.emitted)

        def run(engname, eng, lo, hi):
            for op in prog.ops[engname][lo:hi]:
                for r in op.waits:
                    if r[0] == 'op':
                        eng.wait_ge(prog.esem[r[1]], prog.ops[r[1]][r[2]].sigidx)
                    else:
                        eng.wait_ge(r[1].sem, r[2])
                if op.fn is None:
                    continue
                inst = op.fn(eng)
                if op.chan is not None:
                    inst.then_inc(op.chan.sem, 16)
                elif op.signal:
                    inst.then_inc(prog.esem[engname], 1)
                op.fn = None

        for bnd in bounds:
            if all(bnd[e] == prev[e] for e in ENGS):
                continue
            with nc.Block() as block:
                if bnd['pe'] > prev['pe']:
                    @block.tensor
                    def _(t, lo=prev['pe'], hi=bnd['pe']):
                        run('pe', t, lo, hi)
                if bnd['act'] > prev['act']:
                    @block.scalar
                    def _(s, lo=prev['act'], hi=bnd['act']):
                        run('act', s, lo, hi)
                if bnd['dve'] > prev['dve']:
                    @block.vector
                    def _(v, lo=prev['dve'], hi=bnd['dve']):
                        run('dve', v, lo, hi)
                if bnd['pool'] > prev['pool']:
                    @block.gpsimd
                    def _(g, lo=prev['pool'], hi=bnd['pool']):
                        run('pool', g, lo, hi)
                if bnd['sp'] > prev['sp']:
                    @block.sync
                    def _(sy, lo=prev['sp'], hi=bnd['sp']):
                        run('sp', sy, lo, hi)
            prev = bnd
        self.emitted = end


def _kblocks(w, cols):
    K = w.shape[0]
    sub = w[:, cols]
    return np.ascontiguousarray(sub.reshape(K // 128, 128, len(cols)).transpose(1, 0, 2)).reshape(128, -1)


def _swap_mla(d):
    return (d + 16) % 32


def _swap_ax(d):
    return (d + 16) % 32 if d < 32 else 32 + ((d - 32) + 16) % 32


def _swap_diff(d):
    return (d + 8) % 16 if d < 16 else d


def _rope_tables(pos, rows, cols):
    f32 = np.float32
    n = pos.shape[0]
    out = np.zeros((6, 128, n), f32)
    posf = pos.astype(f32)
    rowf = rows.astype(f32)
    colf = cols.astype(f32)
    inv_m = (f32(500000.0) ** (-np.arange(0, 16, dtype=f32) * f32(2.0) / f32(32))).astype(f32)
    inv_a = (f32(10000.0) ** (-np.arange(0, 16, dtype=f32) * f32(2.0) / f32(32))).astype(f32)
    inv_d = (f32(500000.0) ** (-np.arange(0, 8, dtype=f32) * f32(2.0) / f32(16))).astype(f32)
    for p in range(128):
        d = p % 32
        ang = (posf * inv_m[d % 16]).astype(f32)
        out[0, p] = np.cos(ang)
        out[1, p] = np.sin(ang) * (f32(-1.0) if d < 16 else f32(1.0))
        d = p % 64
        if d < 32:
            ang = (rowf * inv_a[d % 16]).astype(f32)
            sg = -1.0 if d < 16 else 1.0
        else:
            ang = (colf * inv_a[(d - 32) % 16]).astype(f32)
            sg = -1.0 if (d - 32) < 16 else 1.0
        out[2, p] = np.cos(ang)
        out[3, p] = np.sin(ang) * f32(sg)
        if d < 16:
            ang = (posf * inv_d[d % 8]).astype(f32)
            out[4, p] = np.cos(ang)
            out[5, p] = np.sin(ang) * (f32(-1.0) if d < 8 else f32(1.0))
        else:
            out[4, p] = 1.0
            out[5, p] = 0.0
    return out


def _prep_shared(inp):
    sh = {}
    w_ada = inp['w_ada']
    wada = np.empty((32, 128, 8 * 1152), np.float32)
    for i in range(4):
        for b in range(8):
            wada[i * 8 + b] = _kblocks(w_ada[i], np.arange(b * 1152, (b + 1) * 1152))
    sh['wada'] = wada
    fin = np.empty((8, 11, 128, 4096), np.float32)
    fout = np.empty((8, 4, 128, 5632), np.float32)
    for i in range(4):
        for k in range(2):
            w_in = inp['ffn_w_in'][i, k]
            w_out = inp['ffn_w_out'][i, k]
            for b in range(11):
                cols = np.concatenate([np.arange(256 * b, 256 * b + 256), DFF + np.arange(256 * b, 256 * b + 256)])
                fin[i * 2 + k, b] = _kblocks(w_in, cols)
            for c2 in range(4):
                fout[i * 2 + k, c2] = _kblocks(w_out, np.arange(256 * c2, 256 * c2 + 256))
    sh['ffn_in'] = fin
    sh['ffn_out'] = fout
    ev_in = np.empty((2, 128, 8 * 1216), np.float32)
    ev_uq = np.empty((2, 128, 3 * 1024), np.float32)
    ev_ukv = np.empty((2, 128, 2 * 1024), np.float32)
    ev_out = np.empty((2, 2, 128, 4096), np.float32)
    poolw = np.empty((128, 2 * 4 * 128), np.float32)
    for j in range(2):
        w = inp['ev_w_in'][j]
        kpe = 640 + np.arange(32)
        kpes = 640 + np.array([_swap_mla(d) for d in range(32)])
        cols = np.concatenate([np.arange(0, 640), 672 + np.arange(512), kpe, kpes])
        blocks = [cols[0:512], cols[512:1024], cols[1024:1216]]
        ev_in[j] = np.concatenate([_kblocks(w, b) for b in blocks], axis=1)
        wq = inp['mla_w_uq'][j]
        qc = []
        for h in range(8):
            qc.append(h * 96 + np.arange(64))
        for h in range(8):
            qc.append(h * 96 + 64 + np.arange(32))
        for h in range(8):
            qc.append(h * 96 + 64 + np.array([_swap_mla(d) for d in range(32)]))
        ev_uq[j] = _kblocks(wq, np.concatenate(qc))
        wkv = inp['mla_w_ukv'][j]
        kc = [h * 128 + np.arange(64) for h in range(8)] + [h * 128 + 64 + np.arange(64) for h in range(8)]
        ev_ukv[j] = _kblocks(wkv, np.concatenate(kc))
        wo = inp['ev_w_out'][j]
        for b in range(2):
            ev_out[j, b] = _kblocks(wo, np.arange(512 * b, 512 * b + 512))
        for g in range(4):
            poolw[:, (j * 4 + g) * 128:(j * 4 + g + 1) * 128] = inp['pool_w'][j, g]
    sh['ev_in'] = ev_in
    sh['ev_uq'] = ev_uq
    sh['ev_ukv'] = ev_ukv
    sh['ev_out'] = ev_out
    sh['poolw'] = poolw
    od_in = np.empty((2, 128, 8 * 3328), np.float32)
    od_v = np.empty((2, 128, 8 * 640), np.float32)
    od_out = np.empty((2, 2, 128, 4096), np.float32)
    sw_ax = np.array([_swap_ax(d) for d in range(64)])
    sw_df = np.array([_swap_diff(d) for d in range(64)])
    for j in range(2):
        w = inp['od_w_in'][j]
        chunks = []

        def ch(base, c, sw):
            direct = base + c * 128 + np.arange(128)
            swp = base + c * 128 + np.concatenate([sw, 64 + sw])
            chunks.append(direct)
            chunks.append(swp)

        for c in range(4):
            ch(0, c, sw_ax)
        ch(512, 0, sw_ax)
        for c in range(4):
            ch(768, c, sw_df)
        for c in range(4):
            ch(1280, c, sw_df)
        cols = np.concatenate(chunks)
        blocks = [cols[b * 512:(b + 1) * 512] for b in range(7)]
        od_in[j] = np.concatenate([_kblocks(w, b) for b in blocks], axis=1)
        od_v[j] = np.concatenate([_kblocks(w, 1792 + np.arange(512)), _kblocks(w, 640 + np.arange(128))], axis=1)
        wo = inp['od_w_out'][j]
        for b in range(2):
            od_out[j, b] = _kblocks(wo, np.arange(512 * b, 512 * b + 512))
    sh['od_in'] = od_in
    sh['od_v'] = od_v
    sh['od_out'] = od_out
    sp = np.zeros((128, NS), np.float32)
    for i in range(4):
        sp[:, i * 72:(i + 1) * 72] = inp['b_ada'][i].reshape(72, 128).T
        for n in range(3):
            sp[:, 288 + (i * 3 + n) * 8:288 + (i * 3 + n + 1) * 8] = inp['norm_g'][i, n].reshape(8, 128).T
    sp[:, 384:392] = inp['final_g'].reshape(8, 128).T
    pidx = np.arange(128) % 64
    for j in range(2):
        sp[:, 392 + j * 3:392 + j * 3 + 3] = inp['mla_gq'][j].reshape(3, 128).T
        sp[:, 398 + j * 2:398 + j * 2 + 2] = inp['mla_gkv'][j].reshape(2, 128).T
        sp[:, 402 + j * 4:402 + j * 4 + 4] = inp['pool_scale'][j].reshape(4, 128).T
        sp[:, 410 + j] = inp['gqa_gq'][j][pidx]
        sp[:, 412 + j] = inp['gqa_gq'][j][sw_ax[pidx]]
        sp[:, 414 + j] = inp['gqa_gk'][j][pidx]
        sp[:, 416 + j] = inp['gqa_gk'][j][sw_ax[pidx]]
        sp[:, 418 + j] = inp['diff_subln_g'][j]
    sh['smallp'] = sp
    lv = np.zeros((1, 512), np.float32)
    for j in range(2):
        for q, nm in enumerate(('diff_lq1', 'diff_lk1', 'diff_lq2', 'diff_lk2')):
            lv[0, (j * 4 + q) * 64:(j * 4 + q + 1) * 64] = inp[nm][j]
    sh['lvec'] = lv
    return sh


def _prep_core(inp, r, tabs_s, tabs_p, icnt_s, icnt_p):
    m = {}
    if r < 4:
        x = inp['x_sample'][r]
        cseg = np.stack([inp['c_sample'][r]] * 4)
        m['tabs'] = tabs_s
        m['icnt'] = icnt_s
        m['maskb'] = np.zeros((128, 16), np.float32)
        m['flag'] = np.ones((128, 1), np.float32)
    else:
        q = r - 4
        x = inp['x_prompt'][4 * q:4 * q + 4].reshape(NTOK, D)
        cseg = inp['c_prompt'][4 * q:4 * q + 4]
        m['tabs'] = tabs_p
        m['icnt'] = icnt_p
        mb = np.full((4, 4), -30000.0, np.float32)
        mb[np.arange(4), np.arange(4)] = 0.0
        m['maskb'] = np.ascontiguousarray(np.broadcast_to(mb.reshape(1, 16), (128, 16)))
        m['flag'] = np.zeros((128, 1), np.float32)
    m['xT'] = np.ascontiguousarray(x.T).reshape(8, 128, NTOK)
    m['c4T'] = np.ascontiguousarray(cseg.reshape(4, 8, 128).transpose(2, 1, 0)).reshape(128, 32)
    return m


def _icnt(S):
    t = np.arange(NTOK) % S
    out = np.empty((4, NTOK), np.float32)
    for g, w in enumerate((2, 4, 8, 16)):
        lo = np.clip(t - w // 2, 0, S)
        hi = np.clip(t + w - w // 2, 0, S)
        out[g] = (np.float32(1.0) / (hi - lo).astype(np.float32)).astype(np.float32)
    return out


def build_program():
    nc = bass.Bass("TRN2", target_bir_lowering=False)

    def din(name, shape, dt=F32):
        return nc.dram_tensor(name, list(shape), dt, kind="ExternalInput").ap()

    def dscr(name, shape, dt):
        return nc.dram_tensor(name, list(shape), dt).ap()

    xT_in = din("xT", [8, 128, NTOK])
    c4T_in = din("c4T", [128, 32])
    tabs_in = din("tabs", [6, 128, NTOK])
    maskb_in = din("maskb", [128, 16])
    flag_in = din("flag", [128, 1])
    icnt_in = din("icnt", [4, NTOK])
    smallp_in = din("smallp", [128, NS])
    lvec_in = din("lvec", [1, 512])
    wada_in = din("wada", [32, 128, 9216])
    ffn_in_f = din("ffn_in", [8, 11, 128, 4096])
    ffn_out_f = din("ffn_out", [8, 4, 128, 5632])
    ev_in_f = din("ev_in", [2, 128, 9728])
    ev_uq_f = din("ev_uq", [2, 128, 3072])
    ev_ukv_f = din("ev_ukv", [2, 128, 2048])
    ev_out_f = din("ev_out", [2, 2, 128, 4096])
    poolw_f = din("poolw", [128, 1024])
    od_in_f = din("od_in", [2, 128, 26624])
    od_v_f = din("od_v", [2, 128, 5120])
    od_out_f = din("od_out", [2, 2, 128, 4096])
    yT_out = nc.dram_tensor("yT", [8, 128, NTOK], F32, kind="ExternalOutput").ap()

    ffn_in_b = dscr("ffn_in_b", [8, 11, 128, 4096], BF16)
    ffn_out_b = dscr("ffn_out_b", [8, 4, 128, 5632], BF16)
    ev_in_b = dscr("ev_in_b", [2, 128, 9728], BF16)
    ev_uq_b = dscr("ev_uq_b", [2, 128, 3072], BF16)
    ev_ukv_b = dscr("ev_ukv_b", [2, 128, 2048], BF16)
    ev_out_b = dscr("ev_out_b", [2, 2, 128, 4096], BF16)
    od_in_b = dscr("od_in_b", [2, 128, 26624], BF16)
    od_v_b = dscr("od_v_b", [2, 128, 5120], BF16)
    od_out_b = dscr("od_out_b", [2, 2, 128, 4096], BF16)
    xs_d = dscr("xs_d", [8, 128, NTOK], F32)
    qT_d = dscr("qT_d", [16, 128, NTOK], BF16)
    kT_d = dscr("kT_d", [16, 128, NTOK], BF16)
    kpe_d = dscr("kpe_d", [32, NTOK], BF16)
    v64_d = dscr("v64_d", [8, 128, 64, 64], BF16)
    v128_d = dscr("v128_d", [4, 128, 64, 128], BF16)
    oT_d = dscr("oT_d", [8, 128, NTOK], BF16)
    pz_d = dscr("pz_d", [4, 128, NTOK], F32)

    ARENA_BYTES = 204 * 1024
    with contextlib.ExitStack() as es:
        sems = [es.enter_context(nc.semaphore(f"s{i}")) for i in range(60)]
        P = Prog(nc, sems)

        cur_es = [es]
        uid = [0]
        cursor = [0]

        def alloc(nbytes):
            return None

        def vf32(_off, n):
            uid[0] += 1
            return cur_es[0].enter_context(nc.sbuf_tensor(f"f{uid[0]}", [128, n], F32))[:, :]

        def vbf(_off, n):
            uid[0] += 1
            return cur_es[0].enter_context(nc.sbuf_tensor(f"b{uid[0]}", [128, n], BF16))[:, :]

        PSP = None
        PSB = None

        def alloc_psum():
            nonlocal PSP, PSB
            PSP = []
            for k in range(4):
                uid[0] += 1
                PSP.append(cur_es[0].enter_context(nc.psum_tensor(f"ps{uid[0]}", [128, 1024], F32))[:, :])
            PSB = [Buf(f"ps{b}") for b in range(8)]

        def bank(b):
            return PSP[b // 2][:, (b % 2) * 512:(b % 2) * 512 + 512]

        ones_bf = vbf(alloc(256), 128)
        blk_bf = vbf(alloc(256), 128)
        ones_f = vf32(alloc(512), 128)
        smallp = vf32(alloc(NS * 4), NS)
        c4T = vf32(alloc(128), 32)
        scT = vf32(alloc(128), 32)
        modT = [vf32(alloc(1152), 288) for _ in range(4)]
        Aall = [[vf32(alloc(128), 32) for _ in range(3)] for _ in range(4)]
        Gall = [[vf32(alloc(128), 32) for _ in range(3)] for _ in range(4)]
        maskb = vf32(alloc(64), 16)
        flag = vf32(alloc(32), 1)
        lvt = vf32(alloc(2048), 512)
        lamc = vf32(alloc(64), 16)
        sublng = vf32(alloc(32), 2)
        epsc = vf32(alloc(32), 1)
        poolw = vbf(alloc(2048), 1024)
        B_const = Buf("const")
        B_mod = Buf("mod")

        cst = P.chan("const")

        def cload(dst, src):
            P.add('sp', lambda e: e.dma_start(out=dst, in_=src), writes=[B_const], chan=cst)

        cload(smallp, smallp_in[:, :])
        cload(c4T, c4T_in[:, :])
        cload(maskb, maskb_in[:, :])
        cload(flag, flag_in[:, :])
        cload(lvt, lvec_in[:, :].partition_broadcast(128))
        cvt0 = P.chan("cvt_pw")
        P.add('pool', lambda e: e.dma_start(out=poolw, in_=poolw_f[:, :]), writes=[B_const], chan=cvt0)
        P.add('dve', lambda e: e.memset(ones_bf, 1.0), writes=[B_const])
        P.add('dve', lambda e: e.memset(ones_f, 1.0), writes=[B_const])
        P.add('dve', lambda e: e.memset(epsc, EPS), writes=[B_const])
        P.add('dve', lambda e: e.memset(blk_bf, 0.0), writes=[B_const])
        P.add('dve', lambda e: e.memset(blk_bf[0:64, 0:64], 1.0), writes=[B_const])
        P.add('dve', lambda e: e.memset(blk_bf[64:128, 64:128], 1.0), writes=[B_const])

        WD = [Buf(f"wd{i}") for i in range(4)]
        for i in range(4):
            ch = P.chan(f"cvt{i}")
            ch.nobar = True
            j = i // 2

            def cv(dst, src, ch=ch, i=i):
                P.add('pool', lambda e: e.dma_start(out=dst, in_=src), writes=[WD[i]], chan=ch)

            for b in range(11):
                cv(ffn_in_b[i * 2, b], ffn_in_f[i * 2, b])
            for c2 in range(4):
                cv(ffn_out_b[i * 2, c2], ffn_out_f[i * 2, c2])
            if i % 2 == 0:
                cv(ev_in_b[j], ev_in_f[j])
                cv(ev_uq_b[j], ev_uq_f[j])
                cv(ev_ukv_b[j], ev_ukv_f[j])
                for b in range(2):
                    cv(ev_out_b[j, b], ev_out_f[j, b])
            else:
                for b in range(7):
                    n = 4096 if b < 6 else 2048
                    cv(od_in_b[j][:, b * 4096:b * 4096 + n], od_in_f[j][:, b * 4096:b * 4096 + n])
                cv(od_v_b[j], od_v_f[j])
                for b in range(2):
                    cv(od_out_b[j, b], od_out_f[j, b])
            for b in range(11):
                cv(ffn_in_b[i * 2 + 1, b], ffn_in_f[i * 2 + 1, b])
            for c2 in range(4):
                cv(ffn_out_b[i * 2 + 1, c2], ffn_out_f[i * 2 + 1, c2])
            WD[i].last_w = ('dma', ch, ch.count)
            WD[i].readers = []

        pro_es = contextlib.ExitStack()
        cur_es[0] = pro_es
        alloc_psum()
        B_const.last_w = ('dma', cst, cst.count)
        P.wait_refs('dve', [('dma', cvt0, cvt0.count)])
        P.add('act', lambda e: e.activation(out=scT, in_=c4T, func=AF.Silu), reads=[B_const], writes=[B_mod])
        ada_off = [alloc(8 * 1152 * 4) for _ in range(2)]
        ada_buf = [vf32(o, 9216) for o in ada_off]
        ada_B = [Buf("ada0"), Buf("ada1")]
        ada_ch = [P.chan("ada0"), P.chan("ada1")]
        scT3 = scT.rearrange("p (k s) -> p k s", s=4)
        nb = 0
        for i in range(4):
            for b in range(8):
                sl = nb % 2
                wb = ada_buf[sl].rearrange("p (k n) -> p k n", n=1152)
                P.add('sp', lambda e, sl=sl, i=i, b=b: e.dma_start(out=ada_buf[sl], in_=wada_in[i * 8 + b]),
                      writes=[ada_B[sl]], chan=ada_ch[sl])
                pb = nb % 2
                for jj in range(9):
                    for kc in range(8):
                        P.add('pe', lambda e, wb=wb, jj=jj, kc=kc, pb=pb: e.matmul(
                            bank(pb)[:, jj * 4:jj * 4 + 4], wb[:, kc, jj * 128:(jj + 1) * 128], scT3[:, kc, :],
                            start=(kc == 0), stop=(kc == 7)),
                            reads=[ada_B[sl], B_mod], writes=[PSB[pb]])
                for jj in range(9):
                    jcol = b * 9 + jj
                    P.add('dve', lambda e, i=i, jj=jj, jcol=jcol, pb=pb: e.tensor_scalar(
                        out=modT[i][:, jcol * 4:jcol * 4 + 4], in0=bank(pb)[:, jj * 4:jj * 4 + 4],
                        scalar1=smallp[:, i * 72 + jcol:i * 72 + jcol + 1], scalar2=None, op0=ALU.add),
                        reads=[PSB[pb], B_const], writes=[B_mod])
                nb += 1
        for i in range(4):
            for n in range(3):
                for c in range(8):
                    js = (3 * n + 1) * 8 + c
                    P.add('dve', lambda e, i=i, n=n, c=c, js=js: e.tensor_scalar(
                        out=Aall[i][n][:, c * 4:c * 4 + 4], in0=modT[i][:, js * 4:js * 4 + 4],
                        scalar1=1.0, scalar2=smallp[:, 288 + (i * 3 + n) * 8 + c:288 + (i * 3 + n) * 8 + c + 1],
                        op0=ALU.add, op1=ALU.mult), reads=[B_mod, B_const], writes=[B_mod])
                jg = (3 * n + 2) * 8
                P.add('dve', lambda e, i=i, n=n, jg=jg: e.tensor_scalar(
                    out=Gall[i][n], in0=modT[i][:, jg * 4:jg * 4 + 32],
                    scalar1=(1.0 if n == 1 else 0.5), scalar2=None, op0=ALU.mult),
                    reads=[B_mod], writes=[B_mod])
        lam_tmp = vf32(alloc(256), 64)
        for j in range(2):
            li = 2 * j + 1
            lam_init = 0.8 - 0.6 * float(np.exp(-0.3 * li))
            for q in range(2):
                a = lvt[:, (j * 4 + 2 * q) * 64:(j * 4 + 2 * q + 1) * 64]
                b_ = lvt[:, (j * 4 + 2 * q + 1) * 64:(j * 4 + 2 * q + 2) * 64]
                P.add('dve', lambda e, a=a, b_=b_: e.tensor_tensor(out=lam_tmp, in0=a, in1=b_, op=ALU.mult),
                      reads=[B_const, B_mod], writes=[B_mod])
                P.add('dve', lambda e, j=j, q=q: e.reduce_sum(out=lamc[:, j * 4 + 2 + q:j * 4 + 3 + q], in_=lam_tmp, axis=AX.X),
                      reads=[B_mod], writes=[B_mod])
            P.add('act', lambda e, j=j: e.activation(out=lamc[:, j * 4 + 2:j * 4 + 4], in_=lamc[:, j * 4 + 2:j * 4 + 4], func=AF.Exp),
                  reads=[B_mod], writes=[B_mod])
            P.add('dve', lambda e, j=j, lam_init=lam_init: e.scalar_tensor_tensor(
                out=lamc[:, j * 4 + 1:j * 4 + 2], in0=lamc[:, j * 4 + 3:j * 4 + 4], scalar=-lam_init,
                in1=lamc[:, j * 4 + 2:j * 4 + 3], op0=ALU.add, op1=ALU.subtract), reads=[B_mod], writes=[B_mod])
            P.add('dve', lambda e, j=j, lam_init=lam_init: e.tensor_scalar(
                out=sublng[:, j:j + 1], in0=smallp[:, 418 + j:419 + j], scalar1=(1.0 - lam_init), scalar2=None, op0=ALU.mult),
                reads=[B_mod, B_const], writes=[B_mod])
        P.barrier()
        P.emit_pending(nc)
        pro_es.close()

        tmp = XTall = XT = XB = HTall = HT = HB = AT = AB = SQ = SQB = NTMP = TMP = TMPB = tmp_i = RT = RTB = RSTD = RSTDB = NRA = RA = RAB = RACH = ra_i = NRB = RBv = RBB = RBCH = rb_i = OTall = OT = OTB = OTCH = TAB = TABB = TABCH = NST = STG = STGB = STGCH = stg_i = VST = VSTB = VSTCH = PZS = PZSB = PZSCH = CQN = CQNB = XCH = XSCH = TOKEN_END = PSP = PSB = None

        def alloc_token():
            nonlocal tmp, XTall, XT, XB, HTall, HT, HB, AT, AB, SQ, SQB, NTMP, TMP, TMPB, tmp_i, RT, RTB, RSTD, RSTDB, NRA, RA, RAB, RACH, ra_i, NRB, RBv, RBB, RBCH, rb_i, OTall, OT, OTB, OTCH, TAB, TABB, TABCH, NST, STG, STGB, STGCH, stg_i, VST, VSTB, VSTCH, PZS, PZSB, PZSCH, CQN, CQNB, XCH, XSCH, TOKEN_END, PSP, PSB
            alloc_psum()
            XTall = vf32(None, 8 * T)
            XT = [XTall[:, c * T:(c + 1) * T] for c in range(8)]
            XB = [Buf(f"x{c}") for c in range(8)]
            HTall = vbf(None, 8 * T)
            HT = [HTall[:, c * T:(c + 1) * T] for c in range(8)]
            HB = [Buf(f"h{c}") for c in range(8)]
            AT = [vbf(None, T) for f in range(22)]
            AB = [Buf(f"a{f}") for f in range(22)]
            SQ = [vbf(None, T) for c in range(8)]
            SQB = [Buf(f"sq{c}") for c in range(8)]
            NTMP = 4
            TMP = [vf32(alloc(T * 4), T) for _ in range(NTMP)]
            TMPB = [Buf(f"tmp{k}") for k in range(NTMP)]
            tmp_i = [0]

            def tmp():
                k = tmp_i[0] % NTMP
                tmp_i[0] += 1
                return TMP[k], TMPB[k]

            RT = vf32(alloc(T * 4), T)
            RTB = Buf("rt")
            RSTD = vf32(alloc(T * 4), T)
            RSTDB = Buf("rstd")
            NRA = 4
            RA = [vbf(alloc(4096 * 2), 4096) for _ in range(NRA)]
            RAB = [Buf(f"ra{k}") for k in range(NRA)]
            RACH = [P.chan(f"ra{k}") for k in range(NRA)]
            ra_i = [0]
            NRB = 2
            RBv = [vbf(alloc(5632 * 2), 5632) for _ in range(NRB)]
            RBB = [Buf(f"rb{k}") for k in range(NRB)]
            RBCH = [P.chan(f"rb{k}") for k in range(NRB)]
            rb_i = [0]
            OTall = vbf(None, 8 * T)
            OT = [OTall[:, c * T:(c + 1) * T] for c in range(8)]
            OTB = Buf("ot")
            OTCH = P.chan("ot")
            TAB = [vf32(alloc(T * 4), T) for _ in range(4)]
            TABB = Buf("tab")
            TABCH = P.chan("tab")
            NST = 4
            STG = [vbf(alloc(T * 2), T) for _ in range(NST)]
            STGB = [Buf(f"stg{k}") for k in range(NST)]
            STGCH = [P.chan(f"stg{k}") for k in range(NST)]
            stg_i = [0]
            VST = vbf(alloc(4 * 640 * 2), 4 * 640)
            VSTB = Buf("vst")
            VSTCH = P.chan("vst")
            PZS = vf32(alloc(4 * T * 4), 4 * T)
            PZSB = Buf("pzs")
            PZSCH = P.chan("pzs")
            CQN = [vbf(alloc(T * 2), T) for _ in range(3)]
            CQNB = [Buf(f"cqn{k}") for k in range(3)]
            XCH = P.chan("xld")
            XSCH = P.chan("xst")
            TOKEN_END = cursor[0]


        def stg():
            k = stg_i[0] % NST
            stg_i[0] += 1
            return STG[k], STGB[k], STGCH[k]

        def ringA(src, n, wd):
            k = ra_i[0] % NRA
            ra_i[0] += 1
            dst = RA[k][:, 0:n]
            P.add('sp', lambda e: e.dma_start(out=dst, in_=src), reads=[wd], writes=[RAB[k]], chan=RACH[k])
            return RA[k], RAB[k]

        def ringB(src, wd):
            k = rb_i[0] % NRB
            rb_i[0] += 1
            dst = RBv[k]
            P.add('sp', lambda e: e.dma_start(out=dst, in_=src), reads=[wd], writes=[RBB[k]], chan=RBCH[k])
            return RBv[k], RBB[k]

        def mm(bk, lhsT, rhs, start, stop, reads, rows=None):
            out = bank(bk) if rows is None else bank(bk)[rows[0]:rows[1], :]
            P.add('pe', lambda e: e.matmul(out, lhsT, rhs, start=start, stop=stop), reads=reads, writes=[PSB[bk]])

        def sqrt_recip(bk, inv_n, rows=(0, 128)):
            r0, r1 = rows
            P.add('act', lambda e: e.activation(out=RT[r0:r1, :], in_=bank(bk)[r0:r1, :], func=AF.Sqrt,
                                                bias=epsc[r0:r1, :], scale=inv_n),
                  reads=[PSB[bk], B_const], writes=[RTB])
            P.add('dve', lambda e: e.reciprocal(out=RSTD[r0:r1, :], in_=RT[r0:r1, :]), reads=[RTB], writes=[RSTDB])

        def norm_main(Acols, Bcols, s):
            for c in range(8):
                P.add('act', lambda e, c=c: e.activation(out=SQ[c], in_=XT[c], func=AF.Square),
                      reads=[XB[c]], writes=[SQB[c]])
            for c in range(8):
                mm(6, ones_bf, SQ[c], c == 0, c == 7, [SQB[c], B_const])
            sqrt_recip(6, 1.0 / D)
            for c in range(8):
                tv, tb = tmp()
                P.add('dve', lambda e, c=c, tv=tv: e.scalar_tensor_tensor(
                    out=tv, in0=XT[c], scalar=Acols[:, c * 4 + s:c * 4 + s + 1], in1=RSTD, op0=ALU.mult, op1=ALU.mult),
                    reads=[XB[c], RSTDB, B_mod], writes=[tb])
                if Bcols is None:
                    continue
                P.add('act', lambda e, c=c, tv=tv: e.activation(out=HT[c], in_=tv, func=AF.Identity,
                                                                bias=Bcols[:, c * 4 + s:c * 4 + s + 1], scale=1.0),
                      reads=[tb, B_mod], writes=[HB[c]])

        def ffn(i, k, s):
            G = Gall[i][0 if k == 0 else 2]
            pp = 0
            for b in range(11):
                wv, wb = ringA(ffn_in_b[i * 2 + k, b], 4096, WD[i])
                w3 = wv.rearrange("p (k n) -> p k n", n=512)
                for j in range(2):
                    f = 2 * b + j
                    bg, bu = (0, 1) if pp % 2 == 0 else (2, 3)
                    pp += 1
                    for kc in range(8):
                        mm(bg, w3[:, kc, j * 128:(j + 1) * 128], HT[kc], kc == 0, kc == 7, [wb, HB[kc]])
                    for kc in range(8):
                        mm(bu, w3[:, kc, 256 + j * 128:256 + (j + 1) * 128], HT[kc], kc == 0, kc == 7, [wb, HB[kc]])
                    tv, tb = tmp()
                    P.add('act', lambda e, bg=bg, tv=tv: e.activation(out=tv, in_=bank(bg), func=AF.Silu),
                          reads=[PSB[bg]], writes=[tb])
                    P.add('dve', lambda e, bu=bu, tv=tv, f=f: e.tensor_tensor(out=AT[f], in0=bank(bu), in1=tv, op=ALU.mult),
                          reads=[PSB[bu], tb], writes=[AB[f]])
            for c2 in range(4):
                wv, wb = ringB(ffn_out_b[i * 2 + k, c2], WD[i])
                w3 = wv.rearrange("p (f n) -> p f n", n=256)
                for cc in range(2):
                    c = 2 * c2 + cc
                    by = 4 + (c % 2)
                    for f in range(22):
                        mm(by, w3[:, f, cc * 128:(cc + 1) * 128], AT[f], f == 0, f == 21, [wb, AB[f]])
                    P.add('dve', lambda e, c=c, by=by: e.scalar_tensor_tensor(
                        out=XT[c], in0=bank(by), scalar=G[:, c * 4 + s:c * 4 + s + 1], in1=XT[c], op0=ALU.mult, op1=ALU.add),
                        reads=[PSB[by], B_mod], writes=[XB[c]])

        def outproj(i, t, s):
            j = i // 2
            wsrc = ev_out_b if i % 2 == 0 else od_out_b
            G = Gall[i][1]
            for b in range(2):
                wv, wb = ringA(wsrc[j, b], 4096, WD[i])
                w3 = wv.rearrange("p (k n) -> p k n", n=512)
                for cc in range(4):
                    c = 4 * b + cc
                    by = 4 + (c % 2)
                    for kc in range(8):
                        mm(by, w3[:, kc, cc * 128:(cc + 1) * 128], OT[kc], kc == 0, kc == 7, [wb, OTB])
                    P.add('dve', lambda e, c=c, by=by: e.scalar_tensor_tensor(
                        out=XT[c], in0=bank(by), scalar=G[:, c * 4 + s:c * 4 + s + 1], in1=XT[c], op0=ALU.mult, op1=ALU.add),
                        reads=[PSB[by], B_mod], writes=[XB[c]])

        def store_rows(sv, sb, sch, pieces, t):
            for (r0, r1, dap) in pieces:
                P.add('sp', lambda e, r0=r0, r1=r1, dap=dap: e.dma_start(out=dap[:, t * T:(t + 1) * T], in_=sv[r0:r1, :]),
                      reads=[sb], chan=sch)

        def load_tabs(t, which):
            for q, w in enumerate(which):
                P.add('sp', lambda e, q=q, w=w: e.dma_start(out=TAB[q], in_=tabs_in[w][:, t * T:(t + 1) * T]),
                      writes=[TABB], chan=TABCH)

        def rope_combine(bd, bs, rows, cosT, sinT, out_ap, out_b, gcols=None, rstd=False):
            r0, r1 = rows
            t1, b1 = tmp()
            t2, b2 = tmp()
            if gcols is None:
                P.add('dve', lambda e: e.tensor_tensor(out=t1[r0:r1, :], in0=bank(bd)[r0:r1, :], in1=cosT[r0:r1, :], op=ALU.mult),
                      reads=[PSB[bd], TABB], writes=[b1])
                P.add('dve', lambda e: e.tensor_tensor(out=t2[r0:r1, :], in0=bank(bs)[r0:r1, :], in1=sinT[r0:r1, :], op=ALU.mult),
                      reads=[PSB[bs], TABB], writes=[b2])
            else:
                g, gs = gcols
                P.add('dve', lambda e: e.scalar_tensor_tensor(out=t1[r0:r1, :], in0=bank(bd)[r0:r1, :], scalar=g[r0:r1, :],
                                                              in1=cosT[r0:r1, :], op0=ALU.mult, op1=ALU.mult),
                      reads=[PSB[bd], TABB, B_const], writes=[b1])
                P.add('dve', lambda e: e.scalar_tensor_tensor(out=t2[r0:r1, :], in0=bank(bs)[r0:r1, :], scalar=gs[r0:r1, :],
                                                              in1=sinT[r0:r1, :], op0=ALU.mult, op1=ALU.mult),
                      reads=[PSB[bs], TABB, B_const], writes=[b2])
            if not rstd:
                P.add('dve', lambda e: e.tensor_tensor(out=out_ap[r0:r1, :], in0=t1[r0:r1, :], in1=t2[r0:r1, :], op=ALU.add),
                      reads=[b1, b2], writes=[out_b])
            else:
                P.add('dve', lambda e: e.tensor_tensor(out=t1[r0:r1, :], in0=t1[r0:r1, :], in1=t2[r0:r1, :], op=ALU.add),
                      reads=[b2], writes=[b1])
                P.add('dve', lambda e: e.tensor_tensor(out=out_ap[r0:r1, :], in0=t1[r0:r1, :], in1=RSTD[r0:r1, :], op=ALU.mult),
                      reads=[b1, RSTDB], writes=[out_b])

        def sub_norm(banks, gbase, nfeat):
            n = len(banks)
            for k, bk in enumerate(banks):
                P.add('act', lambda e, k=k, bk=bk: e.activation(out=SQ[k], in_=bank(bk), func=AF.Square),
                      reads=[PSB[bk]], writes=[SQB[k]])
            for k in range(n):
                mm(6, ones_bf, SQ[k], k == 0, k == n - 1, [SQB[k], B_const])
            sqrt_recip(6, 1.0 / nfeat)
            for k, bk in enumerate(banks):
                P.add('dve', lambda e, k=k, bk=bk: e.scalar_tensor_tensor(
                    out=CQN[k], in0=bank(bk), scalar=smallp[:, gbase + k:gbase + k + 1], in1=RSTD, op0=ALU.mult, op1=ALU.mult),
                    reads=[PSB[bk], RSTDB, B_const], writes=[CQNB[k]])

        def inproj_even(i, t):
            j = i // 2
            load_tabs(t, (0, 1))
            W = []
            offs = [(0, 4096), (4096, 4096), (8192, 1536)]
            ncol = [512, 512, 192]

            def getw(b):
                wv, wb = ringA(ev_in_b[j][:, offs[b][0]:offs[b][0] + offs[b][1]], offs[b][1], WD[i])
                return wv[:, 0:offs[b][1]].rearrange("p (k n) -> p k n", n=ncol[b]), wb

            def proj(bk, w3, wb, c0, c1, rows=None):
                for kc in range(8):
                    mm(bk, w3[:, kc, c0:c1], HT[kc], kc == 0, kc == 7, [wb, HB[kc]], rows=rows)

            w3, wb = getw(0)
            for c in range(3):
                proj(c, w3, wb, c * 128, (c + 1) * 128)
            proj(3, w3, wb, 384, 512)
            w3, wb = getw(1)
            proj(4, w3, wb, 0, 128)
            pzb = [5, 7, 5, 7]
            for g in range(3):
                proj(pzb[g], w3, wb, 128 + g * 128, 256 + g * 128)
                P.add('act', lambda e, g=g: e.activation(out=PZS[:, g * T:(g + 1) * T], in_=bank(pzb[g]), func=AF.Copy),
                      reads=[PSB[pzb[g]]], writes=[PZSB])
            w3, wb = getw(2)
            proj(7, w3, wb, 0, 128)
            P.add('act', lambda e: e.activation(out=PZS[:, 3 * T:4 * T], in_=bank(7), func=AF.Copy),
                  reads=[PSB[7]], writes=[PZSB])
            P.add('sp', lambda e: e.dma_start(out=pz_d[:, :, t * T:(t + 1) * T].rearrange("g p n -> p g n"),
                                              in_=PZS.rearrange("p (g n) -> p g n", n=T)), reads=[PZSB], chan=PZSCH)
            proj(5, w3, wb, 128, 160, rows=(0, 32))
            proj(7, w3, wb, 160, 192, rows=(0, 32))
            sv, sb, sch = stg()
            rope_combine(5, 7, (0, 32), TAB[0], TAB[1], sv, sb)
            store_rows(sv, sb, sch, [(0, 32, kpe_d)], t)
            sub_norm([0, 1, 2], 392 + j * 3, 384)
            wv, wb = ringA(ev_uq_b[j], 3072, WD[i])
            w3 = wv[:, 0:3072].rearrange("p (k n) -> p k n", n=1024)
            for cn in range(4):
                bk = cn % 2
                for kc in range(3):
                    mm(bk, w3[:, kc, cn * 128:(cn + 1) * 128], CQN[kc], kc == 0, kc == 2, [wb, CQNB[kc]])
                sv, sb, sch = stg()
                P.add('act', lambda e, bk=bk, sv=sv: e.activation(out=sv, in_=bank(bk), func=AF.Copy),
                      reads=[PSB[bk]], writes=[sb])
                store_rows(sv, sb, sch, [(0, 64, qT_d[2 * cn][0:64]), (64, 128, qT_d[2 * cn + 1][0:64])], t)
            for cr in range(2):
                for kc in range(3):
                    mm(0, w3[:, kc, 512 + cr * 128:512 + (cr + 1) * 128], CQN[kc], kc == 0, kc == 2, [wb, CQNB[kc]])
                for kc in range(3):
                    mm(1, w3[:, kc, 768 + cr * 128:768 + (cr + 1) * 128], CQN[kc], kc == 0, kc == 2, [wb, CQNB[kc]])
                sv, sb, sch = stg()
                rope_combine(0, 1, (0, 128), TAB[0], TAB[1], sv, sb)
                store_rows(sv, sb, sch, [(hh * 32, hh * 32 + 32, qT_d[4 * cr + hh][64:96]) for hh in range(4)], t)
            sub_norm([3, 4], 398 + j * 2, 256)
            wv, wb = ringA(ev_ukv_b[j], 2048, WD[i])
            w3 = wv[:, 0:2048].rearrange("p (k n) -> p k n", n=1024)
            for ck in range(4):
                bk = ck % 2
                for kc in range(2):
                    mm(bk, w3[:, kc, ck * 128:(ck + 1) * 128], CQN[kc], kc == 0, kc == 1, [wb, CQNB[kc]])
                sv, sb, sch = stg()
                P.add('act', lambda e, bk=bk, sv=sv: e.activation(out=sv, in_=bank(bk), func=AF.Copy),
                      reads=[PSB[bk]], writes=[sb])
                store_rows(sv, sb, sch, [(0, 64, kT_d[2 * ck][0:64]), (64, 128, kT_d[2 * ck + 1][0:64])], t)
            vst3 = VST.rearrange("p (b n) -> p b n", n=640)
            for tb_ in range(4):
                bk = 2 + tb_ % 2
                for kc in range(2):
                    mm(bk, CQN[kc][:, tb_ * 128:(tb_ + 1) * 128], w3[:, kc, 512:1024], kc == 0, kc == 1, [wb, CQNB[kc]])
                P.add('act', lambda e, bk=bk, tb_=tb_: e.activation(out=vst3[:, tb_, 0:512], in_=bank(bk), func=AF.Copy),
                      reads=[PSB[bk]], writes=[VSTB])
            for h in range(8):
                P.add('sp', lambda e, h=h: e.dma_start(out=v64_d[h][:, t * 4:(t + 1) * 4, :], in_=vst3[:, :, h * 64:(h + 1) * 64]),
                      reads=[VSTB], chan=VSTCH)

        def inproj_odd(i, t):
            j = i // 2
            load_tabs(t, (2, 3, 4, 5))
            gq = (smallp[:, 410 + j:411 + j], smallp[:, 412 + j:413 + j])
            gk = (smallp[:, 414 + j:415 + j], smallp[:, 416 + j:417 + j])
            seq = [('qc', c) for c in range(4)] + [('kc', 0)] + [('qd', c) for c in range(4)] + [('kd', c) for c in range(4)]
            cur = {'b': -1, 'w3': None, 'wb': None}

            def wcols(ci):
                b = ci // 4
                if b != cur['b']:
                    n = 4096 if b < 6 else 2048
                    wv, wb = ringA(od_in_b[j][:, b * 4096:b * 4096 + n], n, WD[i])
                    cur['b'] = b
                    cur['w3'] = wv[:, 0:n].rearrange("p (k n) -> p k n", n=n // 8)
                    cur['wb'] = wb
                return cur['w3'], cur['wb'], (ci % 4) * 128

            pp = 0
            for si, (kind, idx) in enumerate(seq):
                bd, bs = (0, 1) if pp % 2 == 0 else (2, 3)
                pp += 1
                for q, bk in enumerate((bd, bs)):
                    w3, wb, c0 = wcols(2 * si + q)
                    for kc in range(8):
                        mm(bk, w3[:, kc, c0:c0 + 128], HT[kc], kc == 0, kc == 7, [wb, HB[kc]])
                sv, sb, sch = stg()
                if kind in ('qc', 'kc'):
                    P.add('act', lambda e, bd=bd: e.activation(out=SQ[0], in_=bank(bd), func=AF.Square),
                          reads=[PSB[bd]], writes=[SQB[0]])
                    mm(6, blk_bf, SQ[0], True, True, [SQB[0], B_const])
                    sqrt_recip(6, 1.0 / 64)
                    rope_combine(bd, bs, (0, 128), TAB[0], TAB[1], sv, sb, gcols=(gq if kind == 'qc' else gk), rstd=True)
                    if kind == 'qc':
                        pieces = [(0, 64, qT_d[2 * idx][0:64]), (64, 128, qT_d[2 * idx + 1][0:64])]
                    else:
                        pieces = [(0, 64, kT_d[0][0:64]), (64, 128, kT_d[1][0:64])]
                else:
                    rope_combine(bd, bs, (0, 128), TAB[2], TAB[3], sv, sb)
                    if kind == 'qd':
                        pieces = [(0, 64, qT_d[8 + 2 * idx][0:64]), (64, 128, qT_d[8 + 2 * idx + 1][0:64])]
                    else:
                        pieces = [(0, 64, kT_d[2 + 2 * idx][0:64]), (64, 128, kT_d[2 + 2 * idx + 1][0:64])]
                store_rows(sv, sb, sch, pieces, t)
            wv, wb = ringA(od_v_b[j][:, 0:4096], 4096, WD[i])
            w3 = wv.rearrange("p (k n) -> p k n", n=512)
            wv2, wb2 = ringA(od_v_b[j][:, 4096:5120], 1024, WD[i])
            w32 = wv2[:, 0:1024].rearrange("p (k n) -> p k n", n=128)
            vst3 = VST.rearrange("p (b n) -> p b n", n=640)
            for tb_ in range(4):
                bk = tb_ % 2
                for kc in range(8):
                    mm(bk, HT[kc][:, tb_ * 128:(tb_ + 1) * 128], w3[:, kc, :], kc == 0, kc == 7, [wb, HB[kc]])
                P.add('act', lambda e, bk=bk, tb_=tb_: e.activation(out=vst3[:, tb_, 0:512], in_=bank(bk), func=AF.Copy),
                      reads=[PSB[bk]], writes=[VSTB])
                bk2 = 2 + tb_ % 2
                for kc in range(8):
                    P.add('pe', lambda e, kc=kc, bk2=bk2, tb_=tb_: e.matmul(
                        bank(bk2)[:, 0:128], HT[kc][:, tb_ * 128:(tb_ + 1) * 128], w32[:, kc, :], start=(kc == 0), stop=(kc == 7)),
                        reads=[wb2, HB[kc]], writes=[PSB[bk2]])
                P.add('act', lambda e, bk2=bk2, tb_=tb_: e.activation(out=vst3[:, tb_, 512:640], in_=bank(bk2)[:, 0:128], func=AF.Copy),
                      reads=[PSB[bk2]], writes=[VSTB])
            for h in range(4):
                P.add('sp', lambda e, h=h: e.dma_start(out=v128_d[h][:, t * 4:(t + 1) * 4, :], in_=vst3[:, :, h * 128:(h + 1) * 128]),
                      reads=[VSTB], chan=VSTCH)
            for h in range(2):
                P.add('sp', lambda e, h=h: e.dma_start(out=v64_d[h][:, t * 4:(t + 1) * 4, :], in_=vst3[:, :, 512 + h * 64:512 + (h + 1) * 64]),
                      reads=[VSTB], chan=VSTCH)

        def token_phase(ph):
            for t in range(NT):
                P.cut()
                s = t // 4
                src = xT_in if ph == 0 else xs_d
                P.add('sp', lambda e, src=src, t=t: e.dma_start(
                    out=XTall.rearrange("p (c n) -> p c n", n=T), in_=src[:, :, t * T:(t + 1) * T].rearrange("c p n -> p c n")),
                    writes=XB, chan=XCH)
                if ph > 0:
                    i = ph - 1
                    P.add('sp', lambda e, t=t: e.dma_start(
                        out=OTall.rearrange("p (c n) -> p c n", n=T), in_=oT_d[:, :, t * T:(t + 1) * T].rearrange("c p n -> p c n")),
                        writes=[OTB], chan=OTCH)
                    outproj(i, t, s)
                    norm_main(Aall[i][2], modT[i][:, 6 * 32:7 * 32], s)
                    ffn(i, 1, s)
                if ph < 4:
                    i = ph
                    norm_main(Aall[i][0], modT[i][:, 0:32], s)
                    ffn(i, 0, s)
                    norm_main(Aall[i][1], modT[i][:, 3 * 32:4 * 32], s)
                    if i % 2 == 0:
                        inproj_even(i, t)
                    else:
                        inproj_odd(i, t)
                    P.add('sp', lambda e, t=t: e.dma_start(
                        out=xs_d[:, :, t * T:(t + 1) * T].rearrange("c p n -> p c n"), in_=XTall.rearrange("p (c n) -> p c n", n=T)),
                        reads=XB, chan=XSCH)
                else:
                    for c in range(8):
                        P.add('act', lambda e, c=c: e.activation(out=SQ[c], in_=XT[c], func=AF.Square),
                              reads=[XB[c]], writes=[SQB[c]])
                    for c in range(8):
                        mm(6, ones_bf, SQ[c], c == 0, c == 7, [SQB[c], B_const])
                    sqrt_recip(6, 1.0 / D)
                    for c in range(8):
                        P.add('dve', lambda e, c=c: e.scalar_tensor_tensor(
                            out=XT[c], in0=XT[c], scalar=smallp[:, 384 + c:385 + c], in1=RSTD, op0=ALU.mult, op1=ALU.mult),
                            reads=[RSTDB, B_const], writes=[XB[c]])
                    P.add('sp', lambda e, t=t: e.dma_start(
                        out=yT_out[:, :, t * T:(t + 1) * T].rearrange("c p n -> p c n"), in_=XTall.rearrange("p (c n) -> p c n", n=T)),
                        reads=XB, chan=XSCH)

        ACC = ACCB = KT = QT = VA = VBt = KTB = QTB = VAB = VBB = KCH = QCH = VCH = NPT = PT = PTB = REC = RECB = NOS = OST = OSTB = OSTCH = ost_i = O1 = O2 = O1B = O2B = ASQ = ASQB = ART = ARTB = ARS = ARSB = ATT_END = PSP = PSB = None

        def alloc_attn():
            nonlocal ACC, ACCB, KT, QT, VA, VBt, KTB, QTB, VAB, VBB, KCH, QCH, VCH, NPT, PT, PTB, REC, RECB, NOS, OST, OSTB, OSTCH, ost_i, O1, O2, O1B, O2B, ASQ, ASQB, ART, ARTB, ARS, ARSB, ATT_END, PSP, PSB
            alloc_psum()
            KT = [vbf(alloc(NTOK * 2), NTOK) for _ in range(2)]
            QT = [vbf(alloc(NTOK * 2), NTOK) for _ in range(2)]
            VA = [vbf(alloc(64 * 128 * 2), 64 * 128) for _ in range(2)]
            VBt = [vbf(alloc(64 * 128 * 2), 64 * 128) for _ in range(2)]
            KTB = [Buf("kt0"), Buf("kt1")]
            QTB = [Buf("qt0"), Buf("qt1")]
            VAB = [Buf("va0"), Buf("va1")]
            VBB = [Buf("vb0"), Buf("vb1")]
            KCH = [P.chan("k0"), P.chan("k1")]
            QCH = [P.chan("q0"), P.chan("q1")]
            VCH = [P.chan("v0"), P.chan("v1")]
            NPT = 3
            PT = [vbf(alloc(1024 * 2), 1024) for _ in range(NPT)]
            PTB = [Buf(f"pt{k}") for k in range(NPT)]
            REC = [vf32(alloc(T * 4), T) for _ in range(2)]
            RECB = [Buf("rec0"), Buf("rec1")]
            NOS = 3
            OST = [vbf(alloc(T * 2), T) for _ in range(NOS)]
            OSTB = [Buf(f"ost{k}") for k in range(NOS)]
            OSTCH = [P.chan(f"ost{k}") for k in range(NOS)]
            ost_i = [0]
            O1 = vf32(alloc(T * 4), T)
            O2 = vf32(alloc(T * 4), T)
            O1B = Buf("o1")
            O2B = Buf("o2")
            ASQ = vbf(alloc(T * 2), T)
            ASQB = Buf("asq")
            ART = vf32(alloc(T * 4), T)
            ARTB = Buf("art")
            ARS = vf32(alloc(T * 4), T)
            ARSB = Buf("ars")
            ACC = [vf32(alloc(1024 * 4), 1024) for _ in range(4)]
            ACCB = [Buf(f"acc{k}") for k in range(4)]
            ATT_END = cursor[0]

        PZW = PZ = PW1 = PW2 = ICN = PD = PZB = PW1B = PW2B = ICNB = PDB = PZCH = ICNCH = HLCH = HRCH = POST = POSTB = POSTCH = POOL_END = PSP = PSB = None

        def alloc_pool():
            nonlocal PZW, PZ, PW1, PW2, ICN, PD, PZB, PW1B, PW2B, ICNB, PDB, PZCH, ICNCH, HLCH, HRCH, POST, POSTB, POSTCH, POOL_END, PSP, PSB
            alloc_psum()
            PZW = 2048 + 16
            PZ = vf32(alloc(PZW * 4), PZW)
            PW1 = vf32(alloc(PZW * 4), PZW)
            PW2 = vf32(alloc(PZW * 4), PZW)
            ICN = vf32(alloc(2048 * 4), 2048)
            PD = vbf(alloc(2048 * 2), 2048)
            PZB, PW1B, PW2B, ICNB, PDB = Buf("pz"), Buf("pw1"), Buf("pw2"), Buf("icn"), Buf("pd")
            PZCH = P.chan("pzl")
            ICNCH = P.chan("icn")
            HLCH = P.chan("hl")
            HRCH = P.chan("hr")
            POST = [vbf(alloc(T * 2), T) for _ in range(2)]
            POSTB = [Buf("post0"), Buf("post1")]
            POSTCH = [P.chan("post0"), P.chan("post1")]
            POOL_END = cursor[0]


        def ost():
            k = ost_i[0] % NOS
            ost_i[0] += 1
            return OST[k], OSTB[k], OSTCH[k]

        set_i = [0]

        def load_qk(qsrcs, ksrcs):
            sl = set_i[0] % 2
            for (r0, r1, ap) in qsrcs:
                P.add('sp', lambda e, r0=r0, r1=r1, ap=ap: e.dma_start(out=QT[sl][r0:r1, :], in_=ap), writes=[QTB[sl]], chan=QCH[sl])
            for (r0, r1, ap) in ksrcs:
                P.add('sp', lambda e, r0=r0, r1=r1, ap=ap: e.dma_start(out=KT[sl][r0:r1, :], in_=ap), writes=[KTB[sl]], chan=KCH[sl])
            return sl

        NSS = 3

        def attn_units(sl, dk, scale, pv, obanks_for_qg, finalize, vreads):
            units = [(qg, kp) for qg in range(16) for kp in range(32)]
            pt_i = [0]

            def qk(u):
                qg, kp = units[u]
                sb = (u % NSS) * 2
                for kk in range(2):
                    kb = 2 * kp + kk
                    r0, r1 = (64, 128) if (dk == 64 and kk == 1) else (0, dk)
                    mm(sb + kk, KT[sl][r0:r1, kb * 128:(kb + 1) * 128], QT[sl][r0:r1, qg * T:(qg + 1) * T], True, True,
                       [KTB[sl], QTB[sl]])

            qk(0)
            for u in range(len(units)):
                qg, kp = units[u]
                if u + 1 < len(units):
                    qk(u + 1)
                sb = (u % NSS) * 2
                k = pt_i[0] % NPT
                pt_i[0] += 1
                kseg = (2 * kp) // 16
                mcol = kseg * 4 + qg // 4
                P.add('act', lambda e, sb=sb, k=k, mcol=mcol: e.activation(
                    out=PT[k], in_=PSP[sb // 2], func=AF.Exp, bias=maskb[:, mcol:mcol + 1], scale=scale),
                    reads=[PSB[sb], PSB[sb + 1], B_const], writes=[PTB[k]])
                obs = obanks_for_qg(qg)
                for kk in range(2):
                    kb = 2 * kp + kk
                    for (getl, ob) in zip(pv, obs):
                        mm(ob, getl(kb), PT[k][:, kk * T:(kk + 1) * T], kb == 0, kb == 63, [PTB[k]] + vreads)
                if kp == 31:
                    finalize(qg, obs)

        va_ones_done = [False]

        def ensure_ones():
            for sl in range(2):
                va3 = VA[sl].rearrange("p (k n) -> p k n", n=128)
                vb3 = VBt[sl].rearrange("p (k n) -> p k n", n=128)
                P.add('dve', lambda e, va3=va3: e.memset(va3[:, :, 64:128], 1.0), writes=[VAB[sl]])
                P.add('dve', lambda e, vb3=vb3: e.memset(vb3[:, :, 0:64], 1.0), writes=[VBB[sl]])

        def dv64_load(spec):
            qsrcs, ksrcs, vsrc = spec[0], spec[1], spec[2]
            sl = load_qk(qsrcs, ksrcs)
            set_i[0] += 1
            va3 = VA[sl].rearrange("p (k n) -> p k n", n=128)
            P.add('sp', lambda e: e.dma_start(out=va3[:, :, 0:64], in_=vsrc), writes=[VAB[sl]], chan=VCH[sl])
            return sl

        def dv64_compute(spec, sl):
            dk, scale, out_chunk, out_row0, obase = spec[3:]
            va3 = VA[sl].rearrange("p (k n) -> p k n", n=128)

            def fin(qg, obs):
                ob = obs[0]
                r = qg % 2
                P.add('dve', lambda e: e.reciprocal(out=REC[r][64:128, :], in_=bank(ob)[64:128, :]), reads=[PSB[ob]], writes=[RECB[r]])
                sv, sb_, sch = ost()
                P.add('dve', lambda e: e.tensor_tensor(out=sv[0:64, :], in0=bank(ob)[0:64, :], in1=REC[r][64:128, :], op=ALU.mult),
                      reads=[PSB[ob], RECB[r]], writes=[sb_])
                P.add('sp', lambda e: e.dma_start(out=oT_d[out_chunk][out_row0:out_row0 + 64, qg * T:(qg + 1) * T], in_=sv[0:64, :]),
                      reads=[sb_], chan=sch)

            attn_units(sl, dk, scale, [lambda kb: va3[:, kb, :]], lambda qg: [6 + (qg + obase) % 2], fin, [VAB[sl]])

        def attn_dv64_list(specs):
            sl = dv64_load(specs[0])
            for m in range(len(specs)):
                nsl = dv64_load(specs[m + 1]) if m + 1 < len(specs) else None
                dv64_compute(specs[m], sl)
                P.cut()
                sl = nsl

        def attn_diff(j, h):
            sls = []
            for w in range(2):
                sl = load_qk([(0, 64, qT_d[8 + 2 * h + w][0:64]), (64, 128, qT_d[8 + 2 * h + w][0:64])],
                             [(0, 64, kT_d[2 + 2 * h + w][0:64]), (64, 128, kT_d[2 + 2 * h + w][0:64])])
                set_i[0] += 1
                sls.append(sl)
            vs = h % 2
            va3 = VA[vs].rearrange("p (k n) -> p k n", n=128)
            P.add('sp', lambda e: e.dma_start(out=va3, in_=v128_d[h]), writes=[VAB[vs]], chan=VCH[vs])
            for qg in range(16):
                if qg % 4 == 0:
                    P.cut()
                for w in range(2):
                    sl = sls[w]
                    ob = 4 + w

                    def qk(kp, sl=sl):
                        sb = (kp % 2) * 2
                        for kk in range(2):
                            kb = 2 * kp + kk
                            r0 = 64 * kk
                            mm(sb + kk, KT[sl][r0:r0 + 64, kb * 128:(kb + 1) * 128], QT[sl][r0:r0 + 64, qg * T:(qg + 1) * T], True, True,
                               [KTB[sl], QTB[sl]])
                    qk(0)
                    for kp in range(32):
                        if kp + 1 < 32:
                            qk(kp + 1)
                        sb = (kp % 2) * 2
                        k = (kp + w) % NPT
                        mcol = ((2 * kp) // 16) * 4 + qg // 4
                        P.add('act', lambda e, sb=sb, k=k, mcol=mcol: e.activation(
                            out=PT[k], in_=PSP[sb // 2], func=AF.Exp, bias=maskb[:, mcol:mcol + 1], scale=0.125),
                            reads=[PSB[sb], PSB[sb + 1], B_const], writes=[PTB[k]])
                        ai = 2 * w + (kp % 2)
                        if kp < 2:
                            P.add('dve', lambda e, ai=ai, k=k: e.tensor_copy(ACC[ai], PT[k]), reads=[PTB[k]], writes=[ACCB[ai]])
                        else:
                            P.add('dve', lambda e, ai=ai, k=k: e.tensor_tensor(out=ACC[ai], in0=ACC[ai], in1=PT[k], op=ALU.add),
                                  reads=[PTB[k]], writes=[ACCB[ai]])
                        for kk in range(2):
                            kb = 2 * kp + kk
                            mm(ob, va3[:, kb, :], PT[k][:, kk * T:(kk + 1) * T], kb == 0, kb == 63, [PTB[k], VAB[vs]])
                    n = 0
                    for ai in (2 * w, 2 * w + 1):
                        for hh in range(2):
                            mm(6, ones_f, ACC[ai][:, hh * T:(hh + 1) * T], n == 0, n == 3, [ACCB[ai], B_const])
                            n += 1
                    ov, ovb = (O1, O1B) if w == 0 else (O2, O2B)
                    P.add('dve', lambda e, w=w: e.reciprocal(out=REC[w], in_=bank(6)), reads=[PSB[6]], writes=[RECB[w]])
                    P.add('dve', lambda e, w=w, ob=ob, ov=ov: e.tensor_tensor(out=ov, in0=bank(ob), in1=REC[w], op=ALU.mult),
                          reads=[PSB[ob], RECB[w]], writes=[ovb])
                P.add('dve', lambda e: e.scalar_tensor_tensor(out=O1, in0=O2, scalar=lamc[:, j * 4 + 1:j * 4 + 2], in1=O1, op0=ALU.mult, op1=ALU.add),
                      reads=[O2B, B_mod], writes=[O1B])
                P.add('act', lambda e: e.activation(out=ASQ, in_=O1, func=AF.Square), reads=[O1B], writes=[ASQB])
                mm(7, ones_bf, ASQ, True, True, [ASQB, B_const])
                P.add('act', lambda e: e.activation(out=ART, in_=bank(7), func=AF.Sqrt, bias=epsc, scale=1.0 / 128), reads=[PSB[7], B_const], writes=[ARTB])
                P.add('dve', lambda e: e.reciprocal(out=ARS, in_=ART), reads=[ARTB], writes=[ARSB])
                sv, sb_, sch = ost()
                P.add('dve', lambda e, sv=sv: e.scalar_tensor_tensor(out=sv, in0=O1, scalar=sublng[:, j:j + 1], in1=ARS, op0=ALU.mult, op1=ALU.mult),
                      reads=[O1B, ARSB, B_mod], writes=[sb_])
                P.add('sp', lambda e, sv=sv, qg=qg: e.dma_start(out=oT_d[4 + h][:, qg * T:(qg + 1) * T], in_=sv), reads=[sb_], chan=sch)

        def pool_branch(j):
            pw = poolw.rearrange("p (g n) -> p g n", n=128)
            post_i = 0
            for g, w in enumerate((2, 4, 8, 16)):
                P.cut()
                for seg in range(4):
                    t0 = seg * 2048
                    P.add('sp', lambda e, g=g, t0=t0: e.dma_start(out=PZ[:, 8:8 + 2048], in_=pz_d[g][:, t0:t0 + 2048]), writes=[PZB], chan=PZCH)
                    P.add('sp', lambda e, g=g, t0=t0: e.dma_start(out=ICN, in_=icnt_in[g:g + 1, t0:t0 + 2048].partition_broadcast(128)), writes=[ICNB], chan=ICNCH)
                    if seg > 0:
                        P.add('sp', lambda e, g=g, t0=t0: e.dma_start(out=PZ[:, 0:8], in_=pz_d[g][:, t0 - 8:t0]), writes=[PZB], chan=HLCH)
                        P.add('dve', lambda e: e.tensor_scalar(out=PZ[:, 0:8], in0=PZ[:, 0:8], scalar1=flag[:, 0:1], scalar2=None, op0=ALU.mult),
                              reads=[B_const], writes=[PZB])
                    else:
                        P.add('dve', lambda e: e.memset(PZ[:, 0:8], 0.0), writes=[PZB])
                    if seg < 3:
                        P.add('sp', lambda e, g=g, t0=t0: e.dma_start(out=PZ[:, 2056:2064], in_=pz_d[g][:, t0 + 2048:t0 + 2056]), writes=[PZB], chan=HRCH)
                        P.add('dve', lambda e: e.tensor_scalar(out=PZ[:, 2056:2064], in0=PZ[:, 2056:2064], scalar1=flag[:, 0:1], scalar2=None, op0=ALU.mult),
                              reads=[B_const], writes=[PZB])
                    else:
                        P.add('dve', lambda e: e.memset(PZ[:, 2056:2064], 0.0), writes=[PZB])
                    src, srcb = PZ, PZB
                    step = 1
                    n = PZW
                    bufs = [(PW1, PW1B), (PW2, PW2B)]
                    bi = 0
                    while step < w:
                        dst, dstb = bufs[bi % 2]
                        bi += 1
                        n2 = n - step
                        P.add('dve', lambda e, src=src, dst=dst, n2=n2, step=step: e.tensor_tensor(
                            out=dst[:, 0:n2], in0=src[:, 0:n2], in1=src[:, step:step + n2], op=ALU.add),
                            reads=[srcb], writes=[dstb])
                        src, srcb = dst, dstb
                        n = n2
                        step *= 2
                    o0 = 8 - w // 2
                    dst, dstb = bufs[bi % 2]
                    P.add('dve', lambda e, src=src, dst=dst, o0=o0: e.tensor_tensor(
                        out=dst[:, 0:2048], in0=src[:, o0:o0 + 2048], in1=ICN, op=ALU.mult), reads=[srcb, ICNB], writes=[dstb])
                    P.add('dve', lambda e, dst=dst: e.tensor_tensor(out=PD, in0=dst[:, 0:2048], in1=PZ[:, 8:8 + 2048], op=ALU.subtract),
                          reads=[dstb, PZB], writes=[PDB])
                    for tq in range(4):
                        bk = tq % 2
                        mm(bk, pw[:, j * 4 + g, :], PD[:, tq * T:(tq + 1) * T], True, True, [PDB, B_const])
                        k = post_i % 2
                        post_i += 1
                        P.add('act', lambda e, bk=bk, k=k, g=g: e.activation(out=POST[k], in_=bank(bk), func=AF.Identity,
                                                                           scale=smallp[:, 402 + j * 4 + g:403 + j * 4 + g]),
                              reads=[PSB[bk], B_const], writes=[POSTB[k]])
                        P.add('sp', lambda e, k=k, g=g, t0=t0, tq=tq: e.dma_start(
                            out=oT_d[4 + g][:, t0 + tq * T:t0 + (tq + 1) * T], in_=POST[k]), reads=[POSTB[k]], chan=POSTCH[k])

        def dv64_specs_even():
            return [([(0, 96, qT_d[h][0:96])], [(0, 64, kT_d[h][0:64]), (64, 96, kpe_d[:, :])], v64_d[h],
                     96, float(96 ** -0.5), h // 2, (h % 2) * 64, h) for h in range(8)]

        def dv64_specs_odd():
            return [([(0, 64, qT_d[m][0:64]), (64, 128, qT_d[m][0:64])],
                     [(0, 64, kT_d[m // 4][0:64]), (64, 128, kT_d[m // 4][0:64])], v64_d[m // 4],
                     64, 0.125, m // 2, (m % 2) * 64, m) for m in range(8)]

        def attention_phase(i):
            j = i // 2
            if i % 2 == 0:
                with contextlib.ExitStack() as pes:
                    cur_es[0] = pes
                    alloc_pool()
                    pool_branch(j)
                    P.barrier()
                    P.emit_pending(nc)
                with contextlib.ExitStack() as pes:
                    cur_es[0] = pes
                    alloc_attn()
                    ensure_ones()
                    attn_dv64_list(dv64_specs_even())
                    P.barrier()
                    P.emit_pending(nc)
            else:
                with contextlib.ExitStack() as pes:
                    cur_es[0] = pes
                    alloc_attn()
                    ensure_ones()
                    attn_dv64_list(dv64_specs_odd())
                    P.barrier()
                    for h in range(4):
                        attn_diff(j, h)
                    P.barrier()
                    P.emit_pending(nc)

        for ph in range(5):
            with contextlib.ExitStack() as pes:
                cur_es[0] = pes
                alloc_token()
                token_phase(ph)
                P.barrier()
                P.emit_pending(nc)
            if ph < 4:
                attention_phase(ph)
    return nc


_CACHE = {}


def kernel(**inputs):
    inp = {k: np.asarray(v) for k, v in inputs.items()}
    sh = _prep_shared(inp)
    pos_s = np.arange(NTOK)
    pos_p = np.arange(NTOK) % 2048
    tabs_s = _rope_tables(pos_s, pos_s // 64, pos_s % 64)
    tabs_p = _rope_tables(pos_p, pos_p // 64, pos_p % 64)
    icnt_s = _icnt(NTOK)
    icnt_p = _icnt(2048)
    in_maps = []
    for r in range(NCORES):
        m = _prep_core(inp, r, tabs_s, tabs_p, icnt_s, icnt_p)
        m.update(sh)
        in_maps.append(m)
    if 'nc' not in _CACHE:
        _CACHE['nc'] = build_program()
    nc = _CACHE['nc']
    res = run_bass_kernel_spmd(nc, in_maps, core_ids=list(range(NCORES)))
    outs = [np.asarray(r["yT"]).reshape(D, NTOK).T for r in res.results]
    y_sample = np.stack([np.ascontiguousarray(outs[r]) for r in range(4)]).astype(np.float32)
    y_prompt = np.concatenate([outs[r].reshape(4, 2048, D) for r in range(4, 8)], axis=0).astype(np.float32)
    return (np.ascontiguousarray(y_prompt), y_sample)
```

```python
import contextlib
import numpy as np
import concourse.bass as bass
import concourse.mybir as mybir
from concourse.bass_utils import run_bass_kernel_spmd

F32 = mybir.dt.float32
BF16 = mybir.dt.bfloat16
AF = mybir.ActivationFunctionType
ALU = mybir.AluOpType
AX = mybir.AxisListType

D = 1024
DFF = 2816
DEPTH = 4
NTOK = 8192
T = 512
NT = NTOK // T
EPS = 1e-6
NCORES = 8
NS = 420

ENGS = ['pe', 'act', 'dve', 'pool', 'sp']


class Buf:
    __slots__ = ('name', 'last_w', 'readers')

    def __init__(self, name):
        self.name = name
        self.last_w = None
        self.readers = []


class Chan:
    __slots__ = ('name', 'sem', 'count', 'nobar')

    def __init__(self, name, sem):
        self.name = name
        self.sem = sem
        self.count = 0
        self.nobar = False


class Op:
    __slots__ = ('eng', 'fn', 'waits', 'signal', 'chan', 'sigidx')

    def __init__(self, eng, fn):
        self.eng = eng
        self.fn = fn
        self.waits = []
        self.signal = False
        self.chan = None
        self.sigidx = 0


class Prog:
    def __init__(self, nc, sems):
        self.nc = nc
        self.free_sems = list(sems)
        self.ops = {e: [] for e in ENGS}
        self.esem = {e: self.free_sems.pop() for e in ENGS}
        self.waited_op = {e: {t: -1 for t in ENGS} for e in ENGS}
        self.waited_ch = {e: {} for e in ENGS}
        self.chans = []
        self.cuts = []
        self.emitted = {e: 0 for e in ENGS}
        self.nsig = {e: 0 for e in ENGS}

    def chan(self, name):
        for c in self.chans:
            if c.name == name:
                return c
        c = Chan(name, self.free_sems.pop())
        self.chans.append(c)
        return c

    def _add_wait(self, op, ref):
        e = op.eng
        if ref[0] == 'op':
            _, te, idx = ref
            if te == e and e in ('pe', 'sp'):
                return
            if self.waited_op[e][te] >= idx:
                return
            self.waited_op[e][te] = idx
            if idx < self.emitted[te] and not self.ops[te][idx].signal:
                raise AssertionError(f"late signal request on emitted op {te}[{idx}] from {e}")
            self.ops[te][idx].signal = True
            op.waits.append(ref)
        else:
            _, ch, cnt = ref
            if self.waited_ch[e].get(ch, 0) >= cnt:
                return
            self.waited_ch[e][ch] = cnt
            op.waits.append(ref)

    def add(self, eng, fn, reads=(), writes=(), chan=None):
        op = Op(eng, fn)
        for b in reads:
            if b.last_w is not None:
                self._add_wait(op, b.last_w)
        for b in writes:
            if b.last_w is not None:
                self._add_wait(op, b.last_w)
            for r in b.readers:
                self._add_wait(op, r)
        idx = len(self.ops[eng])
        self.ops[eng].append(op)
        if chan is not None:
            chan.count += 16
            op.chan = chan
            ref = ('dma', chan, chan.count)
        else:
            ref = ('op', eng, idx)
        for b in writes:
            b.last_w = ref
            b.readers = []
        for b in reads:
            if b in writes:
                continue
            if ref[0] == 'op':
                b.readers = [r for r in b.readers if not (r[0] == 'op' and r[1] == eng)]
            else:
                b.readers = [r for r in b.readers if not (r[0] == 'dma' and r[1] is chan)]
            b.readers.append(ref)
        return ref

    def wait_refs(self, eng, refs):
        op = Op(eng, None)
        for r in refs:
            self._add_wait(op, r)
        self.ops[eng].append(op)

    def barrier(self):
        refs = []
        for e in ('pe', 'act', 'dve'):
            if self.ops[e]:
                for idx in range(len(self.ops[e]) - 1, -1, -1):
                    if self.ops[e][idx].fn is not None and self.ops[e][idx].chan is None:
                        refs.append(('op', e, idx))
                        break
        for c in self.chans:
            if c.count and not c.nobar:
                refs.append(('dma', c, c.count))
        for e in ENGS:
            self.wait_refs(e, refs)

    def cut(self):
        self.cuts.append({e: len(self.ops[e]) for e in ENGS})

    def emit_pending(self, nc):
        prog = self
        for e in ENGS:
            for op in self.ops[e][self.emitted[e]:]:
                if op.signal:
                    self.nsig[e] += 1
                    op.sigidx = self.nsig[e]
        end = {e: len(self.ops[e]) for e in ENGS}
        bounds = [c for c in self.cuts if all(c[e] >= self.emitted[e] for e in ENGS)] + [end]
        self.cuts = []
        prev = dict(self.emitted)

        def run(engname, eng, lo, hi):
            for op in prog.ops[engname][lo:hi]:
                for r in op.waits:
                    if r[0] == 'op':
                        eng.wait_ge(prog.esem[r[1]], prog.ops[r[1]][r[2]].sigidx)
                    else:
                        eng.wait_ge(r[1].sem, r[2])
                if op.fn is None:
                    continue
                inst = op.fn(eng)
                if op.chan is not None:
                    inst.then_inc(op.chan.sem, 16)
                elif op.signal:
                    inst.then_inc(prog.esem[engname], 1)
                op.fn = None

        for bnd in bounds:
            if all(bnd[e] == prev[e] for e in ENGS):
                continue
            with nc.Block() as block:
                if bnd['pe'] > prev['pe']:
                    @block.tensor
                    def _(t, lo=prev['pe'], hi=bnd['pe']):
                        run('pe', t, lo, hi)
                if bnd['act'] > prev['act']:
                    @block.scalar
                    def _(s, lo=prev['act'], hi=bnd['act']):
                        run('act', s, lo, hi)
                if bnd['dve'] > prev['dve']:
                    @block.vector
                    def _(v, lo=prev['dve'], hi=bnd['dve']):
                        run('dve', v, lo, hi)
                if bnd['pool'] > prev['pool']:
                    @block.gpsimd
                    def _(g, lo=prev['pool'], hi=bnd['pool']):
                        run('pool', g, lo, hi)
                if bnd['sp'] > prev['sp']:
                    @block.sync
                    def _(sy, lo=prev['sp'], hi=bnd['sp']):
                        run('sp', sy, lo, hi)
            prev = bnd
        self.emitted = end


def _kblocks(w, cols):
    K = w.shape[0]
    sub = w[:, cols]
    return np.ascontiguousarray(sub.reshape(K // 128, 128, len(cols)).transpose(1, 0, 2)).reshape(128, -1)


def _swap_mla(d):
    return (d + 16) % 32


def _swap_ax(d):
    return (d + 16) % 32 if d < 32 else 32 + ((d - 32) + 16) % 32


def _swap_diff(d):
    return (d + 8) % 16 if d < 16 else d


def _rope_tables(pos, rows, cols):
    f32 = np.float32
    n = pos.shape[0]
    out = np.zeros((6, 128, n), f32)
    posf = pos.astype(f32)
    rowf = rows.astype(f32)
    colf = cols.astype(f32)
    inv_m = (f32(500000.0) ** (-np.arange(0, 16, dtype=f32) * f32(2.0) / f32(32))).astype(f32)
    inv_a = (f32(10000.0) ** (-np.arange(0, 16, dtype=f32) * f32(2.0) / f32(32))).astype(f32)
    inv_d = (f32(500000.0) ** (-np.arange(0, 8, dtype=f32) * f32(2.0) / f32(16))).astype(f32)
    for p in range(128):
        d = p % 32
        ang = (posf * inv_m[d % 16]).astype(f32)
        out[0, p] = np.cos(ang)
        out[1, p] = np.sin(ang) * (f32(-1.0) if d < 16 else f32(1.0))
        d = p % 64
        if d < 32:
            ang = (rowf * inv_a[d % 16]).astype(f32)
            sg = -1.0 if d < 16 else 1.0
        else:
            ang = (colf * inv_a[(d - 32) % 16]).astype(f32)
            sg = -1.0 if (d - 32) < 16 else 1.0
        out[2, p] = np.cos(ang)
        out[3, p] = np.sin(ang) * f32(sg)
        if d < 16:
            ang = (posf * inv_d[d % 8]).astype(f32)
            out[4, p] = np.cos(ang)
            out[5, p] = np.sin(ang) * (f32(-1.0) if d < 8 else f32(1.0))
        else:
            out[4, p] = 1.0
            out[5, p] = 0.0
    return out


def _prep_shared(inp):
    sh = {}
    w_ada = inp['w_ada']
    wada = np.empty((32, 128, 8 * 1152), np.float32)
    for i in range(4):
        for b in range(8):
            wada[i * 8 + b] = _kblocks(w_ada[i], np.arange(b * 1152, (b + 1) * 1152))
    sh['wada'] = wada
    fin = np.empty((8, 11, 128, 4096), np.float32)
    fout = np.empty((8, 4, 128, 5632), np.float32)
    for i in range(4):
        for k in range(2):
            w_in = inp['ffn_w_in'][i, k]
            w_out = inp['ffn_w_out'][i, k]
            for b in range(11):
                cols = np.concatenate([np.arange(256 * b, 256 * b + 256), DFF + np.arange(256 * b, 256 * b + 256)])
                fin[i * 2 + k, b] = _kblocks(w_in, cols)
            for c2 in range(4):
                fout[i * 2 + k, c2] = _kblocks(w_out, np.arange(256 * c2, 256 * c2 + 256))
    sh['ffn_in'] = fin
    sh['ffn_out'] = fout
    ev_in = np.empty((2, 128, 8 * 1216), np.float32)
    ev_uq = np.empty((2, 128, 3 * 1024), np.float32)
    ev_ukv = np.empty((2, 128, 2 * 1024), np.float32)
    ev_out = np.empty((2, 2, 128, 4096), np.float32)
    poolw = np.empty((128, 2 * 4 * 128), np.float32)
    for j in range(2):
        w = inp['ev_w_in'][j]
        kpe = 640 + np.arange(32)
        kpes = 640 + np.array([_swap_mla(d) for d in range(32)])
        cols = np.concatenate([np.arange(0, 640), 672 + np.arange(512), kpe, kpes])
        blocks = [cols[0:512], cols[512:1024], cols[1024:1216]]
        ev_in[j] = np.concatenate([_kblocks(w, b) for b in blocks], axis=1)
        wq = inp['mla_w_uq'][j]
        qc = []
        for h in range(8):
            qc.append(h * 96 + np.arange(64))
        for h in range(8):
            qc.append(h * 96 + 64 + np.arange(32))
        for h in range(8):
            qc.append(h * 96 + 64 + np.array([_swap_mla(d) for d in range(32)]))
        ev_uq[j] = _kblocks(wq, np.concatenate(qc))
        wkv = inp['mla_w_ukv'][j]
        kc = [h * 128 + np.arange(64) for h in range(8)] + [h * 128 + 64 + np.arange(64) for h in range(8)]
        ev_ukv[j] = _kblocks(wkv, np.concatenate(kc))
        wo = inp['ev_w_out'][j]
        for b in range(2):
            ev_out[j, b] = _kblocks(wo, np.arange(512 * b, 512 * b + 512))
        for g in range(4):
            poolw[:, (j * 4 + g) * 128:(j * 4 + g + 1) * 128] = inp['pool_w'][j, g]
    sh['ev_in'] = ev_in
    sh['ev_uq'] = ev_uq
    sh['ev_ukv'] = ev_ukv
    sh['ev_out'] = ev_out
    sh['poolw'] = poolw
    od_in = np.empty((2, 128, 8 * 3328), np.float32)
    od_v = np.empty((2, 128, 8 * 640), np.float32)
    od_out = np.empty((2, 2, 128, 4096), np.float32)
    sw_ax = np.array([_swap_ax(d) for d in range(64)])
    sw_df = np.array([_swap_diff(d) for d in range(64)])
    for j in range(2):
        w = inp['od_w_in'][j]
        chunks = []

        def ch(base, c, sw):
            direct = base + c * 128 + np.arange(128)
            swp = base + c * 128 + np.concatenate([sw, 64 + sw])
            chunks.append(direct)
            chunks.append(swp)

        for c in range(4):
            ch(0, c, sw_ax)
        ch(512, 0, sw_ax)
        for c in range(4):
            ch(768, c, sw_df)
        for c in range(4):
            ch(1280, c, sw_df)
        cols = np.concatenate(chunks)
        blocks = [cols[b * 512:(b + 1) * 512] for b in range(7)]
        od_in[j] = np.concatenate([_kblocks(w, b) for b in blocks], axis=1)
        od_v[j] = np.concatenate([_kblocks(w, 1792 + np.arange(512)), _kblocks(w, 640 + np.arange(128))], axis=1)
        wo = inp['od_w_out'][j]
        for b in range(2):
            od_out[j, b] = _kblocks(wo, np.arange(512 * b, 512 * b + 512))
    sh['od_in'] = od_in
    sh['od_v'] = od_v
    sh['od_out'] = od_out
    sp = np.zeros((128, NS), np.float32)
    for i in range(4):
        sp[:, i * 72:(i + 1) * 72] = inp['b_ada'][i].reshape(72, 128).T
        for n in range(3):
            sp[:, 288 + (i * 3 + n) * 8:288 + (i * 3 + n + 1) * 8] = inp['norm_g'][i, n].reshape(8, 128).T
    sp[:, 384:392] = inp['final_g'].reshape(8, 128).T
    pidx = np.arange(128) % 64
    for j in range(2):
        sp[:, 392 + j * 3:392 + j * 3 + 3] = inp['mla_gq'][j].reshape(3, 128).T
        sp[:, 398 + j * 2:398 + j * 2 + 2] = inp['mla_gkv'][j].reshape(2, 128).T
        sp[:, 402 + j * 4:402 + j * 4 + 4] = inp['pool_scale'][j].reshape(4, 128).T
        sp[:, 410 + j] = inp['gqa_gq'][j][pidx]
        sp[:, 412 + j] = inp['gqa_gq'][j][sw_ax[pidx]]
        sp[:, 414 + j] = inp['gqa_gk'][j][pidx]
        sp[:, 416 + j] = inp['gqa_gk'][j][sw_ax[pidx]]
        sp[:, 418 + j] = inp['diff_subln_g'][j]
    sh['smallp'] = sp
    lv = np.zeros((1, 512), np.float32)
    for j in range(2):
        for q, nm in enumerate(('diff_lq1', 'diff_lk1', 'diff_lq2', 'diff_lk2')):
            lv[0, (j * 4 + q) * 64:(j * 4 + q + 1) * 64] = inp[nm][j]
    sh['lvec'] = lv
    return sh


def _prep_core(inp, r, tabs_s, tabs_p, icnt_s, icnt_p):
    m = {}
    if r < 4:
        x = inp['x_sample'][r]
        cseg = np.stack([inp['c_sample'][r]] * 4)
        m['tabs'] = tabs_s
        m['icnt'] = icnt_s
        m['maskb'] = np.zeros((128, 16), np.float32)
        m['flag'] = np.ones((128, 1), np.float32)
    else:
        q = r - 4
        x = inp['x_prompt'][4 * q:4 * q + 4].reshape(NTOK, D)
        cseg = inp['c_prompt'][4 * q:4 * q + 4]
        m['tabs'] = tabs_p
        m['icnt'] = icnt_p
        mb = np.full((4, 4), -30000.0, np.float32)
        mb[np.arange(4), np.arange(4)] = 0.0
        m['maskb'] = np.ascontiguousarray(np.broadcast_to(mb.reshape(1, 16), (128, 16)))
        m['flag'] = np.zeros((128, 1), np.float32)
    m['xT'] = np.ascontiguousarray(x.T).reshape(8, 128, NTOK)
    m['c4T'] = np.ascontiguousarray(cseg.reshape(4, 8, 128).transpose(2, 1, 0)).reshape(128, 32)
    return m


def _icnt(S):
    t = np.arange(NTOK) % S
    out = np.empty((4, NTOK), np.float32)
    for g, w in enumerate((2, 4, 8, 16)):
        lo = np.clip(t - w // 2, 0, S)
        hi = np.clip(t + w - w // 2, 0, S)
        out[g] = (np.float32(1.0) / (hi - lo).astype(np.float32)).astype(np.float32)
    return out


def build_program():
    nc = bass.Bass("TRN2", target_bir_lowering=False)

    def din(name, shape, dt=F32):
        return nc.dram_tensor(name, list(shape), dt, kind="ExternalInput").ap()

    def dscr(name, shape, dt):
        return nc.dram_tensor(name, list(shape), dt).ap()

    xT_in = din("xT", [8, 128, NTOK])
    c4T_in = din("c4T", [128, 32])
    tabs_in = din("tabs", [6, 128, NTOK])
    maskb_in = din("maskb", [128, 16])
    flag_in = din("flag", [128, 1])
    icnt_in = din("icnt", [4, NTOK])
    smallp_in = din("smallp", [128, NS])
    lvec_in = din("lvec", [1, 512])
    wada_in = din("wada", [32, 128, 9216])
    ffn_in_f = din("ffn_in", [8, 11, 128, 4096])
    ffn_out_f = din("ffn_out", [8, 4, 128, 5632])
    ev_in_f = din("ev_in", [2, 128, 9728])
    ev_uq_f = din("ev_uq", [2, 128, 3072])
    ev_ukv_f = din("ev_ukv", [2, 128, 2048])
    ev_out_f = din("ev_out", [2, 2, 128, 4096])
    poolw_f = din("poolw", [128, 1024])
    od_in_f = din("od_in", [2, 128, 26624])
    od_v_f = din("od_v", [2, 128, 5120])
    od_out_f = din("od_out", [2, 2, 128, 4096])
    yT_out = nc.dram_tensor("yT", [8, 128, NTOK], F32, kind="ExternalOutput").ap()

    ffn_in_b = dscr("ffn_in_b", [8, 11, 128, 4096], BF16)
    ffn_out_b = dscr("ffn_out_b", [8, 4, 128, 5632], BF16)
    ev_in_b = dscr("ev_in_b", [2, 128, 9728], BF16)
    ev_uq_b = dscr("ev_uq_b", [2, 128, 3072], BF16)
    ev_ukv_b = dscr("ev_ukv_b", [2, 128, 2048], BF16)
    ev_out_b = dscr("ev_out_b", [2, 2, 128, 4096], BF16)
    od_in_b = dscr("od_in_b", [2, 128, 26624], BF16)
    od_v_b = dscr("od_v_b", [2, 128, 5120], BF16)
    od_out_b = dscr("od_out_b", [2, 2, 128, 4096], BF16)
    xs_d = dscr("xs_d", [8, 128, NTOK], F32)
    qT_d = dscr("qT_d", [16, 128, NTOK], BF16)
    kT_d = dscr("kT_d", [16, 128, NTOK], BF16)
    kpe_d = dscr("kpe_d", [32, NTOK], BF16)
    v64_d = dscr("v64_d", [8, 128, 64, 64], BF16)
    v128_d = dscr("v128_d", [4, 128, 64, 128], BF16)
    oT_d = dscr("oT_d", [8, 128, NTOK], BF16)
    pz_d = dscr("pz_d", [4, 128, NTOK], F32)

    ARENA_BYTES = 204 * 1024
    with contextlib.ExitStack() as es:
        sems = [es.enter_context(nc.semaphore(f"s{i}")) for i in range(60)]
        P = Prog(nc, sems)

        cur_es = [es]
        uid = [0]
        cursor = [0]

        def alloc(nbytes):
            return None

        def vf32(_off, n):
            uid[0] += 1
            return cur_es[0].enter_context(nc.sbuf_tensor(f"f{uid[0]}", [128, n], F32))[:, :]

        def vbf(_off, n):
            uid[0] += 1
            return cur_es[0].enter_context(nc.sbuf_tensor(f"b{uid[0]}", [128, n], BF16))[:, :]

        PSP = None
        PSB = None

        def alloc_psum():
            nonlocal PSP, PSB
            PSP = []
            for k in range(4):
                uid[0] += 1
                PSP.append(cur_es[0].enter_context(nc.psum_tensor(f"ps{uid[0]}", [128, 1024], F32))[:, :])
            PSB = [Buf(f"ps{b}") for b in range(8)]

        def bank(b):
            return PSP[b // 2][:, (b % 2) * 512:(b % 2) * 512 + 512]

        ones_bf = vbf(alloc(256), 128)
        blk_bf = vbf(alloc(256), 128)
        ones_f = vf32(alloc(512), 128)
        smallp = vf32(alloc(NS * 4), NS)
        c4T = vf32(alloc(128), 32)
        scT = vf32(alloc(128), 32)
        modT = [vf32(alloc(1152), 288) for _ in range(4)]
        Aall = [[vf32(alloc(128), 32) for _ in range(3)] for _ in range(4)]
        Gall = [[vf32(alloc(128), 32) for _ in range(3)] for _ in range(4)]
        maskb = vf32(alloc(64), 16)
        flag = vf32(alloc(32), 1)
        lvt = vf32(alloc(2048), 512)
        lamc = vf32(alloc(64), 16)
        sublng = vf32(alloc(32), 2)
        epsc = vf32(alloc(32), 1)
        poolw = vbf(alloc(2048), 1024)
        B_const = Buf("const")
        B_mod = Buf("mod")

        cst = P.chan("const")

        def cload(dst, src):
            P.add('sp', lambda e: e.dma_start(out=dst, in_=src), writes=[B_const], chan=cst)

        cload(smallp, smallp_in[:, :])
        cload(c4T, c4T_in[:, :])
        cload(maskb, maskb_in[:, :])
        cload(flag, flag_in[:, :])
        cload(lvt, lvec_in[:, :].partition_broadcast(128))
        cvt0 = P.chan("cvt_pw")
        P.add('pool', lambda e: e.dma_start(out=poolw, in_=poolw_f[:, :]), writes=[B_const], chan=cvt0)
        P.add('dve', lambda e: e.memset(ones_bf, 1.0), writes=[B_const])
        P.add('dve', lambda e: e.memset(ones_f, 1.0), writes=[B_const])
        P.add('dve', lambda e: e.memset(epsc, EPS), writes=[B_const])
        P.add('dve', lambda e: e.memset(blk_bf, 0.0), writes=[B_const])
        P.add('dve', lambda e: e.memset(blk_bf[0:64, 0:64], 1.0), writes=[B_const])
        P.add('dve', lambda e: e.memset(blk_bf[64:128, 64:128], 1.0), writes=[B_const])

        WD = [Buf(f"wd{i}") for i in range(4)]
        for i in range(4):
            ch = P.chan(f"cvt{i}")
            ch.nobar = True
            j = i // 2

            def cv(dst, src, ch=ch, i=i):
                P.add('pool', lambda e: e.dma_start(out=dst, in_=src), writes=[WD[i]], chan=ch)

            for b in range(11):
                cv(ffn_in_b[i * 2, b], ffn_in_f[i * 2, b])
            for c2 in range(4):
                cv(ffn_out_b[i * 2, c2], ffn_out_f[i * 2, c2])
            if i % 2 == 0:
                cv(ev_in_b[j], ev_in_f[j])
                cv(ev_uq_b[j], ev_uq_f[j])
                cv(ev_ukv_b[j], ev_ukv_f[j])
                for b in range(2):
                    cv(ev_out_b[j, b], ev_out_f[j, b])
            else:
                for b in range(7):
                    n = 4096 if b < 6 else 2048
                    cv(od_in_b[j][:, b * 4096:b * 4096 + n], od_in_f[j][:, b * 4096:b * 4096 + n])
                cv(od_v_b[j], od_v_f[j])
                for b in range(2):
                    cv(od_out_b[j, b], od_out_f[j, b])
            for b in range(11):
                cv(ffn_in_b[i * 2 + 1, b], ffn_in_f[i * 2 + 1, b])
            for c2 in range(4):
                cv(ffn_out_b[i * 2 + 1, c2], ffn_out_f[i * 2 + 1, c2])
            WD[i].last_w = ('dma', ch, ch.count)
            WD[i].readers = []

        pro_es = contextlib.ExitStack()
        cur_es[0] = pro_es
        alloc_psum()
        B_const.last_w = ('dma', cst, cst.count)
        P.wait_refs('dve', [('dma', cvt0, cvt0.count)])
        P.add('act', lambda e: e.activation(out=scT, in_=c4T, func=AF.Silu), reads=[B_const], writes=[B_mod])
        ada_off = [alloc(8 * 1152 * 4) for _ in range(2)]
        ada_buf = [vf32(o, 9216) for o in ada_off]
        ada_B = [Buf("ada0"), Buf("ada1")]
        ada_ch = [P.chan("ada0"), P.chan("ada1")]
        scT3 = scT.rearrange("p (k s) -> p k s", s=4)
        nb = 0
        for i in range(4):
            for b in range(8):
                sl = nb % 2
                wb = ada_buf[sl].rearrange("p (k n) -> p k n", n=1152)
                P.add('sp', lambda e, sl=sl, i=i, b=b: e.dma_start(out=ada_buf[sl], in_=wada_in[i * 8 + b]),
                      writes=[ada_B[sl]], chan=ada_ch[sl])
                pb = nb % 2
                for jj in range(9):
                    for kc in range(8):
                        P.add('pe', lambda e, wb=wb, jj=jj, kc=kc, pb=pb: e.matmul(
                            bank(pb)[:, jj * 4:jj * 4 + 4], wb[:, kc, jj * 128:(jj + 1) * 128], scT3[:, kc, :],
                            start=(kc == 0), stop=(kc == 7)),
                            reads=[ada_B[sl], B_mod], writes=[PSB[pb]])
                for jj in range(9):
                    jcol = b * 9 + jj
                    P.add('dve', lambda e, i=i, jj=jj, jcol=jcol, pb=pb: e.tensor_scalar(
                        out=modT[i][:, jcol * 4:jcol * 4 + 4], in0=bank(pb)[:, jj * 4:jj * 4 + 4],
                        scalar1=smallp[:, i * 72 + jcol:i * 72 + jcol + 1], scalar2=None, op0=ALU.add),
                        reads=[PSB[pb], B_const], writes=[B_mod])
                nb += 1
        for i in range(4):
            for n in range(3):
                for c in range(8):
                    js = (3 * n + 1) * 8 + c
                    P.add('dve', lambda e, i=i, n=n, c=c, js=js: e.tensor_scalar(
                        out=Aall[i][n][:, c * 4:c * 4 + 4], in0=modT[i][:, js * 4:js * 4 + 4],
                        scalar1=1.0, scalar2=smallp[:, 288 + (i * 3 + n) * 8 + c:288 + (i * 3 + n) * 8 + c + 1],
                        op0=ALU.add, op1=ALU.mult), reads=[B_mod, B_const], writes=[B_mod])
                jg = (3 * n + 2) * 8
                P.add('dve', lambda e, i=i, n=n, jg=jg: e.tensor_scalar(
                    out=Gall[i][n], in0=modT[i][:, jg * 4:jg * 4 + 32],
                    scalar1=(1.0 if n == 1 else 0.5), scalar2=None, op0=ALU.mult),
                    reads=[B_mod], writes=[B_mod])
        lam_tmp = vf32(alloc(256), 64)
        for j in range(2):
            li = 2 * j + 1
            lam_init = 0.8 - 0.6 * float(np.exp(-0.3 * li))
            for q in range(2):
                a = lvt[:, (j * 4 + 2 * q) * 64:(j * 4 + 2 * q + 1) * 64]
                b_ = lvt[:, (j * 4 + 2 * q + 1) * 64:(j * 4 + 2 * q + 2) * 64]
                P.add('dve', lambda e, a=a, b_=b_: e.tensor_tensor(out=lam_tmp, in0=a, in1=b_, op=ALU.mult),
                      reads=[B_const, B_mod], writes=[B_mod])
                P.add('dve', lambda e, j=j, q=q: e.reduce_sum(out=lamc[:, j * 4 + 2 + q:j * 4 + 3 + q], in_=lam_tmp, axis=AX.X),
                      reads=[B_mod], writes=[B_mod])
            P.add('act', lambda e, j=j: e.activation(out=lamc[:, j * 4 + 2:j * 4 + 4], in_=lamc[:, j * 4 + 2:j * 4 + 4], func=AF.Exp),
                  reads=[B_mod], writes=[B_mod])
            P.add('dve', lambda e, j=j, lam_init=lam_init: e.scalar_tensor_tensor(
                out=lamc[:, j * 4 + 1:j * 4 + 2], in0=lamc[:, j * 4 + 3:j * 4 + 4], scalar=-lam_init,
                in1=lamc[:, j * 4 + 2:j * 4 + 3], op0=ALU.add, op1=ALU.subtract), reads=[B_mod], writes=[B_mod])
            P.add('dve', lambda e, j=j, lam_init=lam_init: e.tensor_scalar(
                out=sublng[:, j:j + 1], in0=smallp[:, 418 + j:419 + j], scalar1=(1.0 - lam_init), scalar2=None, op0=ALU.mult),
                reads=[B_mod, B_const], writes=[B_mod])
        P.barrier()
        P.emit_pending(nc)
        pro_es.close()

        tmp = XTall = XT = XB = HTall = HT = HB = AT = AB = SQ = SQB = NTMP = TMP = TMPB = tmp_i = RT = RTB = RSTD = RSTDB = NRA = RA = RAB = RACH = ra_i = NRB = RBv = RBB = RBCH = rb_i = OTall = OT = OTB = OTCH = TAB = TABB = TABCH = NST = STG = STGB = STGCH = stg_i = VST = VSTB = VSTCH = PZS = PZSB = PZSCH = CQN = CQNB = XCH = XSCH = TOKEN_END = PSP = PSB = None

        def alloc_token():
            nonlocal tmp, XTall, XT, XB, HTall, HT, HB, AT, AB, SQ, SQB, NTMP, TMP, TMPB, tmp_i, RT, RTB, RSTD, RSTDB, NRA, RA, RAB, RACH, ra_i, NRB, RBv, RBB, RBCH, rb_i, OTall, OT, OTB, OTCH, TAB, TABB, TABCH, NST, STG, STGB, STGCH, stg_i, VST, VSTB, VSTCH, PZS, PZSB, PZSCH, CQN, CQNB, XCH, XSCH, TOKEN_END, PSP, PSB
            alloc_psum()
            XTall = vf32(None, 8 * T)
            XT = [XTall[:, c * T:(c + 1) * T] for c in range(8)]
            XB = [Buf(f"x{c}") for c in range(8)]
            HTall = vbf(None, 8 * T)
            HT = [HTall[:, c * T:(c + 1) * T] for c in range(8)]
            HB = [Buf(f"h{c}") for c in range(8)]
            AT = [vbf(None, T) for f in range(22)]
            AB = [Buf(f"a{f}") for f in range(22)]
            SQ = [vbf(None, T) for c in range(8)]
            SQB = [Buf(f"sq{c}") for c in range(8)]
            NTMP = 4
            TMP = [vf32(alloc(T * 4), T) for _ in range(NTMP)]
            TMPB = [Buf(f"tmp{k}") for k in range(NTMP)]
            tmp_i = [0]

            def tmp():
                k = tmp_i[0] % NTMP
                tmp_i[0] += 1
                return TMP[k], TMPB[k]

            RT = vf32(alloc(T * 4), T)
            RTB = Buf("rt")
            RSTD = vf32(alloc(T * 4), T)
            RSTDB = Buf("rstd")
            NRA = 4
            RA = [vbf(alloc(4096 * 2), 4096) for _ in range(NRA)]
            RAB = [Buf(f"ra{k}") for k in range(NRA)]
            RACH = [P.chan(f"ra{k}") for k in range(NRA)]
            ra_i = [0]
            NRB = 2
            RBv = [vbf(alloc(5632 * 2), 5632) for _ in range(NRB)]
            RBB = [Buf(f"rb{k}") for k in range(NRB)]
            RBCH = [P.chan(f"rb{k}") for k in range(NRB)]
            rb_i = [0]
            OTall = vbf(None, 8 * T)
            OT = [OTall[:, c * T:(c + 1) * T] for c in range(8)]
            OTB = Buf("ot")
            OTCH = P.chan("ot")
            TAB = [vf32(alloc(T * 4), T) for _ in range(4)]
            TABB = Buf("tab")
            TABCH = P.chan("tab")
            NST = 4
            STG = [vbf(alloc(T * 2), T) for _ in range(NST)]
            STGB = [Buf(f"stg{k}") for k in range(NST)]
            STGCH = [P.chan(f"stg{k}") for k in range(NST)]
            stg_i = [0]
            VST = vbf(alloc(4 * 640 * 2), 4 * 640)
            VSTB = Buf("vst")
            VSTCH = P.chan("vst")
            PZS = vf32(alloc(4 * T * 4), 4 * T)
            PZSB = Buf("pzs")
            PZSCH = P.chan("pzs")
            CQN = [vbf(alloc(T * 2), T) for _ in range(3)]
            CQNB = [Buf(f"cqn{k}") for k in range(3)]
            XCH = P.chan("xld")
            XSCH = P.chan("xst")
            TOKEN_END = cursor[0]


        def stg():
            k = stg_i[0] % NST
            stg_i[0] += 1
            return STG[k], STGB[k], STGCH[k]

        def ringA(src, n, wd):
            k = ra_i[0] % NRA
            ra_i[0] += 1
            dst = RA[k][:, 0:n]
            P.add('sp', lambda e: e.dma_start(out=dst, in_=src), reads=[wd], writes=[RAB[k]], chan=RACH[k])
            return RA[k], RAB[k]

        def ringB(src, wd):
            k = rb_i[0] % NRB
            rb_i[0] += 1
            dst = RBv[k]
            P.add('sp', lambda e: e.dma_start(out=dst, in_=src), reads=[wd], writes=[RBB[k]], chan=RBCH[k])
            return RBv[k], RBB[k]

        def mm(bk, lhsT, rhs, start, stop, reads, rows=None):
            out = bank(bk) if rows is None else bank(bk)[rows[0]:rows[1], :]
            P.add('pe', lambda e: e.matmul(out, lhsT, rhs, start=start, stop=stop), reads=reads, writes=[PSB[bk]])

        def sqrt_recip(bk, inv_n, rows=(0, 128)):
            r0, r1 = rows
            P.add('act', lambda e: e.activation(out=RT[r0:r1, :], in_=bank(bk)[r0:r1, :], func=AF.Sqrt,
                                                bias=epsc[r0:r1, :], scale=inv_n),
                  reads=[PSB[bk], B_const], writes=[RTB])
            P.add('dve', lambda e: e.reciprocal(out=RSTD[r0:r1, :], in_=RT[r0:r1, :]), reads=[RTB], writes=[RSTDB])

        def norm_main(Acols, Bcols, s):
            for c in range(8):
                P.add('act', lambda e, c=c: e.activation(out=SQ[c], in_=XT[c], func=AF.Square),
                      reads=[XB[c]], writes=[SQB[c]])
            for c in range(8):
                mm(6, ones_bf, SQ[c], c == 0, c == 7, [SQB[c], B_const])
            sqrt_recip(6, 1.0 / D)
            for c in range(8):
                tv, tb = tmp()
                P.add('dve', lambda e, c=c, tv=tv: e.scalar_tensor_tensor(
                    out=tv, in0=XT[c], scalar=Acols[:, c * 4 + s:c * 4 + s + 1], in1=RSTD, op0=ALU.mult, op1=ALU.mult),
                    reads=[XB[c], RSTDB, B_mod], writes=[tb])
                if Bcols is None:
                    continue
                P.add('act', lambda e, c=c, tv=tv: e.activation(out=HT[c], in_=tv, func=AF.Identity,
                                                                bias=Bcols[:, c * 4 + s:c * 4 + s + 1], scale=1.0),
                      reads=[tb, B_mod], writes=[HB[c]])

        def ffn(i, k, s):
            G = Gall[i][0 if k == 0 else 2]
            pp = 0
            for b in range(11):
                wv, wb = ringA(ffn_in_b[i * 2 + k, b], 4096, WD[i])
                w3 = wv.rearrange("p (k n) -> p k n", n=512)
                for j in range(2):
                    f = 2 * b + j
                    bg, bu = (0, 1) if pp % 2 == 0 else (2, 3)
                    pp += 1
                    for kc in range(8):
                        mm(bg, w3[:, kc, j * 128:(j + 1) * 128], HT[kc], kc == 0, kc == 7, [wb, HB[kc]])
                    for kc in range(8):
                        mm(bu, w3[:, kc, 256 + j * 128:256 + (j + 1) * 128], HT[kc], kc == 0, kc == 7, [wb, HB[kc]])
                    tv, tb = tmp()
                    P.add('act', lambda e, bg=bg, tv=tv: e.activation(out=tv, in_=bank(bg), func=AF.Silu),
                          reads=[PSB[bg]], writes=[tb])
                    P.add('dve', lambda e, bu=bu, tv=tv, f=f: e.tensor_tensor(out=AT[f], in0=bank(bu), in1=tv, op=ALU.mult),
                          reads=[PSB[bu], tb], writes=[AB[f]])
            for c2 in range(4):
                wv, wb = ringB(ffn_out_b[i * 2 + k, c2], WD[i])
                w3 = wv.rearrange("p (f n) -> p f n", n=256)
                for cc in range(2):
                    c = 2 * c2 + cc
                    by = 4 + (c % 2)
                    for f in range(22):
                        mm(by, w3[:, f, cc * 128:(cc + 1) * 128], AT[f], f == 0, f == 21, [wb, AB[f]])
                    P.add('dve', lambda e, c=c, by=by: e.scalar_tensor_tensor(
                        out=XT[c], in0=bank(by), scalar=G[:, c * 4 + s:c * 4 + s + 1], in1=XT[c], op0=ALU.mult, op1=ALU.add),
                        reads=[PSB[by], B_mod], writes=[XB[c]])

        def outproj(i, t, s):
            j = i // 2
            wsrc = ev_out_b if i % 2 == 0 else od_out_b
            G = Gall[i][1]
            for b in range(2):
                wv, wb = ringA(wsrc[j, b], 4096, WD[i])
                w3 = wv.rearrange("p (k n) -> p k n", n=512)
                for cc in range(4):
                    c = 4 * b + cc
                    by = 4 + (c % 2)
                    for kc in range(8):
                        mm(by, w3[:, kc, cc * 128:(cc + 1) * 128], OT[kc], kc == 0, kc == 7, [wb, OTB])
                    P.add('dve', lambda e, c=c, by=by: e.scalar_tensor_tensor(
                        out=XT[c], in0=bank(by), scalar=G[:, c * 4 + s:c * 4 + s + 1], in1=XT[c], op0=ALU.mult, op1=ALU.add),
                        reads=[PSB[by], B_mod], writes=[XB[c]])

        def store_rows(sv, sb, sch, pieces, t):
            for (r0, r1, dap) in pieces:
                P.add('sp', lambda e, r0=r0, r1=r1, dap=dap: e.dma_start(out=dap[:, t * T:(t + 1) * T], in_=sv[r0:r1, :]),
                      reads=[sb], chan=sch)

        def load_tabs(t, which):
            for q, w in enumerate(which):
                P.add('sp', lambda e, q=q, w=w: e.dma_start(out=TAB[q], in_=tabs_in[w][:, t * T:(t + 1) * T]),
                      writes=[TABB], chan=TABCH)

        def rope_combine(bd, bs, rows, cosT, sinT, out_ap, out_b, gcols=None, rstd=False):
            r0, r1 = rows
            t1, b1 = tmp()
            t2, b2 = tmp()
            if gcols is None:
                P.add('dve', lambda e: e.tensor_tensor(out=t1[r0:r1, :], in0=bank(bd)[r0:r1, :], in1=cosT[r0:r1, :], op=ALU.mult),
                      reads=[PSB[bd], TABB], writes=[b1])
                P.add('dve', lambda e: e.tensor_tensor(out=t2[r0:r1, :], in0=bank(bs)[r0:r1, :], in1=sinT[r0:r1, :], op=ALU.mult),
                      reads=[PSB[bs], TABB], writes=[b2])
            else:
                g, gs = gcols
                P.add('dve', lambda e: e.scalar_tensor_tensor(out=t1[r0:r1, :], in0=bank(bd)[r0:r1, :], scalar=g[r0:r1, :],
                                                              in1=cosT[r0:r1, :], op0=ALU.mult, op1=ALU.mult),
                      reads=[PSB[bd], TABB, B_const], writes=[b1])
                P.add('dve', lambda e: e.scalar_tensor_tensor(out=t2[r0:r1, :], in0=bank(bs)[r0:r1, :], scalar=gs[r0:r1, :],
                                                              in1=sinT[r0:r1, :], op0=ALU.mult, op1=ALU.mult),
                      reads=[PSB[bs], TABB, B_const], writes=[b2])
            if not rstd:
                P.add('dve', lambda e: e.tensor_tensor(out=out_ap[r0:r1, :], in0=t1[r0:r1, :], in1=t2[r0:r1, :], op=ALU.add),
                      reads=[b1, b2], writes=[out_b])
            else:
                P.add('dve', lambda e: e.tensor_tensor(out=t1[r0:r1, :], in0=t1[r0:r1, :], in1=t2[r0:r1, :], op=ALU.add),
                      reads=[b2], writes=[b1])
                P.add('dve', lambda e: e.tensor_tensor(out=out_ap[r0:r1, :], in0=t1[r0:r1, :], in1=RSTD[r0:r1, :], op=ALU.mult),
                      reads=[b1, RSTDB], writes=[out_b])

        def sub_norm(banks, gbase, nfeat):
            n = len(banks)
            for k, bk in enumerate(banks):
                P.add('act', lambda e, k=k, bk=bk: e.activation(out=SQ[k], in_=bank(bk), func=AF.Square),
                      reads=[PSB[bk]], writes=[SQB[k]])
            for k in range(n):
                mm(6, ones_bf, SQ[k], k == 0, k == n - 1, [SQB[k], B_const])
            sqrt_recip(6, 1.0 / nfeat)
            for k, bk in enumerate(banks):
                P.add('dve', lambda e, k=k, bk=bk: e.scalar_tensor_tensor(
                    out=CQN[k], in0=bank(bk), scalar=smallp[:, gbase + k:gbase + k + 1], in1=RSTD, op0=ALU.mult, op1=ALU.mult),
                    reads=[PSB[bk], RSTDB, B_const], writes=[CQNB[k]])

        def inproj_even(i, t):
            j = i // 2
            load_tabs(t, (0, 1))
            W = []
            offs = [(0, 4096), (4096, 4096), (8192, 1536)]
            ncol = [512, 512, 192]

            def getw(b):
                wv, wb = ringA(ev_in_b[j][:, offs[b][0]:offs[b][0] + offs[b][1]], offs[b][1], WD[i])
                return wv[:, 0:offs[b][1]].rearrange("p (k n) -> p k n", n=ncol[b]), wb

            def proj(bk, w3, wb, c0, c1, rows=None):
                for kc in range(8):
                    mm(bk, w3[:, kc, c0:c1], HT[kc], kc == 0, kc == 7, [wb, HB[kc]], rows=rows)

            w3, wb = getw(0)
            for c in range(3):
                proj(c, w3, wb, c * 128, (c + 1) * 128)
            proj(3, w3, wb, 384, 512)
            w3, wb = getw(1)
            proj(4, w3, wb, 0, 128)
            pzb = [5, 7, 5, 7]
            for g in range(3):
                proj(pzb[g], w3, wb, 128 + g * 128, 256 + g * 128)
                P.add('act', lambda e, g=g: e.activation(out=PZS[:, g * T:(g + 1) * T], in_=bank(pzb[g]), func=AF.Copy),
                      reads=[PSB[pzb[g]]], writes=[PZSB])
            w3, wb = getw(2)
            proj(7, w3, wb, 0, 128)
            P.add('act', lambda e: e.activation(out=PZS[:, 3 * T:4 * T], in_=bank(7), func=AF.Copy),
                  reads=[PSB[7]], writes=[PZSB])
            P.add('sp', lambda e: e.dma_start(out=pz_d[:, :, t * T:(t + 1) * T].rearrange("g p n -> p g n"),
                                              in_=PZS.rearrange("p (g n) -> p g n", n=T)), reads=[PZSB], chan=PZSCH)
            proj(5, w3, wb, 128, 160, rows=(0, 32))
            proj(7, w3, wb, 160, 192, rows=(0, 32))
            sv, sb, sch = stg()
            rope_combine(5, 7, (0, 32), TAB[0], TAB[1], sv, sb)
            store_rows(sv, sb, sch, [(0, 32, kpe_d)], t)
            sub_norm([0, 1, 2], 392 + j * 3, 384)
            wv, wb = ringA(ev_uq_b[j], 3072, WD[i])
            w3 = wv[:, 0:3072].rearrange("p (k n) -> p k n", n=1024)
            for cn in range(4):
                bk = cn % 2
                for kc in range(3):
                    mm(bk, w3[:, kc, cn * 128:(cn + 1) * 128], CQN[kc], kc == 0, kc == 2, [wb, CQNB[kc]])
                sv, sb, sch = stg()
                P.add('act', lambda e, bk=bk, sv=sv: e.activation(out=sv, in_=bank(bk), func=AF.Copy),
                      reads=[PSB[bk]], writes=[sb])
                store_rows(sv, sb, sch, [(0, 64, qT_d[2 * cn][0:64]), (64, 128, qT_d[2 * cn + 1][0:64])], t)
            for cr in range(2):
                for kc in range(3):
                    mm(0, w3[:, kc, 512 + cr * 128:512 + (cr + 1) * 128], CQN[kc], kc == 0, kc == 2, [wb, CQNB[kc]])
                for kc in range(3):
                    mm(1, w3[:, kc, 768 + cr * 128:768 + (cr + 1) * 128], CQN[kc], kc == 0, kc == 2, [wb, CQNB[kc]])
                sv, sb, sch = stg()
                rope_combine(0, 1, (0, 128), TAB[0], TAB[1], sv, sb)
                store_rows(sv, sb, sch, [(hh * 32, hh * 32 + 32, qT_d[4 * cr + hh][64:96]) for hh in range(4)], t)
            sub_norm([3, 4], 398 + j * 2, 256)
            wv, wb = ringA(ev_ukv_b[j], 2048, WD[i])
            w3 = wv[:, 0:2048].rearrange("p (k n) -> p k n", n=1024)
            for ck in range(4):
                bk = ck % 2
                for kc in range(2):
                    mm(bk, w3[:, kc, ck * 128:(ck + 1) * 128], CQN[kc], kc == 0, kc == 1, [wb, CQNB[kc]])
                sv, sb, sch = stg()
                P.add('act', lambda e, bk=bk, sv=sv: e.activation(out=sv, in_=bank(bk), func=AF.Copy),
                      reads=[PSB[bk]], writes=[sb])
                store_rows(sv, sb, sch, [(0, 64, kT_d[2 * ck][0:64]), (64, 128, kT_d[2 * ck + 1][0:64])], t)
            vst3 = VST.rearrange("p (b n) -> p b n", n=640)
            for tb_ in range(4):
                bk = 2 + tb_ % 2
                for kc in range(2):
                    mm(bk, CQN[kc][:, tb_ * 128:(tb_ + 1) * 128], w3[:, kc, 512:1024], kc == 0, kc == 1, [wb, CQNB[kc]])
                P.add('act', lambda e, bk=bk, tb_=tb_: e.activation(out=vst3[:, tb_, 0:512], in_=bank(bk), func=AF.Copy),
                      reads=[PSB[bk]], writes=[VSTB])
            for h in range(8):
                P.add('sp', lambda e, h=h: e.dma_start(out=v64_d[h][:, t * 4:(t + 1) * 4, :], in_=vst3[:, :, h * 64:(h + 1) * 64]),
                      reads=[VSTB], chan=VSTCH)

        def inproj_odd(i, t):
            j = i // 2
            load_tabs(t, (2, 3, 4, 5))
            gq = (smallp[:, 410 + j:411 + j], smallp[:, 412 + j:413 + j])
            gk = (smallp[:, 414 + j:415 + j], smallp[:, 416 + j:417 + j])
            seq = [('qc', c) for c in range(4)] + [('kc', 0)] + [('qd', c) for c in range(4)] + [('kd', c) for c in range(4)]
            cur = {'b': -1, 'w3': None, 'wb': None}

            def wcols(ci):
                b = ci // 4
                if b != cur['b']:
                    n = 4096 if b < 6 else 2048
                    wv, wb = ringA(od_in_b[j][:, b * 4096:b * 4096 + n], n, WD[i])
                    cur['b'] = b
                    cur['w3'] = wv[:, 0:n].rearrange("p (k n) -> p k n", n=n // 8)
                    cur['wb'] = wb
                return cur['w3'], cur['wb'], (ci % 4) * 128

            pp = 0
            for si, (kind, idx) in enumerate(seq):
                bd, bs = (0, 1) if pp % 2 == 0 else (2, 3)
                pp += 1
                for q, bk in enumerate((bd, bs)):
                    w3, wb, c0 = wcols(2 * si + q)
                    for kc in range(8):
                        mm(bk, w3[:, kc, c0:c0 + 128], HT[kc], kc == 0, kc == 7, [wb, HB[kc]])
                sv, sb, sch = stg()
                if kind in ('qc', 'kc'):
                    P.add('act', lambda e, bd=bd: e.activation(out=SQ[0], in_=bank(bd), func=AF.Square),
                          reads=[PSB[bd]], writes=[SQB[0]])
                    mm(6, blk_bf, SQ[0], True, True, [SQB[0], B_const])
                    sqrt_recip(6, 1.0 / 64)
                    rope_combine(bd, bs, (0, 128), TAB[0], TAB[1], sv, sb, gcols=(gq if kind == 'qc' else gk), rstd=True)
                    if kind == 'qc':
                        pieces = [(0, 64, qT_d[2 * idx][0:64]), (64, 128, qT_d[2 * idx + 1][0:64])]
                    else:
                        pieces = [(0, 64, kT_d[0][0:64]), (64, 128, kT_d[1][0:64])]
                else:
                    rope_combine(bd, bs, (0, 128), TAB[2], TAB[3], sv, sb)
                    if kind == 'qd':
                        pieces = [(0, 64, qT_d[8 + 2 * idx][0:64]), (64, 128, qT_d[8 + 2 * idx + 1][0:64])]
                    else:
                        pieces = [(0, 64, kT_d[2 + 2 * idx][0:64]), (64, 128, kT_d[2 + 2 * idx + 1][0:64])]
                store_rows(sv, sb, sch, pieces, t)
            wv, wb = ringA(od_v_b[j][:, 0:4096], 4096, WD[i])
            w3 = wv.rearrange("p (k n) -> p k n", n=512)
            wv2, wb2 = ringA(od_v_b[j][:, 4096:5120], 1024, WD[i])
            w32 = wv2[:, 0:1024].rearrange("p (k n) -> p k n", n=128)
            vst3 = VST.rearrange("p (b n) -> p b n", n=640)
            for tb_ in range(4):
                bk = tb_ % 2
                for kc in range(8):
                    mm(bk, HT[kc][:, tb_ * 128:(tb_ + 1) * 128], w3[:, kc, :], kc == 0, kc == 7, [wb, HB[kc]])
                P.add('act', lambda e, bk=bk, tb_=tb_: e.activation(out=vst3[:, tb_, 0:512], in_=bank(bk), func=AF.Copy),
                      reads=[PSB[bk]], writes=[VSTB])
                bk2 = 2 + tb_ % 2
                for kc in range(8):
                    P.add('pe', lambda e, kc=kc, bk2=bk2, tb_=tb_: e.matmul(
                        bank(bk2)[:, 0:128], HT[kc][:, tb_ * 128:(tb_ + 1) * 128], w32[:, kc, :], start=(kc == 0), stop=(kc == 7)),
                        reads=[wb2, HB[kc]], writes=[PSB[bk2]])
                P.add('act', lambda e, bk2=bk2, tb_=tb_: e.activation(out=vst3[:, tb_, 512:640], in_=bank(bk2)[:, 0:128], func=AF.Copy),
                      reads=[PSB[bk2]], writes=[VSTB])
            for h in range(4):
                P.add('sp', lambda e, h=h: e.dma_start(out=v128_d[h][:, t * 4:(t + 1) * 4, :], in_=vst3[:, :, h * 128:(h + 1) * 128]),
                      reads=[VSTB], chan=VSTCH)
            for h in range(2):
                P.add('sp', lambda e, h=h: e.dma_start(out=v64_d[h][:, t * 4:(t + 1) * 4, :], in_=vst3[:, :, 512 + h * 64:512 + (h + 1) * 64]),
                      reads=[VSTB], chan=VSTCH)

        def token_phase(ph):
            for t in range(NT):
                P.cut()
                s = t // 4
                src = xT_in if ph == 0 else xs_d
                P.add('sp', lambda e, src=src, t=t: e.dma_start(
                    out=XTall.rearrange("p (c n) -> p c n", n=T), in_=src[:, :, t * T:(t + 1) * T].rearrange("c p n -> p c n")),
                    writes=XB, chan=XCH)
                if ph > 0:
                    i = ph - 1
                    P.add('sp', lambda e, t=t: e.dma_start(
                        out=OTall.rearrange("p (c n) -> p c n", n=T), in_=oT_d[:, :, t * T:(t + 1) * T].rearrange("c p n -> p c n")),
                        writes=[OTB], chan=OTCH)
                    outproj(i, t, s)
                    norm_main(Aall[i][2], modT[i][:, 6 * 32:7 * 32], s)
                    ffn(i, 1, s)
                if ph < 4:
                    i = ph
                    norm_main(Aall[i][0], modT[i][:, 0:32], s)
                    ffn(i, 0, s)
                    norm_main(Aall[i][1], modT[i][:, 3 * 32:4 * 32], s)
                    if i % 2 == 0:
                        inproj_even(i, t)
                    else:
                        inproj_odd(i, t)
                    P.add('sp', lambda e, t=t: e.dma_start(
                        out=xs_d[:, :, t * T:(t + 1) * T].rearrange("c p n -> p c n"), in_=XTall.rearrange("p (c n) -> p c n", n=T)),
                        reads=XB, chan=XSCH)
                else:
                    for c in range(8):
                        P.add('act', lambda e, c=c: e.activation(out=SQ[c], in_=XT[c], func=AF.Square),
                              reads=[XB[c]], writes=[SQB[c]])
                    for c in range(8):
                        mm(6, ones_bf, SQ[c], c == 0, c == 7, [SQB[c], B_const])
                    sqrt_recip(6, 1.0 / D)
                    for c in range(8):
                        P.add('dve', lambda e, c=c: e.scalar_tensor_tensor(
                            out=XT[c], in0=XT[c], scalar=smallp[:, 384 + c:385 + c], in1=RSTD, op0=ALU.mult, op1=ALU.mult),
                            reads=[RSTDB, B_const], writes=[XB[c]])
                    P.add('sp', lambda e, t=t: e.dma_start(
                        out=yT_out[:, :, t * T:(t + 1) * T].rearrange("c p n -> p c n"), in_=XTall.rearrange("p (c n) -> p c n", n=T)),
                        reads=XB, chan=XSCH)

        ACC = ACCB = KT = QT = VA = VBt = KTB = QTB = VAB = VBB = KCH = QCH = VCH = NPT = PT = PTB = REC = RECB = NOS = OST = OSTB = OSTCH = ost_i = O1 = O2 = O1B = O2B = ASQ = ASQB = ART = ARTB = ARS = ARSB = ATT_END = PSP = PSB = None

        def alloc_attn():
            nonlocal ACC, ACCB, KT, QT, VA, VBt, KTB, QTB, VAB, VBB, KCH, QCH, VCH, NPT, PT, PTB, REC, RECB, NOS, OST, OSTB, OSTCH, ost_i, O1, O2, O1B, O2B, ASQ, ASQB, ART, ARTB, ARS, ARSB, ATT_END, PSP, PSB
            alloc_psum()
            KT = [vbf(alloc(NTOK * 2), NTOK) for _ in range(2)]
            QT = [vbf(alloc(NTOK * 2), NTOK) for _ in range(2)]
            VA = [vbf(alloc(64 * 128 * 2), 64 * 128) for _ in range(2)]
            VBt = [vbf(alloc(64 * 128 * 2), 64 * 128) for _ in range(2)]
            KTB = [Buf("kt0"), Buf("kt1")]
            QTB = [Buf("qt0"), Buf("qt1")]
            VAB = [Buf("va0"), Buf("va1")]
            VBB = [Buf("vb0"), Buf("vb1")]
            KCH = [P.chan("k0"), P.chan("k1")]
            QCH = [P.chan("q0"), P.chan("q1")]
            VCH = [P.chan("v0"), P.chan("v1")]
            NPT = 3
            PT = [vbf(alloc(1024 * 2), 1024) for _ in range(NPT)]
            PTB = [Buf(f"pt{k}") for k in range(NPT)]
            REC = [vf32(alloc(T * 4), T) for _ in range(2)]
            RECB = [Buf("rec0"), Buf("rec1")]
            NOS = 3
            OST = [vbf(alloc(T * 2), T) for _ in range(NOS)]
            OSTB = [Buf(f"ost{k}") for k in range(NOS)]
            OSTCH = [P.chan(f"ost{k}") for k in range(NOS)]
            ost_i = [0]
            O1 = vf32(alloc(T * 4), T)
            O2 = vf32(alloc(T * 4), T)
            O1B = Buf("o1")
            O2B = Buf("o2")
            ASQ = vbf(alloc(T * 2), T)
            ASQB = Buf("asq")
            ART = vf32(alloc(T * 4), T)
            ARTB = Buf("art")
            ARS = vf32(alloc(T * 4), T)
            ARSB = Buf("ars")
            ACC = [vf32(alloc(1024 * 4), 1024) for _ in range(4)]
            ACCB = [Buf(f"acc{k}") for k in range(4)]
            ATT_END = cursor[0]

        PZW = PZ = PW1 = PW2 = ICN = PD = PZB = PW1B = PW2B = ICNB = PDB = PZCH = ICNCH = HLCH = HRCH = POST = POSTB = POSTCH = POOL_END = PSP = PSB = None

        def alloc_pool():
            nonlocal PZW, PZ, PW1, PW2, ICN, PD, PZB, PW1B, PW2B, ICNB, PDB, PZCH, ICNCH, HLCH, HRCH, POST, POSTB, POSTCH, POOL_END, PSP, PSB
            alloc_psum()
            PZW = 2048 + 16
            PZ = vf32(alloc(PZW * 4), PZW)
            PW1 = vf32(alloc(PZW * 4), PZW)
            PW2 = vf32(alloc(PZW * 4), PZW)
            ICN = vf32(alloc(2048 * 4), 2048)
            PD = vbf(alloc(2048 * 2), 2048)
            PZB, PW1B, PW2B, ICNB, PDB = Buf("pz"), Buf("pw1"), Buf("pw2"), Buf("icn"), Buf("pd")
            PZCH = P.chan("pzl")
            ICNCH = P.chan("icn")
            HLCH = P.chan("hl")
            HRCH = P.chan("hr")
            POST = [vbf(alloc(T * 2), T) for _ in range(2)]
            POSTB = [Buf("post0"), Buf("post1")]
            POSTCH = [P.chan("post0"), P.chan("post1")]
            POOL_END = cursor[0]


        def ost():
            k = ost_i[0] % NOS
            ost_i[0] += 1
            return OST[k], OSTB[k], OSTCH[k]

        set_i = [0]

        def load_qk(qsrcs, ksrcs):
            sl = set_i[0] % 2
            for (r0, r1, ap) in qsrcs:
                P.add('sp', lambda e, r0=r0, r1=r1, ap=ap: e.dma_start(out=QT[sl][r0:r1, :], in_=ap), writes=[QTB[sl]], chan=QCH[sl])
            for (r0, r1, ap) in ksrcs:
                P.add('sp', lambda e, r0=r0, r1=r1, ap=ap: e.dma_start(out=KT[sl][r0:r1, :], in_=ap), writes=[KTB[sl]], chan=KCH[sl])
            return sl

        NSS = 3

        def attn_units(sl, dk, scale, pv, obanks_for_qg, finalize, vreads):
            units = [(qg, kp) for qg in range(16) for kp in range(32)]
            pt_i = [0]

            def qk(u):
                qg, kp = units[u]
                sb = (u % NSS) * 2
                for kk in range(2):
                    kb = 2 * kp + kk
                    r0, r1 = (64, 128) if (dk == 64 and kk == 1) else (0, dk)
                    mm(sb + kk, KT[sl][r0:r1, kb * 128:(kb + 1) * 128], QT[sl][r0:r1, qg * T:(qg + 1) * T], True, True,
                       [KTB[sl], QTB[sl]])

            qk(0)
            qk(1)
            for u in range(len(units)):
                qg, kp = units[u]
                if u + 2 < len(units):
                    qk(u + 2)
                sb = (u % NSS) * 2
                k = pt_i[0] % NPT
                pt_i[0] += 1
                kseg = (2 * kp) // 16
                mcol = kseg * 4 + qg // 4
                P.add('act', lambda e, sb=sb, k=k, mcol=mcol: e.activation(
                    out=PT[k], in_=PSP[sb // 2], func=AF.Exp, bias=maskb[:, mcol:mcol + 1], scale=scale),
                    reads=[PSB[sb], PSB[sb + 1], B_const], writes=[PTB[k]])
                obs = obanks_for_qg(qg)
                for kk in range(2):
                    kb = 2 * kp + kk
                    for (getl, ob) in zip(pv, obs):
                        mm(ob, getl(kb), PT[k][:, kk * T:(kk + 1) * T], kb == 0, kb == 63, [PTB[k]] + vreads)
                if kp == 31:
                    finalize(qg, obs)

        va_ones_done = [False]

        def ensure_ones():
            for sl in range(2):
                va3 = VA[sl].rearrange("p (k n) -> p k n", n=128)
                vb3 = VBt[sl].rearrange("p (k n) -> p k n", n=128)
                P.add('dve', lambda e, va3=va3: e.memset(va3[:, :, 64:128], 1.0), writes=[VAB[sl]])
                P.add('dve', lambda e, vb3=vb3: e.memset(vb3[:, :, 0:64], 1.0), writes=[VBB[sl]])

        def dv64_load(spec):
            qsrcs, ksrcs, vsrc = spec[0], spec[1], spec[2]
            sl = load_qk(qsrcs, ksrcs)
            set_i[0] += 1
            va3 = VA[sl].rearrange("p (k n) -> p k n", n=128)
            P.add('sp', lambda e: e.dma_start(out=va3[:, :, 0:64], in_=vsrc), writes=[VAB[sl]], chan=VCH[sl])
            return sl

        def dv64_compute(spec, sl):
            dk, scale, out_chunk, out_row0, obase = spec[3:]
            va3 = VA[sl].rearrange("p (k n) -> p k n", n=128)

            def fin(qg, obs):
                ob = obs[0]
                r = qg % 2
                P.add('dve', lambda e: e.reciprocal(out=REC[r][64:128, :], in_=bank(ob)[64:128, :]), reads=[PSB[ob]], writes=[RECB[r]])
                sv, sb_, sch = ost()
                P.add('dve', lambda e: e.tensor_tensor(out=sv[0:64, :], in0=bank(ob)[0:64, :], in1=REC[r][64:128, :], op=ALU.mult),
                      reads=[PSB[ob], RECB[r]], writes=[sb_])
                P.add('sp', lambda e: e.dma_start(out=oT_d[out_chunk][out_row0:out_row0 + 64, qg * T:(qg + 1) * T], in_=sv[0:64, :]),
                      reads=[sb_], chan=sch)

            attn_units(sl, dk, scale, [lambda kb: va3[:, kb, :]], lambda qg: [6 + (qg + obase) % 2], fin, [VAB[sl]])

        def attn_dv64_list(specs):
            sl = dv64_load(specs[0])
            for m in range(len(specs)):
                nsl = dv64_load(specs[m + 1]) if m + 1 < len(specs) else None
                dv64_compute(specs[m], sl)
                P.cut()
                sl = nsl

        def attn_diff(j, h):
            sls = []
            for w in range(2):
                sl = load_qk([(0, 64, qT_d[8 + 2 * h + w][0:64]), (64, 128, qT_d[8 + 2 * h + w][0:64])],
                             [(0, 64, kT_d[2 + 2 * h + w][0:64]), (64, 128, kT_d[2 + 2 * h + w][0:64])])
                set_i[0] += 1
                sls.append(sl)
            vs = h % 2
            va3 = VA[vs].rearrange("p (k n) -> p k n", n=128)
            P.add('sp', lambda e: e.dma_start(out=va3, in_=v128_d[h]), writes=[VAB[vs]], chan=VCH[vs])
            for qg in range(16):
                if qg % 4 == 0:
                    P.cut()
                for w in range(2):
                    sl = sls[w]
                    ob = 4 + w

                    def qk(kp, sl=sl):
                        sb = (kp % 2) * 2
                        for kk in range(2):
                            kb = 2 * kp + kk
                            r0 = 64 * kk
                            mm(sb + kk, KT[sl][r0:r0 + 64, kb * 128:(kb + 1) * 128], QT[sl][r0:r0 + 64, qg * T:(qg + 1) * T], True, True,
                               [KTB[sl], QTB[sl]])
                    qk(0)
                    for kp in range(32):
                        if kp + 1 < 32:
                            qk(kp + 1)
                        sb = (kp % 2) * 2
                        k = (kp + w) % NPT
                        mcol = ((2 * kp) // 16) * 4 + qg // 4
                        P.add('act', lambda e, sb=sb, k=k, mcol=mcol: e.activation(
                            out=PT[k], in_=PSP[sb // 2], func=AF.Exp, bias=maskb[:, mcol:mcol + 1], scale=0.125),
                            reads=[PSB[sb], PSB[sb + 1], B_const], writes=[PTB[k]])
                        ai = w
                        if kp % 2 == 0:
                            if kp == 0:
                                P.add('dve', lambda e, ai=ai, k=k: e.tensor_copy(ACC[ai], PT[k]), reads=[PTB[k]], writes=[ACCB[ai]])
                            else:
                                P.add('dve', lambda e, ai=ai, k=k: e.tensor_tensor(out=ACC[ai], in0=ACC[ai], in1=PT[k], op=ALU.add),
                                      reads=[PTB[k]], writes=[ACCB[ai]])
                        for kk in range(2):
                            kb = 2 * kp + kk
                            mm(ob, va3[:, kb, :], PT[k][:, kk * T:(kk + 1) * T], kb == 0, kb == 63, [PTB[k], VAB[vs]])
                        if kp % 2 == 1:
                            for kk in range(2):
                                mm(6, ones_bf, PT[k][:, kk * T:(kk + 1) * T], (kp == 1 and kk == 0), False, [PTB[k], B_const])
                    for hh in range(2):
                        mm(6, ones_f, ACC[w][:, hh * T:(hh + 1) * T], False, hh == 1, [ACCB[w], B_const])
                    ov, ovb = (O1, O1B) if w == 0 else (O2, O2B)
                    P.add('dve', lambda e, w=w: e.reciprocal(out=REC[w], in_=bank(6)), reads=[PSB[6]], writes=[RECB[w]])
                    P.add('dve', lambda e, w=w, ob=ob, ov=ov: e.tensor_tensor(out=ov, in0=bank(ob), in1=REC[w], op=ALU.mult),
                          reads=[PSB[ob], RECB[w]], writes=[ovb])
                P.add('dve', lambda e: e.scalar_tensor_tensor(out=O1, in0=O2, scalar=lamc[:, j * 4 + 1:j * 4 + 2], in1=O1, op0=ALU.mult, op1=ALU.add),
                      reads=[O2B, B_mod], writes=[O1B])
                P.add('act', lambda e: e.activation(out=ASQ, in_=O1, func=AF.Square), reads=[O1B], writes=[ASQB])
                mm(7, ones_bf, ASQ, True, True, [ASQB, B_const])
                P.add('act', lambda e: e.activation(out=ART, in_=bank(7), func=AF.Sqrt, bias=epsc, scale=1.0 / 128), reads=[PSB[7], B_const], writes=[ARTB])
                P.add('dve', lambda e: e.reciprocal(out=ARS, in_=ART), reads=[ARTB], writes=[ARSB])
                sv, sb_, sch = ost()
                P.add('dve', lambda e, sv=sv: e.scalar_tensor_tensor(out=sv, in0=O1, scalar=sublng[:, j:j + 1], in1=ARS, op0=ALU.mult, op1=ALU.mult),
                      reads=[O1B, ARSB, B_mod], writes=[sb_])
                P.add('sp', lambda e, sv=sv, qg=qg: e.dma_start(out=oT_d[4 + h][:, qg * T:(qg + 1) * T], in_=sv), reads=[sb_], chan=sch)

        def pool_branch(j):
            pw = poolw.rearrange("p (g n) -> p g n", n=128)
            post_i = 0
            for g, w in enumerate((2, 4, 8, 16)):
                P.cut()
                for seg in range(4):
                    t0 = seg * 2048
                    P.add('sp', lambda e, g=g, t0=t0: e.dma_start(out=PZ[:, 8:8 + 2048], in_=pz_d[g][:, t0:t0 + 2048]), writes=[PZB], chan=PZCH)
                    P.add('sp', lambda e, g=g, t0=t0: e.dma_start(out=ICN, in_=icnt_in[g:g + 1, t0:t0 + 2048].partition_broadcast(128)), writes=[ICNB], chan=ICNCH)
                    if seg > 0:
                        P.add('sp', lambda e, g=g, t0=t0: e.dma_start(out=PZ[:, 0:8], in_=pz_d[g][:, t0 - 8:t0]), writes=[PZB], chan=HLCH)
                        P.add('dve', lambda e: e.tensor_scalar(out=PZ[:, 0:8], in0=PZ[:, 0:8], scalar1=flag[:, 0:1], scalar2=None, op0=ALU.mult),
                              reads=[B_const], writes=[PZB])
                    else:
                        P.add('dve', lambda e: e.memset(PZ[:, 0:8], 0.0), writes=[PZB])
                    if seg < 3:
                        P.add('sp', lambda e, g=g, t0=t0: e.dma_start(out=PZ[:, 2056:2064], in_=pz_d[g][:, t0 + 2048:t0 + 2056]), writes=[PZB], chan=HRCH)
                        P.add('dve', lambda e: e.tensor_scalar(out=PZ[:, 2056:2064], in0=PZ[:, 2056:2064], scalar1=flag[:, 0:1], scalar2=None, op0=ALU.mult),
                              reads=[B_const], writes=[PZB])
                    else:
                        P.add('dve', lambda e: e.memset(PZ[:, 2056:2064], 0.0), writes=[PZB])
                    src, srcb = PZ, PZB
                    step = 1
                    n = PZW
                    bufs = [(PW1, PW1B), (PW2, PW2B)]
                    bi = 0
                    while step < w:
                        dst, dstb = bufs[bi % 2]
                        bi += 1
                        n2 = n - step
                        P.add('dve', lambda e, src=src, dst=dst, n2=n2, step=step: e.tensor_tensor(
                            out=dst[:, 0:n2], in0=src[:, 0:n2], in1=src[:, step:step + n2], op=ALU.add),
                            reads=[srcb], writes=[dstb])
                        src, srcb = dst, dstb
                        n = n2
                        step *= 2
                    o0 = 8 - w // 2
                    dst, dstb = bufs[bi % 2]
                    P.add('dve', lambda e, src=src, dst=dst, o0=o0: e.tensor_tensor(
                        out=dst[:, 0:2048], in0=src[:, o0:o0 + 2048], in1=ICN, op=ALU.mult), reads=[srcb, ICNB], writes=[dstb])
                    P.add('dve', lambda e, dst=dst: e.tensor_tensor(out=PD, in0=dst[:, 0:2048], in1=PZ[:, 8:8 + 2048], op=ALU.subtract),
                          reads=[dstb, PZB], writes=[PDB])
                    for tq in range(4):
                        bk = tq % 2
                        mm(bk, pw[:, j * 4 + g, :], PD[:, tq * T:(tq + 1) * T], True, True, [PDB, B_const])
                        k = post_i % 2
                        post_i += 1
                        P.add('act', lambda e, bk=bk, k=k, g=g: e.activation(out=POST[k], in_=bank(bk), func=AF.Identity,
                                                                           scale=smallp[:, 402 + j * 4 + g:403 + j * 4 + g]),
                              reads=[PSB[bk], B_const], writes=[POSTB[k]])
                        P.add('sp', lambda e, k=k, g=g, t0=t0, tq=tq: e.dma_start(
                            out=oT_d[4 + g][:, t0 + tq * T:t0 + (tq + 1) * T], in_=POST[k]), reads=[POSTB[k]], chan=POSTCH[k])

        def dv64_specs_even():
            return [([(0, 96, qT_d[h][0:96])], [(0, 64, kT_d[h][0:64]), (64, 96, kpe_d[:, :])], v64_d[h],
                     96, float(96 ** -0.5), h // 2, (h % 2) * 64, h) for h in range(8)]

        def dv64_specs_odd():
            return [([(0, 64, qT_d[m][0:64]), (64, 128, qT_d[m][0:64])],
                     [(0, 64, kT_d[m // 4][0:64]), (64, 128, kT_d[m // 4][0:64])], v64_d[m // 4],
                     64, 0.125, m // 2, (m % 2) * 64, m) for m in range(8)]

        def attention_phase(i):
            j = i // 2
            if i % 2 == 0:
                with contextlib.ExitStack() as pes:
                    cur_es[0] = pes
                    alloc_pool()
                    pool_branch(j)
                    P.barrier()
                    P.emit_pending(nc)
                with contextlib.ExitStack() as pes:
                    cur_es[0] = pes
                    alloc_attn()
                    ensure_ones()
                    attn_dv64_list(dv64_specs_even())
                    P.barrier()
                    P.emit_pending(nc)
            else:
                with contextlib.ExitStack() as pes:
                    cur_es[0] = pes
                    alloc_attn()
                    ensure_ones()
                    attn_dv64_list(dv64_specs_odd())
                    P.barrier()
                    for h in range(4):
                        attn_diff(j, h)
                    P.barrier()
                    P.emit_pending(nc)

        for ph in range(5):
            with contextlib.ExitStack() as pes:
                cur_es[0] = pes
                alloc_token()
                token_phase(ph)
                P.barrier()
                P.emit_pending(nc)
            if ph < 4:
                attention_phase(ph)
    return nc


_CACHE = {}


def kernel(**inputs):
    inp = {k: np.asarray(v) for k, v in inputs.items()}
    sh = _prep_shared(inp)
    pos_s = np.arange(NTOK)
    pos_p = np.arange(NTOK) % 2048
    tabs_s = _rope_tables(pos_s, pos_s // 64, pos_s % 64)
    tabs_p = _rope_tables(pos_p, pos_p // 64, pos_p % 64)
    icnt_s = _icnt(NTOK)
    icnt_p = _icnt(2048)
    in_maps = []
    for r in range(NCORES):
        m = _prep_core(inp, r, tabs_s, tabs_p, icnt_s, icnt_p)
        m.update(sh)
        in_maps.append(m)
    if 'nc' not in _CACHE:
        _CACHE['nc'] = build_program()
    nc = _CACHE['nc']
    res = run_bass_kernel_spmd(nc, in_maps, core_ids=list(range(NCORES)))
    outs = [np.asarray(r["yT"]).reshape(D, NTOK).T for r in res.results]
    y_sample = np.stack([np.ascontiguousarray(outs[r]) for r in range(4)]).astype(np.float32)
    y_prompt = np.concatenate([outs[r].reshape(4, 2048, D) for r in range(4, 8)], axis=0).astype(np.float32)
    return (np.ascontiguousarray(y_prompt), y_sample)
```

```python
import contextlib
import numpy as np
import concourse.bass as bass
import concourse.mybir as mybir
from concourse.bass_utils import run_bass_kernel_spmd

F32 = mybir.dt.float32
BF16 = mybir.dt.bfloat16
AF = mybir.ActivationFunctionType
ALU = mybir.AluOpType
AX = mybir.AxisListType

D = 1024
DFF = 2816
DEPTH = 4
NTOK = 8192
T = 512
NT = NTOK // T
EPS = 1e-6
NCORES = 8
NS = 420

ENGS = ['pe', 'act', 'dve', 'pool', 'sp']


class Buf:
    __slots__ = ('name', 'last_w', 'readers')

    def __init__(self, name):
        self.name = name
        self.last_w = None
        self.readers = []


class Chan:
    __slots__ = ('name', 'sem', 'count', 'nobar')

    def __init__(self, name, sem):
        self.name = name
        self.sem = sem
        self.count = 0
        self.nobar = False


class Op:
    __slots__ = ('eng', 'fn', 'waits', 'signal', 'chan', 'sigidx')

    def __init__(self, eng, fn):
        self.eng = eng
        self.fn = fn
        self.waits = []
        self.signal = False
        self.chan = None
        self.sigidx = 0


class Prog:
    def __init__(self, nc, sems):
        self.nc = nc
        self.free_sems = list(sems)
        self.ops = {e: [] for e in ENGS}
        self.esem = {e: self.free_sems.pop() for e in ENGS}
        self.waited_op = {e: {t: -1 for t in ENGS} for e in ENGS}
        self.waited_ch = {e: {} for e in ENGS}
        self.chans = []
        self.cuts = []
        self.emitted = {e: 0 for e in ENGS}
        self.nsig = {e: 0 for e in ENGS}

    def chan(self, name):
        for c in self.chans:
            if c.name == name:
                return c
        c = Chan(name, self.free_sems.pop())
        self.chans.append(c)
        return c

    def _add_wait(self, op, ref):
        e = op.eng
        if ref[0] == 'op':
            _, te, idx = ref
            if te == e and e in ('pe', 'sp'):
                return
            if self.waited_op[e][te] >= idx:
                return
            self.waited_op[e][te] = idx
            if idx < self.emitted[te] and not self.ops[te][idx].signal:
                raise AssertionError(f"late signal request on emitted op {te}[{idx}] from {e}")
            self.ops[te][idx].signal = True
            op.waits.append(ref)
        else:
            _, ch, cnt = ref
            if self.waited_ch[e].get(ch, 0) >= cnt:
                return
            self.waited_ch[e][ch] = cnt
            op.waits.append(ref)

    def add(self, eng, fn, reads=(), writes=(), chan=None):
        op = Op(eng, fn)
        for b in reads:
            if b.last_w is not None:
                self._add_wait(op, b.last_w)
        for b in writes:
            if b.last_w is not None:
                self._add_wait(op, b.last_w)
            for r in b.readers:
                self._add_wait(op, r)
        idx = len(self.ops[eng])
        self.ops[eng].append(op)
        if chan is not None:
            chan.count += 16
            op.chan = chan
            ref = ('dma', chan, chan.count)
        else:
            ref = ('op', eng, idx)
        for b in writes:
            b.last_w = ref
            b.readers = []
        for b in reads:
            if b in writes:
                continue
            if ref[0] == 'op':
                b.readers = [r for r in b.readers if not (r[0] == 'op' and r[1] == eng)]
            else:
                b.readers = [r for r in b.readers if not (r[0] == 'dma' and r[1] is chan)]
            b.readers.append(ref)
        return ref

    def wait_refs(self, eng, refs):
        op = Op(eng, None)
        for r in refs:
            self._add_wait(op, r)
        self.ops[eng].append(op)

    def barrier(self):
        refs = []
        for e in ('pe', 'act', 'dve'):
            if self.ops[e]:
                for idx in range(len(self.ops[e]) - 1, -1, -1):
                    if self.ops[e][idx].fn is not None and self.ops[e][idx].chan is None:
                        refs.append(('op', e, idx))
                        break
        for c in self.chans:
            if c.count and not c.nobar:
                refs.append(('dma', c, c.count))
        for e in ENGS:
            self.wait_refs(e, refs)

    def cut(self):
        self.cuts.append({e: len(self.ops[e]) for e in ENGS})

    def emit_pending(self, nc):
        prog = self
        for e in ENGS:
            for op in self.ops[e][self.emitted[e]:]:
                if op.signal:
                    self.nsig[e] += 1
                    op.sigidx = self.nsig[e]
        end = {e: len(self.ops[e]) for e in ENGS}
        bounds = [c for c in self.cuts if all(c[e] >= self.emitted[e] for e in ENGS)] + [end]
        self.cuts = []
        prev = dict(self.emitted)

        def run(engname, eng, lo, hi):
            for op in prog.ops[engname][lo:hi]:
                for r in op.waits:
                    if r[0] == 'op':
                        eng.wait_ge(prog.esem[r[1]], prog.ops[r[1]][r[2]].sigidx)
                    else:
                        eng.wait_ge(r[1].sem, r[2])
                if op.fn is None:
                    continue
                inst = op.fn(eng)
                if op.chan is not None:
                    inst.then_inc(op.chan.sem, 16)
                elif op.signal:
                    inst.then_inc(prog.esem[engname], 1)
                op.fn = None

        for bnd in bounds:
            if all(bnd[e] == prev[e] for e in ENGS):
                continue
            with nc.Block() as block:
                if bnd['pe'] > prev['pe']:
                    @block.tensor
                    def _(t, lo=prev['pe'], hi=bnd['pe']):
                        run('pe', t, lo, hi)
                if bnd['act'] > prev['act']:
                    @block.scalar
                    def _(s, lo=prev['act'], hi=bnd['act']):
                        run('act', s, lo, hi)
                if bnd['dve'] > prev['dve']:
                    @block.vector
                    def _(v, lo=prev['dve'], hi=bnd['dve']):
                        run('dve', v, lo, hi)
                if bnd['pool'] > prev['pool']:
                    @block.gpsimd
                    def _(g, lo=prev['pool'], hi=bnd['pool']):
                        run('pool', g, lo, hi)
                if bnd['sp'] > prev['sp']:
                    @block.sync
                    def _(sy, lo=prev['sp'], hi=bnd['sp']):
                        run('sp', sy, lo, hi)
            prev = bnd
        self.emitted = end


def _kblocks(w, cols):
    K = w.shape[0]
    sub = w[:, cols]
    return np.ascontiguousarray(sub.reshape(K // 128, 128, len(cols)).transpose(1, 0, 2)).reshape(128, -1)


def _swap_mla(d):
    return (d + 16) % 32


def _swap_ax(d):
    return (d + 16) % 32 if d < 32 else 32 + ((d - 32) + 16) % 32


def _swap_diff(d):
    return (d + 8) % 16 if d < 16 else d


def _rope_tables(pos, rows, cols):
    f32 = np.float32
    n = pos.shape[0]
    out = np.zeros((6, 128, n), f32)
    posf = pos.astype(f32)
    rowf = rows.astype(f32)
    colf = cols.astype(f32)
    inv_m = (f32(500000.0) ** (-np.arange(0, 16, dtype=f32) * f32(2.0) / f32(32))).astype(f32)
    inv_a = (f32(10000.0) ** (-np.arange(0, 16, dtype=f32) * f32(2.0) / f32(32))).astype(f32)
    inv_d = (f32(500000.0) ** (-np.arange(0, 8, dtype=f32) * f32(2.0) / f32(16))).astype(f32)
    for p in range(128):
        d = p % 32
        ang = (posf * inv_m[d % 16]).astype(f32)
        out[0, p] = np.cos(ang)
        out[1, p] = np.sin(ang) * (f32(-1.0) if d < 16 else f32(1.0))
        d = p % 64
        if d < 32:
            ang = (rowf * inv_a[d % 16]).astype(f32)
            sg = -1.0 if d < 16 else 1.0
        else:
            ang = (colf * inv_a[(d - 32) % 16]).astype(f32)
            sg = -1.0 if (d - 32) < 16 else 1.0
        out[2, p] = np.cos(ang)
        out[3, p] = np.sin(ang) * f32(sg)
        if d < 16:
            ang = (posf * inv_d[d % 8]).astype(f32)
            out[4, p] = np.cos(ang)
            out[5, p] = np.sin(ang) * (f32(-1.0) if d < 8 else f32(1.0))
        else:
            out[4, p] = 1.0
            out[5, p] = 0.0
    return out


def _prep_shared(inp):
    sh = {}
    w_ada = inp['w_ada']
    wada = np.empty((32, 128, 8 * 1152), np.float32)
    for i in range(4):
        for b in range(8):
            wada[i * 8 + b] = _kblocks(w_ada[i], np.arange(b * 1152, (b + 1) * 1152))
    sh['wada'] = wada
    fin = np.empty((8, 11, 128, 4096), np.float32)
    fout = np.empty((8, 4, 128, 5632), np.float32)
    for i in range(4):
        for k in range(2):
            w_in = inp['ffn_w_in'][i, k]
            w_out = inp['ffn_w_out'][i, k]
            for b in range(11):
                cols = np.concatenate([np.arange(256 * b, 256 * b + 256), DFF + np.arange(256 * b, 256 * b + 256)])
                fin[i * 2 + k, b] = _kblocks(w_in, cols)
            for c2 in range(4):
                fout[i * 2 + k, c2] = _kblocks(w_out, np.arange(256 * c2, 256 * c2 + 256))
    sh['ffn_in'] = fin
    sh['ffn_out'] = fout
    ev_in = np.empty((2, 128, 8 * 1216), np.float32)
    ev_uq = np.empty((2, 128, 3 * 1024), np.float32)
    ev_ukv = np.empty((2, 128, 2 * 1024), np.float32)
    ev_out = np.empty((2, 2, 128, 4096), np.float32)
    poolw = np.empty((128, 2 * 4 * 128), np.float32)
    for j in range(2):
        w = inp['ev_w_in'][j]
        kpe = 640 + np.arange(32)
        kpes = 640 + np.array([_swap_mla(d) for d in range(32)])
        cols = np.concatenate([np.arange(0, 640), 672 + np.arange(512), kpe, kpes])
        blocks = [cols[0:512], cols[512:1024], cols[1024:1216]]
        ev_in[j] = np.concatenate([_kblocks(w, b) for b in blocks], axis=1)
        wq = inp['mla_w_uq'][j]
        qc = []
        for h in range(8):
            qc.append(h * 96 + np.arange(64))
        for h in range(8):
            qc.append(h * 96 + 64 + np.arange(32))
        for h in range(8):
            qc.append(h * 96 + 64 + np.array([_swap_mla(d) for d in range(32)]))
        ev_uq[j] = _kblocks(wq, np.concatenate(qc))
        wkv = inp['mla_w_ukv'][j]
        kc = [h * 128 + np.arange(64) for h in range(8)] + [h * 128 + 64 + np.arange(64) for h in range(8)]
        ev_ukv[j] = _kblocks(wkv, np.concatenate(kc))
        wo = inp['ev_w_out'][j]
        for b in range(2):
            ev_out[j, b] = _kblocks(wo, np.arange(512 * b, 512 * b + 512))
        for g in range(4):
            poolw[:, (j * 4 + g) * 128:(j * 4 + g + 1) * 128] = inp['pool_w'][j, g]
    sh['ev_in'] = ev_in
    sh['ev_uq'] = ev_uq
    sh['ev_ukv'] = ev_ukv
    sh['ev_out'] = ev_out
    sh['poolw'] = poolw
    od_in = np.empty((2, 128, 8 * 3328), np.float32)
    od_v = np.empty((2, 128, 8 * 640), np.float32)
    od_out = np.empty((2, 2, 128, 4096), np.float32)
    sw_ax = np.array([_swap_ax(d) for d in range(64)])
    sw_df = np.array([_swap_diff(d) for d in range(64)])
    for j in range(2):
        w = inp['od_w_in'][j]
        chunks = []

        def ch(base, c, sw):
            direct = base + c * 128 + np.arange(128)
            swp = base + c * 128 + np.concatenate([sw, 64 + sw])
            chunks.append(direct)
            chunks.append(swp)

        for c in range(4):
            ch(0, c, sw_ax)
        ch(512, 0, sw_ax)
        for c in range(4):
            ch(768, c, sw_df)
        for c in range(4):
            ch(1280, c, sw_df)
        cols = np.concatenate(chunks)
        blocks = [cols[b * 512:(b + 1) * 512] for b in range(7)]
        od_in[j] = np.concatenate([_kblocks(w, b) for b in blocks], axis=1)
        od_v[j] = np.concatenate([_kblocks(w, 1792 + np.arange(512)), _kblocks(w, 640 + np.arange(128))], axis=1)
        wo = inp['od_w_out'][j]
        for b in range(2):
            od_out[j, b] = _kblocks(wo, np.arange(512 * b, 512 * b + 512))
    sh['od_in'] = od_in
    sh['od_v'] = od_v
    sh['od_out'] = od_out
    sp = np.zeros((128, NS), np.float32)
    for i in range(4):
        sp[:, i * 72:(i + 1) * 72] = inp['b_ada'][i].reshape(72, 128).T
        for n in range(3):
            sp[:, 288 + (i * 3 + n) * 8:288 + (i * 3 + n + 1) * 8] = inp['norm_g'][i, n].reshape(8, 128).T
    sp[:, 384:392] = inp['final_g'].reshape(8, 128).T
    pidx = np.arange(128) % 64
    for j in range(2):
        sp[:, 392 + j * 3:392 + j * 3 + 3] = inp['mla_gq'][j].reshape(3, 128).T
        sp[:, 398 + j * 2:398 + j * 2 + 2] = inp['mla_gkv'][j].reshape(2, 128).T
        sp[:, 402 + j * 4:402 + j * 4 + 4] = inp['pool_scale'][j].reshape(4, 128).T
        sp[:, 410 + j] = inp['gqa_gq'][j][pidx]
        sp[:, 412 + j] = inp['gqa_gq'][j][sw_ax[pidx]]
        sp[:, 414 + j] = inp['gqa_gk'][j][pidx]
        sp[:, 416 + j] = inp['gqa_gk'][j][sw_ax[pidx]]
        sp[:, 418 + j] = inp['diff_subln_g'][j]
    sh['smallp'] = sp
    lv = np.zeros((1, 512), np.float32)
    for j in range(2):
        for q, nm in enumerate(('diff_lq1', 'diff_lk1', 'diff_lq2', 'diff_lk2')):
            lv[0, (j * 4 + q) * 64:(j * 4 + q + 1) * 64] = inp[nm][j]
    sh['lvec'] = lv
    return sh


def _prep_core(inp, r, tabs_s, tabs_p, icnt_s, icnt_p):
    m = {}
    if r < 4:
        x = inp['x_sample'][r]
        cseg = np.stack([inp['c_sample'][r]] * 4)
        m['tabs'] = tabs_s
        m['icnt'] = icnt_s
        m['maskb'] = np.zeros((128, 16), np.float32)
        m['flag'] = np.ones((128, 1), np.float32)
    else:
        q = r - 4
        x = inp['x_prompt'][4 * q:4 * q + 4].reshape(NTOK, D)
        cseg = inp['c_prompt'][4 * q:4 * q + 4]
        m['tabs'] = tabs_p
        m['icnt'] = icnt_p
        mb = np.full((4, 4), -30000.0, np.float32)
        mb[np.arange(4), np.arange(4)] = 0.0
        m['maskb'] = np.ascontiguousarray(np.broadcast_to(mb.reshape(1, 16), (128, 16)))
        m['flag'] = np.zeros((128, 1), np.float32)
    m['xT'] = np.ascontiguousarray(x.T).reshape(8, 128, NTOK)
    m['c4T'] = np.ascontiguousarray(cseg.reshape(4, 8, 128).transpose(2, 1, 0)).reshape(128, 32)
    return m


def _icnt(S):
    t = np.arange(NTOK) % S
    out = np.empty((4, NTOK), np.float32)
    for g, w in enumerate((2, 4, 8, 16)):
        lo = np.clip(t - w // 2, 0, S)
        hi = np.clip(t + w - w // 2, 0, S)
        out[g] = (np.float32(1.0) / (hi - lo).astype(np.float32)).astype(np.float32)
    return out


def build_program():
    nc = bass.Bass("TRN2", target_bir_lowering=False)

    def din(name, shape, dt=F32):
        return nc.dram_tensor(name, list(shape), dt, kind="ExternalInput").ap()

    def dscr(name, shape, dt):
        return nc.dram_tensor(name, list(shape), dt).ap()

    xT_in = din("xT", [8, 128, NTOK])
    c4T_in = din("c4T", [128, 32])
    tabs_in = din("tabs", [6, 128, NTOK])
    maskb_in = din("maskb", [128, 16])
    flag_in = din("flag", [128, 1])
    icnt_in = din("icnt", [4, NTOK])
    smallp_in = din("smallp", [128, NS])
    lvec_in = din("lvec", [1, 512])
    wada_in = din("wada", [32, 128, 9216])
    ffn_in_f = din("ffn_in", [8, 11, 128, 4096])
    ffn_out_f = din("ffn_out", [8, 4, 128, 5632])
    ev_in_f = din("ev_in", [2, 128, 9728])
    ev_uq_f = din("ev_uq", [2, 128, 3072])
    ev_ukv_f = din("ev_ukv", [2, 128, 2048])
    ev_out_f = din("ev_out", [2, 2, 128, 4096])
    poolw_f = din("poolw", [128, 1024])
    od_in_f = din("od_in", [2, 128, 26624])
    od_v_f = din("od_v", [2, 128, 5120])
    od_out_f = din("od_out", [2, 2, 128, 4096])
    yT_out = nc.dram_tensor("yT", [8, 128, NTOK], F32, kind="ExternalOutput").ap()

    ffn_in_b = dscr("ffn_in_b", [8, 11, 128, 4096], BF16)
    ffn_out_b = dscr("ffn_out_b", [8, 4, 128, 5632], BF16)
    ev_in_b = dscr("ev_in_b", [2, 128, 9728], BF16)
    ev_uq_b = dscr("ev_uq_b", [2, 128, 3072], BF16)
    ev_ukv_b = dscr("ev_ukv_b", [2, 128, 2048], BF16)
    ev_out_b = dscr("ev_out_b", [2, 2, 128, 4096], BF16)
    od_in_b = dscr("od_in_b", [2, 128, 26624], BF16)
    od_v_b = dscr("od_v_b", [2, 128, 5120], BF16)
    od_out_b = dscr("od_out_b", [2, 2, 128, 4096], BF16)
    xs_d = dscr("xs_d", [8, 128, NTOK], F32)
    qT_d = dscr("qT_d", [16, 128, NTOK], BF16)
    kT_d = dscr("kT_d", [16, 128, NTOK], BF16)
    kpe_d = dscr("kpe_d", [32, NTOK], BF16)
    v64_d = dscr("v64_d", [8, 128, 64, 64], BF16)
    v128_d = dscr("v128_d", [4, 128, 64, 128], BF16)
    oT_d = dscr("oT_d", [8, 128, NTOK], BF16)
    pz_d = dscr("pz_d", [4, 128, NTOK], F32)

    ARENA_BYTES = 204 * 1024
    with contextlib.ExitStack() as es:
        sems = [es.enter_context(nc.semaphore(f"s{i}")) for i in range(60)]
        P = Prog(nc, sems)

        cur_es = [es]
        uid = [0]
        cursor = [0]

        def alloc(nbytes):
            return None

        def vf32(_off, n):
            uid[0] += 1
            return cur_es[0].enter_context(nc.sbuf_tensor(f"f{uid[0]}", [128, n], F32))[:, :]

        def vbf(_off, n):
            uid[0] += 1
            return cur_es[0].enter_context(nc.sbuf_tensor(f"b{uid[0]}", [128, n], BF16))[:, :]

        PSP = None
        PSB = None

        def alloc_psum():
            nonlocal PSP, PSB
            PSP = []
            for k in range(4):
                uid[0] += 1
                PSP.append(cur_es[0].enter_context(nc.psum_tensor(f"ps{uid[0]}", [128, 1024], F32))[:, :])
            PSB = [Buf(f"ps{b}") for b in range(8)]

        def bank(b):
            return PSP[b // 2][:, (b % 2) * 512:(b % 2) * 512 + 512]

        ones_bf = vbf(alloc(256), 128)
        blk_bf = vbf(alloc(256), 128)
        ones_f = vf32(alloc(512), 128)
        smallp = vf32(alloc(NS * 4), NS)
        c4T = vf32(alloc(128), 32)
        scT = vf32(alloc(128), 32)
        modT = [vf32(alloc(1152), 288) for _ in range(4)]
        Aall = [[vf32(alloc(128), 32) for _ in range(3)] for _ in range(4)]
        Gall = [[vf32(alloc(128), 32) for _ in range(3)] for _ in range(4)]
        maskb = vf32(alloc(64), 16)
        flag = vf32(alloc(32), 1)
        lvt = vf32(alloc(2048), 512)
        lamc = vf32(alloc(64), 16)
        sublng = vf32(alloc(32), 2)
        epsc = vf32(alloc(32), 1)
        poolw = vbf(alloc(2048), 1024)
        B_const = Buf("const")
        B_mod = Buf("mod")

        cst = P.chan("const")

        def cload(dst, src):
            P.add('sp', lambda e: e.dma_start(out=dst, in_=src), writes=[B_const], chan=cst)

        cload(smallp, smallp_in[:, :])
        cload(c4T, c4T_in[:, :])
        cload(maskb, maskb_in[:, :])
        cload(flag, flag_in[:, :])
        cload(lvt, lvec_in[:, :].partition_broadcast(128))
        cvt0 = P.chan("cvt_pw")
        P.add('pool', lambda e: e.dma_start(out=poolw, in_=poolw_f[:, :]), writes=[B_const], chan=cvt0)
        P.add('dve', lambda e: e.memset(ones_bf, 1.0), writes=[B_const])
        P.add('dve', lambda e: e.memset(ones_f, 1.0), writes=[B_const])
        P.add('dve', lambda e: e.memset(epsc, EPS), writes=[B_const])
        P.add('dve', lambda e: e.memset(blk_bf, 0.0), writes=[B_const])
        P.add('dve', lambda e: e.memset(blk_bf[0:64, 0:64], 1.0), writes=[B_const])
        P.add('dve', lambda e: e.memset(blk_bf[64:128, 64:128], 1.0), writes=[B_const])

        WD = [Buf(f"wd{i}") for i in range(4)]
        for i in range(4):
            ch = P.chan(f"cvt{i}")
            ch.nobar = True
            j = i // 2

            def cv(dst, src, ch=ch, i=i):
                P.add('pool', lambda e: e.dma_start(out=dst, in_=src), writes=[WD[i]], chan=ch)

            for b in range(11):
                cv(ffn_in_b[i * 2, b], ffn_in_f[i * 2, b])
            for c2 in range(4):
                cv(ffn_out_b[i * 2, c2], ffn_out_f[i * 2, c2])
            if i % 2 == 0:
                cv(ev_in_b[j], ev_in_f[j])
                cv(ev_uq_b[j], ev_uq_f[j])
                cv(ev_ukv_b[j], ev_ukv_f[j])
                for b in range(2):
                    cv(ev_out_b[j, b], ev_out_f[j, b])
            else:
                for b in range(7):
                    n = 4096 if b < 6 else 2048
                    cv(od_in_b[j][:, b * 4096:b * 4096 + n], od_in_f[j][:, b * 4096:b * 4096 + n])
                cv(od_v_b[j], od_v_f[j])
                for b in range(2):
                    cv(od_out_b[j, b], od_out_f[j, b])
            for b in range(11):
                cv(ffn_in_b[i * 2 + 1, b], ffn_in_f[i * 2 + 1, b])
            for c2 in range(4):
                cv(ffn_out_b[i * 2 + 1, c2], ffn_out_f[i * 2 + 1, c2])
            WD[i].last_w = ('dma', ch, ch.count)
            WD[i].readers = []

        pro_es = contextlib.ExitStack()
        cur_es[0] = pro_es
        alloc_psum()
        B_const.last_w = ('dma', cst, cst.count)
        P.wait_refs('dve', [('dma', cvt0, cvt0.count)])
        P.add('act', lambda e: e.activation(out=scT, in_=c4T, func=AF.Silu), reads=[B_const], writes=[B_mod])
        ada_off = [alloc(8 * 1152 * 4) for _ in range(2)]
        ada_buf = [vf32(o, 9216) for o in ada_off]
        ada_B = [Buf("ada0"), Buf("ada1")]
        ada_ch = [P.chan("ada0"), P.chan("ada1")]
        scT3 = scT.rearrange("p (k s) -> p k s", s=4)
        nb = 0
        for i in range(4):
            for b in range(8):
                sl = nb % 2
                wb = ada_buf[sl].rearrange("p (k n) -> p k n", n=1152)
                P.add('sp', lambda e, sl=sl, i=i, b=b: e.dma_start(out=ada_buf[sl], in_=wada_in[i * 8 + b]),
                      writes=[ada_B[sl]], chan=ada_ch[sl])
                pb = nb % 2
                for jj in range(9):
                    for kc in range(8):
                        P.add('pe', lambda e, wb=wb, jj=jj, kc=kc, pb=pb: e.matmul(
                            bank(pb)[:, jj * 4:jj * 4 + 4], wb[:, kc, jj * 128:(jj + 1) * 128], scT3[:, kc, :],
                            start=(kc == 0), stop=(kc == 7)),
                            reads=[ada_B[sl], B_mod], writes=[PSB[pb]])
                for jj in range(9):
                    jcol = b * 9 + jj
                    P.add('dve', lambda e, i=i, jj=jj, jcol=jcol, pb=pb: e.tensor_scalar(
                        out=modT[i][:, jcol * 4:jcol * 4 + 4], in0=bank(pb)[:, jj * 4:jj * 4 + 4],
                        scalar1=smallp[:, i * 72 + jcol:i * 72 + jcol + 1], scalar2=None, op0=ALU.add),
                        reads=[PSB[pb], B_const], writes=[B_mod])
                nb += 1
        for i in range(4):
            for n in range(3):
                for c in range(8):
                    js = (3 * n + 1) * 8 + c
                    P.add('dve', lambda e, i=i, n=n, c=c, js=js: e.tensor_scalar(
                        out=Aall[i][n][:, c * 4:c * 4 + 4], in0=modT[i][:, js * 4:js * 4 + 4],
                        scalar1=1.0, scalar2=smallp[:, 288 + (i * 3 + n) * 8 + c:288 + (i * 3 + n) * 8 + c + 1],
                        op0=ALU.add, op1=ALU.mult), reads=[B_mod, B_const], writes=[B_mod])
                jg = (3 * n + 2) * 8
                P.add('dve', lambda e, i=i, n=n, jg=jg: e.tensor_scalar(
                    out=Gall[i][n], in0=modT[i][:, jg * 4:jg * 4 + 32],
                    scalar1=(1.0 if n == 1 else 0.5), scalar2=None, op0=ALU.mult),
                    reads=[B_mod], writes=[B_mod])
        lam_tmp = vf32(alloc(256), 64)
        for j in range(2):
            li = 2 * j + 1
            lam_init = 0.8 - 0.6 * float(np.exp(-0.3 * li))
            for q in range(2):
                a = lvt[:, (j * 4 + 2 * q) * 64:(j * 4 + 2 * q + 1) * 64]
                b_ = lvt[:, (j * 4 + 2 * q + 1) * 64:(j * 4 + 2 * q + 2) * 64]
                P.add('dve', lambda e, a=a, b_=b_: e.tensor_tensor(out=lam_tmp, in0=a, in1=b_, op=ALU.mult),
                      reads=[B_const, B_mod], writes=[B_mod])
                P.add('dve', lambda e, j=j, q=q: e.reduce_sum(out=lamc[:, j * 4 + 2 + q:j * 4 + 3 + q], in_=lam_tmp, axis=AX.X),
                      reads=[B_mod], writes=[B_mod])
            P.add('act', lambda e, j=j: e.activation(out=lamc[:, j * 4 + 2:j * 4 + 4], in_=lamc[:, j * 4 + 2:j * 4 + 4], func=AF.Exp),
                  reads=[B_mod], writes=[B_mod])
            P.add('dve', lambda e, j=j, lam_init=lam_init: e.scalar_tensor_tensor(
                out=lamc[:, j * 4 + 1:j * 4 + 2], in0=lamc[:, j * 4 + 3:j * 4 + 4], scalar=-lam_init,
                in1=lamc[:, j * 4 + 2:j * 4 + 3], op0=ALU.add, op1=ALU.subtract), reads=[B_mod], writes=[B_mod])
            P.add('dve', lambda e, j=j, lam_init=lam_init: e.tensor_scalar(
                out=sublng[:, j:j + 1], in0=smallp[:, 418 + j:419 + j], scalar1=(1.0 - lam_init), scalar2=None, op0=ALU.mult),
                reads=[B_mod, B_const], writes=[B_mod])
        P.barrier()
        P.emit_pending(nc)
        pro_es.close()

        tmp = XTall = XT = XB = HTall = HT = HB = AT = AB = SQ = SQB = NTMP = TMP = TMPB = tmp_i = RT = RTB = RSTD = RSTDB = NRA = RA = RAB = RACH = ra_i = NRB = RBv = RBB = RBCH = rb_i = OTall = OT = OTB = OTCH = TAB = TABB = TABCH = NST = STG = STGB = STGCH = stg_i = VST = VSTB = VSTCH = PZS = PZSB = PZSCH = CQN = CQNB = XCH = XSCH = TOKEN_END = PSP = PSB = None

        def alloc_token():
            nonlocal tmp, XTall, XT, XB, HTall, HT, HB, AT, AB, SQ, SQB, NTMP, TMP, TMPB, tmp_i, RT, RTB, RSTD, RSTDB, NRA, RA, RAB, RACH, ra_i, NRB, RBv, RBB, RBCH, rb_i, OTall, OT, OTB, OTCH, TAB, TABB, TABCH, NST, STG, STGB, STGCH, stg_i, VST, VSTB, VSTCH, PZS, PZSB, PZSCH, CQN, CQNB, XCH, XSCH, TOKEN_END, PSP, PSB
            alloc_psum()
            XTall = vf32(None, 8 * T)
            XT = [XTall[:, c * T:(c + 1) * T] for c in range(8)]
            XB = [Buf(f"x{c}") for c in range(8)]
            HTall = vbf(None, 8 * T)
            HT = [HTall[:, c * T:(c + 1) * T] for c in range(8)]
            HB = [Buf(f"h{c}") for c in range(8)]
            AT = [vbf(None, T) for f in range(22)]
            AB = [Buf(f"a{f}") for f in range(22)]
            SQ = [vbf(None, T) for c in range(8)]
            SQB = [Buf(f"sq{c}") for c in range(8)]
            NTMP = 4
            TMP = [vf32(alloc(T * 4), T) for _ in range(NTMP)]
            TMPB = [Buf(f"tmp{k}") for k in range(NTMP)]
            tmp_i = [0]

            def tmp():
                k = tmp_i[0] % NTMP
                tmp_i[0] += 1
                return TMP[k], TMPB[k]

            RT = vf32(alloc(T * 4), T)
            RTB = Buf("rt")
            RSTD = vf32(alloc(T * 4), T)
            RSTDB = Buf("rstd")
            NRA = 4
            RA = [vbf(alloc(4096 * 2), 4096) for _ in range(NRA)]
            RAB = [Buf(f"ra{k}") for k in range(NRA)]
            RACH = [P.chan(f"ra{k}") for k in range(NRA)]
            ra_i = [0]
            NRB = 2
            RBv = [vbf(alloc(5632 * 2), 5632) for _ in range(NRB)]
            RBB = [Buf(f"rb{k}") for k in range(NRB)]
            RBCH = [P.chan(f"rb{k}") for k in range(NRB)]
            rb_i = [0]
            OTall = vbf(None, 8 * T)
            OT = [OTall[:, c * T:(c + 1) * T] for c in range(8)]
            OTB = Buf("ot")
            OTCH = P.chan("ot")
            TAB = [vf32(alloc(T * 4), T) for _ in range(4)]
            TABB = Buf("tab")
            TABCH = P.chan("tab")
            NST = 4
            STG = [vbf(alloc(T * 2), T) for _ in range(NST)]
            STGB = [Buf(f"stg{k}") for k in range(NST)]
            STGCH = [P.chan(f"stg{k}") for k in range(NST)]
            stg_i = [0]
            VST = vbf(alloc(4 * 640 * 2), 4 * 640)
            VSTB = Buf("vst")
            VSTCH = P.chan("vst")
            PZS = vf32(alloc(4 * T * 4), 4 * T)
            PZSB = Buf("pzs")
            PZSCH = P.chan("pzs")
            CQN = [vbf(alloc(T * 2), T) for _ in range(3)]
            CQNB = [Buf(f"cqn{k}") for k in range(3)]
            XCH = P.chan("xld")
            XSCH = P.chan("xst")
            TOKEN_END = cursor[0]


        def stg():
            k = stg_i[0] % NST
            stg_i[0] += 1
            return STG[k], STGB[k], STGCH[k]

        def ringA(src, n, wd):
            k = ra_i[0] % NRA
            ra_i[0] += 1
            dst = RA[k][:, 0:n]
            P.add('sp', lambda e: e.dma_start(out=dst, in_=src), reads=[wd], writes=[RAB[k]], chan=RACH[k])
            return RA[k], RAB[k]

        def ringB(src, wd):
            k = rb_i[0] % NRB
            rb_i[0] += 1
            dst = RBv[k]
            P.add('sp', lambda e: e.dma_start(out=dst, in_=src), reads=[wd], writes=[RBB[k]], chan=RBCH[k])
            return RBv[k], RBB[k]

        def mm(bk, lhsT, rhs, start, stop, reads, rows=None):
            out = bank(bk) if rows is None else bank(bk)[rows[0]:rows[1], :]
            P.add('pe', lambda e: e.matmul(out, lhsT, rhs, start=start, stop=stop), reads=reads, writes=[PSB[bk]])

        def sqrt_recip(bk, inv_n, rows=(0, 128)):
            r0, r1 = rows
            P.add('act', lambda e: e.activation(out=RT[r0:r1, :], in_=bank(bk)[r0:r1, :], func=AF.Sqrt,
                                                bias=epsc[r0:r1, :], scale=inv_n),
                  reads=[PSB[bk], B_const], writes=[RTB])
            P.add('dve', lambda e: e.reciprocal(out=RSTD[r0:r1, :], in_=RT[r0:r1, :]), reads=[RTB], writes=[RSTDB])

        def norm_main(Acols, Bcols, s):
            for c in range(8):
                P.add('act', lambda e, c=c: e.activation(out=SQ[c], in_=XT[c], func=AF.Square),
                      reads=[XB[c]], writes=[SQB[c]])
            for c in range(8):
                mm(6, ones_bf, SQ[c], c == 0, c == 7, [SQB[c], B_const])
            sqrt_recip(6, 1.0 / D)
            for c in range(8):
                tv, tb = tmp()
                P.add('dve', lambda e, c=c, tv=tv: e.scalar_tensor_tensor(
                    out=tv, in0=XT[c], scalar=Acols[:, c * 4 + s:c * 4 + s + 1], in1=RSTD, op0=ALU.mult, op1=ALU.mult),
                    reads=[XB[c], RSTDB, B_mod], writes=[tb])
                if Bcols is None:
                    continue
                P.add('act', lambda e, c=c, tv=tv: e.activation(out=HT[c], in_=tv, func=AF.Identity,
                                                                bias=Bcols[:, c * 4 + s:c * 4 + s + 1], scale=1.0),
                      reads=[tb, B_mod], writes=[HB[c]])

        def ffn(i, k, s):
            G = Gall[i][0 if k == 0 else 2]
            pp = 0
            for b in range(11):
                wv, wb = ringA(ffn_in_b[i * 2 + k, b], 4096, WD[i])
                w3 = wv.rearrange("p (k n) -> p k n", n=512)
                for j in range(2):
                    f = 2 * b + j
                    bg, bu = (0, 1) if pp % 2 == 0 else (2, 3)
                    pp += 1
                    for kc in range(8):
                        mm(bg, w3[:, kc, j * 128:(j + 1) * 128], HT[kc], kc == 0, kc == 7, [wb, HB[kc]])
                    for kc in range(8):
                        mm(bu, w3[:, kc, 256 + j * 128:256 + (j + 1) * 128], HT[kc], kc == 0, kc == 7, [wb, HB[kc]])
                    tv, tb = tmp()
                    P.add('act', lambda e, bg=bg, tv=tv: e.activation(out=tv, in_=bank(bg), func=AF.Silu),
                          reads=[PSB[bg]], writes=[tb])
                    P.add('dve', lambda e, bu=bu, tv=tv, f=f: e.tensor_tensor(out=AT[f], in0=bank(bu), in1=tv, op=ALU.mult),
                          reads=[PSB[bu], tb], writes=[AB[f]])
            for c2 in range(4):
                wv, wb = ringB(ffn_out_b[i * 2 + k, c2], WD[i])
                w3 = wv.rearrange("p (f n) -> p f n", n=256)
                for cc in range(2):
                    c = 2 * c2 + cc
                    by = 4 + (c % 2)
                    for f in range(22):
                        mm(by, w3[:, f, cc * 128:(cc + 1) * 128], AT[f], f == 0, f == 21, [wb, AB[f]])
                    P.add('dve', lambda e, c=c, by=by: e.scalar_tensor_tensor(
                        out=XT[c], in0=bank(by), scalar=G[:, c * 4 + s:c * 4 + s + 1], in1=XT[c], op0=ALU.mult, op1=ALU.add),
                        reads=[PSB[by], B_mod], writes=[XB[c]])

        def outproj(i, t, s):
            j = i // 2
            wsrc = ev_out_b if i % 2 == 0 else od_out_b
            G = Gall[i][1]
            for b in range(2):
                wv, wb = ringA(wsrc[j, b], 4096, WD[i])
                w3 = wv.rearrange("p (k n) -> p k n", n=512)
                for cc in range(4):
                    c = 4 * b + cc
                    by = 4 + (c % 2)
                    for kc in range(8):
                        mm(by, w3[:, kc, cc * 128:(cc + 1) * 128], OT[kc], kc == 0, kc == 7, [wb, OTB])
                    P.add('dve', lambda e, c=c, by=by: e.scalar_tensor_tensor(
                        out=XT[c], in0=bank(by), scalar=G[:, c * 4 + s:c * 4 + s + 1], in1=XT[c], op0=ALU.mult, op1=ALU.add),
                        reads=[PSB[by], B_mod], writes=[XB[c]])

        def store_rows(sv, sb, sch, pieces, t):
            for (r0, r1, dap) in pieces:
                P.add('sp', lambda e, r0=r0, r1=r1, dap=dap: e.dma_start(out=dap[:, t * T:(t + 1) * T], in_=sv[r0:r1, :]),
                      reads=[sb], chan=sch)

        def load_tabs(t, which):
            for q, w in enumerate(which):
                P.add('sp', lambda e, q=q, w=w: e.dma_start(out=TAB[q], in_=tabs_in[w][:, t * T:(t + 1) * T]),
                      writes=[TABB], chan=TABCH)

        def rope_combine(bd, bs, rows, cosT, sinT, out_ap, out_b, gcols=None, rstd=False):
            r0, r1 = rows
            t1, b1 = tmp()
            t2, b2 = tmp()
            if gcols is None:
                P.add('dve', lambda e: e.tensor_tensor(out=t1[r0:r1, :], in0=bank(bd)[r0:r1, :], in1=cosT[r0:r1, :], op=ALU.mult),
                      reads=[PSB[bd], TABB], writes=[b1])
                P.add('dve', lambda e: e.tensor_tensor(out=t2[r0:r1, :], in0=bank(bs)[r0:r1, :], in1=sinT[r0:r1, :], op=ALU.mult),
                      reads=[PSB[bs], TABB], writes=[b2])
            else:
                g, gs = gcols
                P.add('dve', lambda e: e.scalar_tensor_tensor(out=t1[r0:r1, :], in0=bank(bd)[r0:r1, :], scalar=g[r0:r1, :],
                                                              in1=cosT[r0:r1, :], op0=ALU.mult, op1=ALU.mult),
                      reads=[PSB[bd], TABB, B_const], writes=[b1])
                P.add('dve', lambda e: e.scalar_tensor_tensor(out=t2[r0:r1, :], in0=bank(bs)[r0:r1, :], scalar=gs[r0:r1, :],
                                                              in1=sinT[r0:r1, :], op0=ALU.mult, op1=ALU.mult),
                      reads=[PSB[bs], TABB, B_const], writes=[b2])
            if not rstd:
                P.add('dve', lambda e: e.tensor_tensor(out=out_ap[r0:r1, :], in0=t1[r0:r1, :], in1=t2[r0:r1, :], op=ALU.add),
                      reads=[b1, b2], writes=[out_b])
            else:
                P.add('dve', lambda e: e.tensor_tensor(out=t1[r0:r1, :], in0=t1[r0:r1, :], in1=t2[r0:r1, :], op=ALU.add),
                      reads=[b2], writes=[b1])
                P.add('dve', lambda e: e.tensor_tensor(out=out_ap[r0:r1, :], in0=t1[r0:r1, :], in1=RSTD[r0:r1, :], op=ALU.mult),
                      reads=[b1, RSTDB], writes=[out_b])

        def sub_norm(banks, gbase, nfeat):
            n = len(banks)
            for k, bk in enumerate(banks):
                P.add('act', lambda e, k=k, bk=bk: e.activation(out=SQ[k], in_=bank(bk), func=AF.Square),
                      reads=[PSB[bk]], writes=[SQB[k]])
            for k in range(n):
                mm(6, ones_bf, SQ[k], k == 0, k == n - 1, [SQB[k], B_const])
            sqrt_recip(6, 1.0 / nfeat)
            for k, bk in enumerate(banks):
                P.add('dve', lambda e, k=k, bk=bk: e.scalar_tensor_tensor(
                    out=CQN[k], in0=bank(bk), scalar=smallp[:, gbase + k:gbase + k + 1], in1=RSTD, op0=ALU.mult, op1=ALU.mult),
                    reads=[PSB[bk], RSTDB, B_const], writes=[CQNB[k]])

        def inproj_even(i, t):
            j = i // 2
            load_tabs(t, (0, 1))
            W = []
            offs = [(0, 4096), (4096, 4096), (8192, 1536)]
            ncol = [512, 512, 192]

            def getw(b):
                wv, wb = ringA(ev_in_b[j][:, offs[b][0]:offs[b][0] + offs[b][1]], offs[b][1], WD[i])
                return wv[:, 0:offs[b][1]].rearrange("p (k n) -> p k n", n=ncol[b]), wb

            def proj(bk, w3, wb, c0, c1, rows=None):
                for kc in range(8):
                    mm(bk, w3[:, kc, c0:c1], HT[kc], kc == 0, kc == 7, [wb, HB[kc]], rows=rows)

            w3, wb = getw(0)
            for c in range(3):
                proj(c, w3, wb, c * 128, (c + 1) * 128)
            proj(3, w3, wb, 384, 512)
            w3, wb = getw(1)
            proj(4, w3, wb, 0, 128)
            pzb = [5, 7, 5, 7]
            for g in range(3):
                proj(pzb[g], w3, wb, 128 + g * 128, 256 + g * 128)
                P.add('act', lambda e, g=g: e.activation(out=PZS[:, g * T:(g + 1) * T], in_=bank(pzb[g]), func=AF.Copy),
                      reads=[PSB[pzb[g]]], writes=[PZSB])
            w3, wb = getw(2)
            proj(7, w3, wb, 0, 128)
            P.add('act', lambda e: e.activation(out=PZS[:, 3 * T:4 * T], in_=bank(7), func=AF.Copy),
                  reads=[PSB[7]], writes=[PZSB])
            P.add('sp', lambda e: e.dma_start(out=pz_d[:, :, t * T:(t + 1) * T].rearrange("g p n -> p g n"),
                                              in_=PZS.rearrange("p (g n) -> p g n", n=T)), reads=[PZSB], chan=PZSCH)
            proj(5, w3, wb, 128, 160, rows=(0, 32))
            proj(7, w3, wb, 160, 192, rows=(0, 32))
            sv, sb, sch = stg()
            rope_combine(5, 7, (0, 32), TAB[0], TAB[1], sv, sb)
            store_rows(sv, sb, sch, [(0, 32, kpe_d)], t)
            sub_norm([0, 1, 2], 392 + j * 3, 384)
            wv, wb = ringA(ev_uq_b[j], 3072, WD[i])
            w3 = wv[:, 0:3072].rearrange("p (k n) -> p k n", n=1024)
            for cn in range(4):
                bk = cn % 2
                for kc in range(3):
                    mm(bk, w3[:, kc, cn * 128:(cn + 1) * 128], CQN[kc], kc == 0, kc == 2, [wb, CQNB[kc]])
                sv, sb, sch = stg()
                P.add('act', lambda e, bk=bk, sv=sv: e.activation(out=sv, in_=bank(bk), func=AF.Copy),
                      reads=[PSB[bk]], writes=[sb])
                store_rows(sv, sb, sch, [(0, 64, qT_d[2 * cn][0:64]), (64, 128, qT_d[2 * cn + 1][0:64])], t)
            for cr in range(2):
                for kc in range(3):
                    mm(0, w3[:, kc, 512 + cr * 128:512 + (cr + 1) * 128], CQN[kc], kc == 0, kc == 2, [wb, CQNB[kc]])
                for kc in range(3):
                    mm(1, w3[:, kc, 768 + cr * 128:768 + (cr + 1) * 128], CQN[kc], kc == 0, kc == 2, [wb, CQNB[kc]])
                sv, sb, sch = stg()
                rope_combine(0, 1, (0, 128), TAB[0], TAB[1], sv, sb)
                store_rows(sv, sb, sch, [(hh * 32, hh * 32 + 32, qT_d[4 * cr + hh][64:96]) for hh in range(4)], t)
            sub_norm([3, 4], 398 + j * 2, 256)
            wv, wb = ringA(ev_ukv_b[j], 2048, WD[i])
            w3 = wv[:, 0:2048].rearrange("p (k n) -> p k n", n=1024)
            for ck in range(4):
                bk = ck % 2
                for kc in range(2):
                    mm(bk, w3[:, kc, ck * 128:(ck + 1) * 128], CQN[kc], kc == 0, kc == 1, [wb, CQNB[kc]])
                sv, sb, sch = stg()
                P.add('act', lambda e, bk=bk, sv=sv: e.activation(out=sv, in_=bank(bk), func=AF.Copy),
                      reads=[PSB[bk]], writes=[sb])
                store_rows(sv, sb, sch, [(0, 64, kT_d[2 * ck][0:64]), (64, 128, kT_d[2 * ck + 1][0:64])], t)
            vst3 = VST.rearrange("p (b n) -> p b n", n=640)
            for tb_ in range(4):
                bk = 2 + tb_ % 2
                for kc in range(2):
                    mm(bk, CQN[kc][:, tb_ * 128:(tb_ + 1) * 128], w3[:, kc, 512:1024], kc == 0, kc == 1, [wb, CQNB[kc]])
                P.add('act', lambda e, bk=bk, tb_=tb_: e.activation(out=vst3[:, tb_, 0:512], in_=bank(bk), func=AF.Copy),
                      reads=[PSB[bk]], writes=[VSTB])
            for h in range(8):
                P.add('sp', lambda e, h=h: e.dma_start(out=v64_d[h][:, t * 4:(t + 1) * 4, :], in_=vst3[:, :, h * 64:(h + 1) * 64]),
                      reads=[VSTB], chan=VSTCH)

        def inproj_odd(i, t):
            j = i // 2
            load_tabs(t, (2, 3, 4, 5))
            gq = (smallp[:, 410 + j:411 + j], smallp[:, 412 + j:413 + j])
            gk = (smallp[:, 414 + j:415 + j], smallp[:, 416 + j:417 + j])
            seq = [('qc', c) for c in range(4)] + [('kc', 0)] + [('qd', c) for c in range(4)] + [('kd', c) for c in range(4)]
            cur = {'b': -1, 'w3': None, 'wb': None}

            def wcols(ci):
                b = ci // 4
                if b != cur['b']:
                    n = 4096 if b < 6 else 2048
                    wv, wb = ringA(od_in_b[j][:, b * 4096:b * 4096 + n], n, WD[i])
                    cur['b'] = b
                    cur['w3'] = wv[:, 0:n].rearrange("p (k n) -> p k n", n=n // 8)
                    cur['wb'] = wb
                return cur['w3'], cur['wb'], (ci % 4) * 128

            pp = 0
            for si, (kind, idx) in enumerate(seq):
                bd, bs = (0, 1) if pp % 2 == 0 else (2, 3)
                pp += 1
                for q, bk in enumerate((bd, bs)):
                    w3, wb, c0 = wcols(2 * si + q)
                    for kc in range(8):
                        mm(bk, w3[:, kc, c0:c0 + 128], HT[kc], kc == 0, kc == 7, [wb, HB[kc]])
                sv, sb, sch = stg()
                if kind in ('qc', 'kc'):
                    P.add('act', lambda e, bd=bd: e.activation(out=SQ[0], in_=bank(bd), func=AF.Square),
                          reads=[PSB[bd]], writes=[SQB[0]])
                    mm(6, blk_bf, SQ[0], True, True, [SQB[0], B_const])
                    sqrt_recip(6, 1.0 / 64)
                    rope_combine(bd, bs, (0, 128), TAB[0], TAB[1], sv, sb, gcols=(gq if kind == 'qc' else gk), rstd=True)
                    if kind == 'qc':
                        pieces = [(0, 64, qT_d[2 * idx][0:64]), (64, 128, qT_d[2 * idx + 1][0:64])]
                    else:
                        pieces = [(0, 64, kT_d[0][0:64]), (64, 128, kT_d[1][0:64])]
                else:
                    rope_combine(bd, bs, (0, 128), TAB[2], TAB[3], sv, sb)
                    if kind == 'qd':
                        pieces = [(0, 64, qT_d[8 + 2 * idx][0:64]), (64, 128, qT_d[8 + 2 * idx + 1][0:64])]
                    else:
                        pieces = [(0, 64, kT_d[2 + 2 * idx][0:64]), (64, 128, kT_d[2 + 2 * idx + 1][0:64])]
                store_rows(sv, sb, sch, pieces, t)
            wv, wb = ringA(od_v_b[j][:, 0:4096], 4096, WD[i])
            w3 = wv.rearrange("p (k n) -> p k n", n=512)
            wv2, wb2 = ringA(od_v_b[j][:, 4096:5120], 1024, WD[i])
            w32 = wv2[:, 0:1024].rearrange("p (k n) -> p k n", n=128)
            vst3 = VST.rearrange("p (b n) -> p b n", n=640)
            for tb_ in range(4):
                bk = tb_ % 2
                for kc in range(8):
                    mm(bk, HT[kc][:, tb_ * 128:(tb_ + 1) * 128], w3[:, kc, :], kc == 0, kc == 7, [wb, HB[kc]])
                P.add('act', lambda e, bk=bk, tb_=tb_: e.activation(out=vst3[:, tb_, 0:512], in_=bank(bk), func=AF.Copy),
                      reads=[PSB[bk]], writes=[VSTB])
                bk2 = 2 + tb_ % 2
                for kc in range(8):
                    P.add('pe', lambda e, kc=kc, bk2=bk2, tb_=tb_: e.matmul(
                        bank(bk2)[:, 0:128], HT[kc][:, tb_ * 128:(tb_ + 1) * 128], w32[:, kc, :], start=(kc == 0), stop=(kc == 7)),
                        reads=[wb2, HB[kc]], writes=[PSB[bk2]])
                P.add('act', lambda e, bk2=bk2, tb_=tb_: e.activation(out=vst3[:, tb_, 512:640], in_=bank(bk2)[:, 0:128], func=AF.Copy),
                      reads=[PSB[bk2]], writes=[VSTB])
            for h in range(4):
                P.add('sp', lambda e, h=h: e.dma_start(out=v128_d[h][:, t * 4:(t + 1) * 4, :], in_=vst3[:, :, h * 128:(h + 1) * 128]),
                      reads=[VSTB], chan=VSTCH)
            for h in range(2):
                P.add('sp', lambda e, h=h: e.dma_start(out=v64_d[h][:, t * 4:(t + 1) * 4, :], in_=vst3[:, :, 512 + h * 64:512 + (h + 1) * 64]),
                      reads=[VSTB], chan=VSTCH)

        def token_phase(ph):
            for t in range(NT):
                P.cut()
                s = t // 4
                src = xT_in if ph == 0 else xs_d
                P.add('sp', lambda e, src=src, t=t: e.dma_start(
                    out=XTall.rearrange("p (c n) -> p c n", n=T), in_=src[:, :, t * T:(t + 1) * T].rearrange("c p n -> p c n")),
                    writes=XB, chan=XCH)
                if ph > 0:
                    i = ph - 1
                    P.add('sp', lambda e, t=t: e.dma_start(
                        out=OTall.rearrange("p (c n) -> p c n", n=T), in_=oT_d[:, :, t * T:(t + 1) * T].rearrange("c p n -> p c n")),
                        writes=[OTB], chan=OTCH)
                    outproj(i, t, s)
                    norm_main(Aall[i][2], modT[i][:, 6 * 32:7 * 32], s)
                    ffn(i, 1, s)
                if ph < 4:
                    i = ph
                    norm_main(Aall[i][0], modT[i][:, 0:32], s)
                    ffn(i, 0, s)
                    norm_main(Aall[i][1], modT[i][:, 3 * 32:4 * 32], s)
                    if i % 2 == 0:
                        inproj_even(i, t)
                    else:
                        inproj_odd(i, t)
                    P.add('sp', lambda e, t=t: e.dma_start(
                        out=xs_d[:, :, t * T:(t + 1) * T].rearrange("c p n -> p c n"), in_=XTall.rearrange("p (c n) -> p c n", n=T)),
                        reads=XB, chan=XSCH)
                else:
                    for c in range(8):
                        P.add('act', lambda e, c=c: e.activation(out=SQ[c], in_=XT[c], func=AF.Square),
                              reads=[XB[c]], writes=[SQB[c]])
                    for c in range(8):
                        mm(6, ones_bf, SQ[c], c == 0, c == 7, [SQB[c], B_const])
                    sqrt_recip(6, 1.0 / D)
                    for c in range(8):
                        P.add('dve', lambda e, c=c: e.scalar_tensor_tensor(
                            out=XT[c], in0=XT[c], scalar=smallp[:, 384 + c:385 + c], in1=RSTD, op0=ALU.mult, op1=ALU.mult),
                            reads=[RSTDB, B_const], writes=[XB[c]])
                    P.add('sp', lambda e, t=t: e.dma_start(
                        out=yT_out[:, :, t * T:(t + 1) * T].rearrange("c p n -> p c n"), in_=XTall.rearrange("p (c n) -> p c n", n=T)),
                        reads=XB, chan=XSCH)

        ACC = ACCB = KT = QT = VA = VBt = KTB = QTB = VAB = VBB = KCH = QCH = VCH = NPT = PT = PTB = REC = RECB = NOS = OST = OSTB = OSTCH = ost_i = O1 = O2 = O1B = O2B = ASQ = ASQB = ART = ARTB = ARS = ARSB = ATT_END = PSP = PSB = None

        def alloc_attn():
            nonlocal ACC, ACCB, KT, QT, VA, VBt, KTB, QTB, VAB, VBB, KCH, QCH, VCH, NPT, PT, PTB, REC, RECB, NOS, OST, OSTB, OSTCH, ost_i, O1, O2, O1B, O2B, ASQ, ASQB, ART, ARTB, ARS, ARSB, ATT_END, PSP, PSB
            alloc_psum()
            KT = [vbf(alloc(NTOK * 2), NTOK) for _ in range(2)]
            QT = [vbf(alloc(NTOK * 2), NTOK) for _ in range(2)]
            VA = [vbf(alloc(64 * 128 * 2), 64 * 128) for _ in range(2)]
            VBt = [vbf(alloc(64 * 128 * 2), 64 * 128) for _ in range(2)]
            KTB = [Buf("kt0"), Buf("kt1")]
            QTB = [Buf("qt0"), Buf("qt1")]
            VAB = [Buf("va0"), Buf("va1")]
            VBB = [Buf("vb0"), Buf("vb1")]
            KCH = [P.chan("k0"), P.chan("k1")]
            QCH = [P.chan("q0"), P.chan("q1")]
            VCH = [P.chan("v0"), P.chan("v1")]
            NPT = 3
            PT = [vbf(alloc(1024 * 2), 1024) for _ in range(NPT)]
            PTB = [Buf(f"pt{k}") for k in range(NPT)]
            REC = [vf32(alloc(T * 4), T) for _ in range(2)]
            RECB = [Buf("rec0"), Buf("rec1")]
            NOS = 3
            OST = [vbf(alloc(T * 2), T) for _ in range(NOS)]
            OSTB = [Buf(f"ost{k}") for k in range(NOS)]
            OSTCH = [P.chan(f"ost{k}") for k in range(NOS)]
            ost_i = [0]
            O1 = vf32(alloc(T * 4), T)
            O2 = vf32(alloc(T * 4), T)
            O1B = Buf("o1")
            O2B = Buf("o2")
            ASQ = vbf(alloc(T * 2), T)
            ASQB = Buf("asq")
            ART = vf32(alloc(T * 4), T)
            ARTB = Buf("art")
            ARS = vf32(alloc(T * 4), T)
            ARSB = Buf("ars")
            ACC = [vf32(alloc(1024 * 4), 1024) for _ in range(4)]
            ACCB = [Buf(f"acc{k}") for k in range(4)]
            ATT_END = cursor[0]

        PZW = PZ = PW1 = PW2 = ICN = PD = PZB = PW1B = PW2B = ICNB = PDB = PZCH = ICNCH = HLCH = HRCH = POST = POSTB = POSTCH = POOL_END = PSP = PSB = None

        def alloc_pool():
            nonlocal PZW, PZ, PW1, PW2, ICN, PD, PZB, PW1B, PW2B, ICNB, PDB, PZCH, ICNCH, HLCH, HRCH, POST, POSTB, POSTCH, POOL_END, PSP, PSB
            alloc_psum()
            PZW = 2048 + 16
            PZ = vf32(alloc(PZW * 4), PZW)
            PW1 = vf32(alloc(PZW * 4), PZW)
            PW2 = vf32(alloc(PZW * 4), PZW)
            ICN = vf32(alloc(2048 * 4), 2048)
            PD = vbf(alloc(2048 * 2), 2048)
            PZB, PW1B, PW2B, ICNB, PDB = Buf("pz"), Buf("pw1"), Buf("pw2"), Buf("icn"), Buf("pd")
            PZCH = P.chan("pzl")
            ICNCH = P.chan("icn")
            HLCH = P.chan("hl")
            HRCH = P.chan("hr")
            POST = [vbf(alloc(T * 2), T) for _ in range(2)]
            POSTB = [Buf("post0"), Buf("post1")]
            POSTCH = [P.chan("post0"), P.chan("post1")]
            POOL_END = cursor[0]


        def ost():
            k = ost_i[0] % NOS
            ost_i[0] += 1
            return OST[k], OSTB[k], OSTCH[k]

        set_i = [0]

        def load_qk(qsrcs, ksrcs):
            sl = set_i[0] % 2
            for (r0, r1, ap) in qsrcs:
                P.add('sp', lambda e, r0=r0, r1=r1, ap=ap: e.dma_start(out=QT[sl][r0:r1, :], in_=ap), writes=[QTB[sl]], chan=QCH[sl])
            for (r0, r1, ap) in ksrcs:
                P.add('sp', lambda e, r0=r0, r1=r1, ap=ap: e.dma_start(out=KT[sl][r0:r1, :], in_=ap), writes=[KTB[sl]], chan=KCH[sl])
            return sl

        NSS = 3

        def attn_units(sl, dk, scale, pv, obanks_for_qg, finalize, vreads):
            units = [(qg, kp) for qg in range(16) for kp in range(32)]
            pt_i = [0]

            def qk(u):
                qg, kp = units[u]
                sb = (u % NSS) * 2
                for kk in range(2):
                    kb = 2 * kp + kk
                    r0, r1 = (64, 128) if (dk == 64 and kk == 1) else (0, dk)
                    mm(sb + kk, KT[sl][r0:r1, kb * 128:(kb + 1) * 128], QT[sl][r0:r1, qg * T:(qg + 1) * T], True, True,
                       [KTB[sl], QTB[sl]])

            qk(0)
            qk(1)
            for u in range(len(units)):
                qg, kp = units[u]
                if u + 2 < len(units):
                    qk(u + 2)
                sb = (u % NSS) * 2
                k = pt_i[0] % NPT
                pt_i[0] += 1
                kseg = (2 * kp) // 16
                mcol = kseg * 4 + qg // 4
                P.add('act', lambda e, sb=sb, k=k, mcol=mcol: e.activation(
                    out=PT[k], in_=PSP[sb // 2], func=AF.Exp, bias=maskb[:, mcol:mcol + 1], scale=scale),
                    reads=[PSB[sb], PSB[sb + 1], B_const], writes=[PTB[k]])
                obs = obanks_for_qg(qg)
                for kk in range(2):
                    kb = 2 * kp + kk
                    for (getl, ob) in zip(pv, obs):
                        mm(ob, getl(kb), PT[k][:, kk * T:(kk + 1) * T], kb == 0, kb == 63, [PTB[k]] + vreads)
                if kp == 31:
                    finalize(qg, obs)

        va_ones_done = [False]

        def ensure_ones():
            for sl in range(2):
                va3 = VA[sl].rearrange("p (k n) -> p k n", n=128)
                vb3 = VBt[sl].rearrange("p (k n) -> p k n", n=128)
                P.add('dve', lambda e, va3=va3: e.memset(va3[:, :, 64:128], 1.0), writes=[VAB[sl]])
                P.add('dve', lambda e, vb3=vb3: e.memset(vb3[:, :, 0:64], 1.0), writes=[VBB[sl]])

        def dv64_load(spec):
            qsrcs, ksrcs, vsrc = spec[0], spec[1], spec[2]
            sl = load_qk(qsrcs, ksrcs)
            set_i[0] += 1
            va3 = VA[sl].rearrange("p (k n) -> p k n", n=128)
            P.add('sp', lambda e: e.dma_start(out=va3[:, :, 0:64], in_=vsrc), writes=[VAB[sl]], chan=VCH[sl])
            return sl

        def dv64_compute(spec, sl):
            dk, scale, out_chunk, out_row0, obase = spec[3:]
            va3 = VA[sl].rearrange("p (k n) -> p k n", n=128)

            def fin(qg, obs):
                ob = obs[0]
                r = qg % 2
                P.add('dve', lambda e: e.reciprocal(out=REC[r][64:128, :], in_=bank(ob)[64:128, :]), reads=[PSB[ob]], writes=[RECB[r]])
                sv, sb_, sch = ost()
                P.add('dve', lambda e: e.tensor_tensor(out=sv[0:64, :], in0=bank(ob)[0:64, :], in1=REC[r][64:128, :], op=ALU.mult),
                      reads=[PSB[ob], RECB[r]], writes=[sb_])
                P.add('sp', lambda e: e.dma_start(out=oT_d[out_chunk][out_row0:out_row0 + 64, qg * T:(qg + 1) * T], in_=sv[0:64, :]),
                      reads=[sb_], chan=sch)

            attn_units(sl, dk, scale, [lambda kb: va3[:, kb, :]], lambda qg: [6 + (qg + obase) % 2], fin, [VAB[sl]])

        def attn_dv64_list(specs):
            sl = dv64_load(specs[0])
            for m in range(len(specs)):
                nsl = dv64_load(specs[m + 1]) if m + 1 < len(specs) else None
                dv64_compute(specs[m], sl)
                P.cut()
                sl = nsl

        def attn_diff(j, h):
            sls = []
            for w in range(2):
                sl = load_qk([(0, 64, qT_d[8 + 2 * h + w][0:64]), (64, 128, qT_d[8 + 2 * h + w][0:64])],
                             [(0, 64, kT_d[2 + 2 * h + w][0:64]), (64, 128, kT_d[2 + 2 * h + w][0:64])])
                set_i[0] += 1
                sls.append(sl)
            vs = h % 2
            va3 = VA[vs].rearrange("p (k n) -> p k n", n=128)
            P.add('sp', lambda e: e.dma_start(out=va3, in_=v128_d[h]), writes=[VAB[vs]], chan=VCH[vs])
            for qg in range(16):
                if qg % 4 == 0:
                    P.cut()
                for w in range(2):
                    sl = sls[w]
                    ob = 6

                    def qk(kp, sl=sl):
                        sb = (kp % 3) * 2
                        for kk in range(2):
                            kb = 2 * kp + kk
                            r0 = 64 * kk
                            mm(sb + kk, KT[sl][r0:r0 + 64, kb * 128:(kb + 1) * 128], QT[sl][r0:r0 + 64, qg * T:(qg + 1) * T], True, True,
                               [KTB[sl], QTB[sl]])
                    qk(0)
                    qk(1)
                    first_rs = True
                    for kp in range(32):
                        if kp + 2 < 32:
                            qk(kp + 2)
                        sb = (kp % 3) * 2
                        k = (kp + w) % NPT
                        mcol = ((2 * kp) // 16) * 4 + qg // 4
                        P.add('act', lambda e, sb=sb, k=k, mcol=mcol: e.activation(
                            out=PT[k], in_=PSP[sb // 2], func=AF.Exp, bias=maskb[:, mcol:mcol + 1], scale=0.125),
                            reads=[PSB[sb], PSB[sb + 1], B_const], writes=[PTB[k]])
                        ai = w
                        if kp % 3 != 2:
                            if kp == 0:
                                P.add('dve', lambda e, ai=ai, k=k: e.tensor_copy(ACC[ai], PT[k]), reads=[PTB[k]], writes=[ACCB[ai]])
                            else:
                                P.add('dve', lambda e, ai=ai, k=k: e.tensor_tensor(out=ACC[ai], in0=ACC[ai], in1=PT[k], op=ALU.add),
                                      reads=[PTB[k]], writes=[ACCB[ai]])
                        for kk in range(2):
                            kb = 2 * kp + kk
                            mm(ob, va3[:, kb, :], PT[k][:, kk * T:(kk + 1) * T], kb == 0, kb == 63, [PTB[k], VAB[vs]])
                        if kp % 3 == 2:
                            for kk in range(2):
                                mm(7, ones_bf, PT[k][:, kk * T:(kk + 1) * T], first_rs, False, [PTB[k], B_const])
                                first_rs = False
                    for hh in range(2):
                        mm(7, ones_f, ACC[w][:, hh * T:(hh + 1) * T], False, hh == 1, [ACCB[w], B_const])
                    ov, ovb = (O1, O1B) if w == 0 else (O2, O2B)
                    P.add('dve', lambda e, w=w: e.reciprocal(out=REC[w], in_=bank(7)), reads=[PSB[7]], writes=[RECB[w]])
                    P.add('dve', lambda e, w=w, ob=ob, ov=ov: e.tensor_tensor(out=ov, in0=bank(ob), in1=REC[w], op=ALU.mult),
                          reads=[PSB[ob], RECB[w]], writes=[ovb])
                P.add('dve', lambda e: e.scalar_tensor_tensor(out=O1, in0=O2, scalar=lamc[:, j * 4 + 1:j * 4 + 2], in1=O1, op0=ALU.mult, op1=ALU.add),
                      reads=[O2B, B_mod], writes=[O1B])
                P.add('act', lambda e: e.activation(out=ASQ, in_=O1, func=AF.Square), reads=[O1B], writes=[ASQB])
                mm(7, ones_bf, ASQ, True, True, [ASQB, B_const])
                P.add('act', lambda e: e.activation(out=ART, in_=bank(7), func=AF.Sqrt, bias=epsc, scale=1.0 / 128), reads=[PSB[7], B_const], writes=[ARTB])
                P.add('dve', lambda e: e.reciprocal(out=ARS, in_=ART), reads=[ARTB], writes=[ARSB])
                sv, sb_, sch = ost()
                P.add('dve', lambda e, sv=sv: e.scalar_tensor_tensor(out=sv, in0=O1, scalar=sublng[:, j:j + 1], in1=ARS, op0=ALU.mult, op1=ALU.mult),
                      reads=[O1B, ARSB, B_mod], writes=[sb_])
                P.add('sp', lambda e, sv=sv, qg=qg: e.dma_start(out=oT_d[4 + h][:, qg * T:(qg + 1) * T], in_=sv), reads=[sb_], chan=sch)

        def pool_branch(j):
            pw = poolw.rearrange("p (g n) -> p g n", n=128)
            post_i = 0
            for g, w in enumerate((2, 4, 8, 16)):
                P.cut()
                for seg in range(4):
                    t0 = seg * 2048
                    P.add('sp', lambda e, g=g, t0=t0: e.dma_start(out=PZ[:, 8:8 + 2048], in_=pz_d[g][:, t0:t0 + 2048]), writes=[PZB], chan=PZCH)
                    P.add('sp', lambda e, g=g, t0=t0: e.dma_start(out=ICN, in_=icnt_in[g:g + 1, t0:t0 + 2048].partition_broadcast(128)), writes=[ICNB], chan=ICNCH)
                    if seg > 0:
                        P.add('sp', lambda e, g=g, t0=t0: e.dma_start(out=PZ[:, 0:8], in_=pz_d[g][:, t0 - 8:t0]), writes=[PZB], chan=HLCH)
                        P.add('dve', lambda e: e.tensor_scalar(out=PZ[:, 0:8], in0=PZ[:, 0:8], scalar1=flag[:, 0:1], scalar2=None, op0=ALU.mult),
                              reads=[B_const], writes=[PZB])
                    else:
                        P.add('dve', lambda e: e.memset(PZ[:, 0:8], 0.0), writes=[PZB])
                    if seg < 3:
                        P.add('sp', lambda e, g=g, t0=t0: e.dma_start(out=PZ[:, 2056:2064], in_=pz_d[g][:, t0 + 2048:t0 + 2056]), writes=[PZB], chan=HRCH)
                        P.add('dve', lambda e: e.tensor_scalar(out=PZ[:, 2056:2064], in0=PZ[:, 2056:2064], scalar1=flag[:, 0:1], scalar2=None, op0=ALU.mult),
                              reads=[B_const], writes=[PZB])
                    else:
                        P.add('dve', lambda e: e.memset(PZ[:, 2056:2064], 0.0), writes=[PZB])
                    src, srcb = PZ, PZB
                    step = 1
                    n = PZW
                    bufs = [(PW1, PW1B), (PW2, PW2B)]
                    bi = 0
                    while step < w:
                        dst, dstb = bufs[bi % 2]
                        bi += 1
                        n2 = n - step
                        P.add('dve', lambda e, src=src, dst=dst, n2=n2, step=step: e.tensor_tensor(
                            out=dst[:, 0:n2], in0=src[:, 0:n2], in1=src[:, step:step + n2], op=ALU.add),
                            reads=[srcb], writes=[dstb])
                        src, srcb = dst, dstb
                        n = n2
                        step *= 2
                    o0 = 8 - w // 2
                    dst, dstb = bufs[bi % 2]
                    P.add('dve', lambda e, src=src, dst=dst, o0=o0: e.tensor_tensor(
                        out=dst[:, 0:2048], in0=src[:, o0:o0 + 2048], in1=ICN, op=ALU.mult), reads=[srcb, ICNB], writes=[dstb])
                    P.add('dve', lambda e, dst=dst: e.tensor_tensor(out=PD, in0=dst[:, 0:2048], in1=PZ[:, 8:8 + 2048], op=ALU.subtract),
                          reads=[dstb, PZB], writes=[PDB])
                    for tq in range(4):
                        bk = tq % 2
                        mm(bk, pw[:, j * 4 + g, :], PD[:, tq * T:(tq + 1) * T], True, True, [PDB, B_const])
                        k = post_i % 2
                        post_i += 1
                        P.add('act', lambda e, bk=bk, k=k, g=g: e.activation(out=POST[k], in_=bank(bk), func=AF.Identity,
                                                                           scale=smallp[:, 402 + j * 4 + g:403 + j * 4 + g]),
                              reads=[PSB[bk], B_const], writes=[POSTB[k]])
                        P.add('sp', lambda e, k=k, g=g, t0=t0, tq=tq: e.dma_start(
                            out=oT_d[4 + g][:, t0 + tq * T:t0 + (tq + 1) * T], in_=POST[k]), reads=[POSTB[k]], chan=POSTCH[k])

        def dv64_specs_even():
            return [([(0, 96, qT_d[h][0:96])], [(0, 64, kT_d[h][0:64]), (64, 96, kpe_d[:, :])], v64_d[h],
                     96, float(96 ** -0.5), h // 2, (h % 2) * 64, h) for h in range(8)]

        def dv64_specs_odd():
            return [([(0, 64, qT_d[m][0:64]), (64, 128, qT_d[m][0:64])],
                     [(0, 64, kT_d[m // 4][0:64]), (64, 128, kT_d[m // 4][0:64])], v64_d[m // 4],
                     64, 0.125, m // 2, (m % 2) * 64, m) for m in range(8)]

        def attention_phase(i):
            j = i // 2
            if i % 2 == 0:
                with contextlib.ExitStack() as pes:
                    cur_es[0] = pes
                    alloc_pool()
                    pool_branch(j)
                    P.barrier()
                    P.emit_pending(nc)
                with contextlib.ExitStack() as pes:
                    cur_es[0] = pes
                    alloc_attn()
                    ensure_ones()
                    attn_dv64_list(dv64_specs_even())
                    P.barrier()
                    P.emit_pending(nc)
            else:
                with contextlib.ExitStack() as pes:
                    cur_es[0] = pes
                    alloc_attn()
                    ensure_ones()
                    attn_dv64_list(dv64_specs_odd())
                    P.barrier()
                    for h in range(4):
                        attn_diff(j, h)
                    P.barrier()
                    P.emit_pending(nc)

        for ph in range(5):
            with contextlib.ExitStack() as pes:
                cur_es[0] = pes
                alloc_token()
                token_phase(ph)
                P.barrier()
                P.emit_pending(nc)
            if ph < 4:
                attention_phase(ph)
    return nc


_CACHE = {}


def kernel(**inputs):
    inp = {k: np.asarray(v) for k, v in inputs.items()}
    sh = _prep_shared(inp)
    pos_s = np.arange(NTOK)
    pos_p = np.arange(NTOK) % 2048
    tabs_s = _rope_tables(pos_s, pos_s // 64, pos_s % 64)
    tabs_p = _rope_tables(pos_p, pos_p // 64, pos_p % 64)
    icnt_s = _icnt(NTOK)
    icnt_p = _icnt(2048)
    in_maps = []
    for r in range(NCORES):
        m = _prep_core(inp, r, tabs_s, tabs_p, icnt_s, icnt_p)
        m.update(sh)
        in_maps.append(m)
    if 'nc' not in _CACHE:
        _CACHE['nc'] = build_program()
    nc = _CACHE['nc']
    res = run_bass_kernel_spmd(nc, in_maps, core_ids=list(range(NCORES)))
    outs = [np.asarray(r["yT"]).reshape(D, NTOK).T for r in res.results]
    y_sample = np.stack([np.ascontiguousarray(outs[r]) for r in range(4)]).astype(np.float32)
    y_prompt = np.concatenate([outs[r].reshape(4, 2048, D) for r in range(4, 8)], axis=0).astype(np.float32)
    return (np.ascontiguousarray(y_prompt), y_sample)
```

```python
import contextlib
import numpy as np
import concourse.bass as bass
import concourse.mybir as mybir
from concourse.bass_utils import run_bass_kernel_spmd

F32 = mybir.dt.float32
BF16 = mybir.dt.bfloat16
AF = mybir.ActivationFunctionType
ALU = mybir.AluOpType
AX = mybir.AxisListType

D = 1024
DFF = 2816
DEPTH = 4
NTOK = 8192
T = 512
NT = NTOK // T
EPS = 1e-6
NCORES = 8
NS = 420

ENGS = ['pe', 'act', 'dve', 'pool', 'sp']


class Buf:
    __slots__ = ('name', 'last_w', 'readers')

    def __init__(self, name):
        self.name = name
        self.last_w = None
        self.readers = []


class Chan:
    __slots__ = ('name', 'sem', 'count', 'nobar')

    def __init__(self, name, sem):
        self.name = name
        self.sem = sem
        self.count = 0
        self.nobar = False


class Op:
    __slots__ = ('eng', 'fn', 'waits', 'signal', 'chan', 'sigidx')

    def __init__(self, eng, fn):
        self.eng = eng
        self.fn = fn
        self.waits = []
        self.signal = False
        self.chan = None
        self.sigidx = 0


class Prog:
    def __init__(self, nc, sems):
        self.nc = nc
        self.free_sems = list(sems)
        self.ops = {e: [] for e in ENGS}
        self.esem = {e: self.free_sems.pop() for e in ENGS}
        self.waited_op = {e: {t: -1 for t in ENGS} for e in ENGS}
        self.waited_ch = {e: {} for e in ENGS}
        self.chans = []
        self.cuts = []
        self.emitted = {e: 0 for e in ENGS}
        self.nsig = {e: 0 for e in ENGS}

    def chan(self, name):
        for c in self.chans:
            if c.name == name:
                return c
        c = Chan(name, self.free_sems.pop())
        self.chans.append(c)
        return c

    def _add_wait(self, op, ref):
        e = op.eng
        if ref[0] == 'op':
            _, te, idx = ref
            if te == e and e in ('pe', 'sp'):
                return
            if self.waited_op[e][te] >= idx:
                return
            self.waited_op[e][te] = idx
            if idx < self.emitted[te] and not self.ops[te][idx].signal:
                raise AssertionError(f"late signal request on emitted op {te}[{idx}] from {e}")
            self.ops[te][idx].signal = True
            op.waits.append(ref)
        else:
            _, ch, cnt = ref
            if self.waited_ch[e].get(ch, 0) >= cnt:
                return
            self.waited_ch[e][ch] = cnt
            op.waits.append(ref)

    def add(self, eng, fn, reads=(), writes=(), chan=None):
        op = Op(eng, fn)
        for b in reads:
            if b.last_w is not None:
                self._add_wait(op, b.last_w)
        for b in writes:
            if b.last_w is not None:
                self._add_wait(op, b.last_w)
            for r in b.readers:
                self._add_wait(op, r)
        idx = len(self.ops[eng])
        self.ops[eng].append(op)
        if chan is not None:
            chan.count += 16
            op.chan = chan
            ref = ('dma', chan, chan.count)
        else:
            ref = ('op', eng, idx)
        for b in writes:
            b.last_w = ref
            b.readers = []
        for b in reads:
            if b in writes:
                continue
            if ref[0] == 'op':
                b.readers = [r for r in b.readers if not (r[0] == 'op' and r[1] == eng)]
            else:
                b.readers = [r for r in b.readers if not (r[0] == 'dma' and r[1] is chan)]
            b.readers.append(ref)
        return ref

    def wait_refs(self, eng, refs):
        op = Op(eng, None)
        for r in refs:
            self._add_wait(op, r)
        self.ops[eng].append(op)

    def barrier(self):
        refs = []
        for e in ('pe', 'act', 'dve'):
            if self.ops[e]:
                for idx in range(len(self.ops[e]) - 1, -1, -1):
                    if self.ops[e][idx].fn is not None and self.ops[e][idx].chan is None:
                        refs.append(('op', e, idx))
                        break
        for c in self.chans:
            if c.count and not c.nobar:
                refs.append(('dma', c, c.count))
        for e in ENGS:
            self.wait_refs(e, refs)

    def cut(self):
        self.cuts.append({e: len(self.ops[e]) for e in ENGS})

    def emit_pending(self, nc):
        prog = self
        for e in ENGS:
            for op in self.ops[e][self.emitted[e]:]:
                if op.signal:
                    self.nsig[e] += 1
                    op.sigidx = self.nsig[e]
        end = {e: len(self.ops[e]) for e in ENGS}
        bounds = [c for c in self.cuts if all(c[e] >= self.emitted[e] for e in ENGS)] + [end]
        self.cuts = []
        prev = dict(self.emitted)

        def run(engname, eng, lo, hi):
            for op in prog.ops[engname][lo:hi]:
                for r in op.waits:
                    if r[0] == 'op':
                        eng.wait_ge(prog.esem[r[1]], prog.ops[r[1]][r[2]].sigidx)
                    else:
                        eng.wait_ge(r[1].sem, r[2])
                if op.fn is None:
                    continue
                inst = op.fn(eng)
                if op.chan is not None:
                    inst.then_inc(op.chan.sem, 16)
                elif op.signal:
                    inst.then_inc(prog.esem[engname], 1)
                op.fn = None

        for bnd in bounds:
            if all(bnd[e] == prev[e] for e in ENGS):
                continue
            with nc.Block() as block:
                if bnd['pe'] > prev['pe']:
                    @block.tensor
                    def _(t, lo=prev['pe'], hi=bnd['pe']):
                        run('pe', t, lo, hi)
                if bnd['act'] > prev['act']:
                    @block.scalar
                    def _(s, lo=prev['act'], hi=bnd['act']):
                        run('act', s, lo, hi)
                if bnd['dve'] > prev['dve']:
                    @block.vector
                    def _(v, lo=prev['dve'], hi=bnd['dve']):
                        run('dve', v, lo, hi)
                if bnd['pool'] > prev['pool']:
                    @block.gpsimd
                    def _(g, lo=prev['pool'], hi=bnd['pool']):
                        run('pool', g, lo, hi)
                if bnd['sp'] > prev['sp']:
                    @block.sync
                    def _(sy, lo=prev['sp'], hi=bnd['sp']):
                        run('sp', sy, lo, hi)
            prev = bnd
        self.emitted = end


def _kblocks(w, cols):
    K = w.shape[0]
    sub = w[:, cols]
    return np.ascontiguousarray(sub.reshape(K // 128, 128, len(cols)).transpose(1, 0, 2)).reshape(128, -1)


def _swap_mla(d):
    return (d + 16) % 32


def _swap_ax(d):
    return (d + 16) % 32 if d < 32 else 32 + ((d - 32) + 16) % 32


def _swap_diff(d):
    return (d + 8) % 16 if d < 16 else d


def _rope_tables(pos, rows, cols):
    f32 = np.float32
    n = pos.shape[0]
    out = np.zeros((6, 128, n), f32)
    posf = pos.astype(f32)
    rowf = rows.astype(f32)
    colf = cols.astype(f32)
    inv_m = (f32(500000.0) ** (-np.arange(0, 16, dtype=f32) * f32(2.0) / f32(32))).astype(f32)
    inv_a = (f32(10000.0) ** (-np.arange(0, 16, dtype=f32) * f32(2.0) / f32(32))).astype(f32)
    inv_d = (f32(500000.0) ** (-np.arange(0, 8, dtype=f32) * f32(2.0) / f32(16))).astype(f32)
    for p in range(128):
        d = p % 32
        ang = (posf * inv_m[d % 16]).astype(f32)
        out[0, p] = np.cos(ang)
        out[1, p] = np.sin(ang) * (f32(-1.0) if d < 16 else f32(1.0))
        d = p % 64
        if d < 32:
            ang = (rowf * inv_a[d % 16]).astype(f32)
            sg = -1.0 if d < 16 else 1.0
        else:
            ang = (colf * inv_a[(d - 32) % 16]).astype(f32)
            sg = -1.0 if (d - 32) < 16 else 1.0
        out[2, p] = np.cos(ang)
        out[3, p] = np.sin(ang) * f32(sg)
        if d < 16:
            ang = (posf * inv_d[d % 8]).astype(f32)
            out[4, p] = np.cos(ang)
            out[5, p] = np.sin(ang) * (f32(-1.0) if d < 8 else f32(1.0))
        else:
            out[4, p] = 1.0
            out[5, p] = 0.0
    return out


def _prep_shared(inp):
    sh = {}
    w_ada = inp['w_ada']
    wada = np.empty((32, 128, 8 * 1152), np.float32)
    for i in range(4):
        for b in range(8):
            wada[i * 8 + b] = _kblocks(w_ada[i], np.arange(b * 1152, (b + 1) * 1152))
    sh['wada'] = wada
    fin = np.empty((8, 11, 128, 4096), np.float32)
    fout = np.empty((8, 4, 128, 5632), np.float32)
    for i in range(4):
        for k in range(2):
            w_in = inp['ffn_w_in'][i, k]
            w_out = inp['ffn_w_out'][i, k]
            for b in range(11):
                cols = np.concatenate([np.arange(256 * b, 256 * b + 256), DFF + np.arange(256 * b, 256 * b + 256)])
                fin[i * 2 + k, b] = _kblocks(w_in, cols)
            for c2 in range(4):
                fout[i * 2 + k, c2] = _kblocks(w_out, np.arange(256 * c2, 256 * c2 + 256))
    sh['ffn_in'] = fin
    sh['ffn_out'] = fout
    ev_in = np.empty((2, 128, 8 * 1216), np.float32)
    ev_uq = np.empty((2, 128, 3 * 1024), np.float32)
    ev_ukv = np.empty((2, 128, 2 * 1024), np.float32)
    ev_out = np.empty((2, 2, 128, 4096), np.float32)
    poolw = np.empty((128, 2 * 4 * 128), np.float32)
    for j in range(2):
        w = inp['ev_w_in'][j]
        kpe = 640 + np.arange(32)
        kpes = 640 + np.array([_swap_mla(d) for d in range(32)])
        cols = np.concatenate([np.arange(0, 640), 672 + np.arange(512), kpe, kpes])
        blocks = [cols[0:512], cols[512:1024], cols[1024:1216]]
        ev_in[j] = np.concatenate([_kblocks(w, b) for b in blocks], axis=1)
        wq = inp['mla_w_uq'][j]
        qc = []
        for h in range(8):
            qc.append(h * 96 + np.arange(64))
        for h in range(8):
            qc.append(h * 96 + 64 + np.arange(32))
        for h in range(8):
            qc.append(h * 96 + 64 + np.array([_swap_mla(d) for d in range(32)]))
        ev_uq[j] = _kblocks(wq, np.concatenate(qc))
        wkv = inp['mla_w_ukv'][j]
        kc = [h * 128 + np.arange(64) for h in range(8)] + [h * 128 + 64 + np.arange(64) for h in range(8)]
        ev_ukv[j] = _kblocks(wkv, np.concatenate(kc))
        wo = inp['ev_w_out'][j]
        for b in range(2):
            ev_out[j, b] = _kblocks(wo, np.arange(512 * b, 512 * b + 512))
        for g in range(4):
            poolw[:, (j * 4 + g) * 128:(j * 4 + g + 1) * 128] = inp['pool_w'][j, g]
    sh['ev_in'] = ev_in
    sh['ev_uq'] = ev_uq
    sh['ev_ukv'] = ev_ukv
    sh['ev_out'] = ev_out
    sh['poolw'] = poolw
    od_in = np.empty((2, 128, 8 * 3328), np.float32)
    od_v = np.empty((2, 128, 8 * 640), np.float32)
    od_out = np.empty((2, 2, 128, 4096), np.float32)
    sw_ax = np.array([_swap_ax(d) for d in range(64)])
    sw_df = np.array([_swap_diff(d) for d in range(64)])
    for j in range(2):
        w = inp['od_w_in'][j]
        chunks = []

        def ch(base, c, sw):
            direct = base + c * 128 + np.arange(128)
            swp = base + c * 128 + np.concatenate([sw, 64 + sw])
            chunks.append(direct)
            chunks.append(swp)

        for c in range(4):
            ch(0, c, sw_ax)
        ch(512, 0, sw_ax)
        for c in range(4):
            ch(768, c, sw_df)
        for c in range(4):
            ch(1280, c, sw_df)
        cols = np.concatenate(chunks)
        blocks = [cols[b * 512:(b + 1) * 512] for b in range(7)]
        od_in[j] = np.concatenate([_kblocks(w, b) for b in blocks], axis=1)
        od_v[j] = np.concatenate([_kblocks(w, 1792 + np.arange(512)), _kblocks(w, 640 + np.arange(128))], axis=1)
        wo = inp['od_w_out'][j]
        for b in range(2):
            od_out[j, b] = _kblocks(wo, np.arange(512 * b, 512 * b + 512))
    sh['od_in'] = od_in
    sh['od_v'] = od_v
    sh['od_out'] = od_out
    sp = np.zeros((128, NS), np.float32)
    for i in range(4):
        sp[:, i * 72:(i + 1) * 72] = inp['b_ada'][i].reshape(72, 128).T
        for n in range(3):
            sp[:, 288 + (i * 3 + n) * 8:288 + (i * 3 + n + 1) * 8] = inp['norm_g'][i, n].reshape(8, 128).T
    sp[:, 384:392] = inp['final_g'].reshape(8, 128).T
    pidx = np.arange(128) % 64
    for j in range(2):
        sp[:, 392 + j * 3:392 + j * 3 + 3] = inp['mla_gq'][j].reshape(3, 128).T
        sp[:, 398 + j * 2:398 + j * 2 + 2] = inp['mla_gkv'][j].reshape(2, 128).T
        sp[:, 402 + j * 4:402 + j * 4 + 4] = inp['pool_scale'][j].reshape(4, 128).T
        sp[:, 410 + j] = inp['gqa_gq'][j][pidx]
        sp[:, 412 + j] = inp['gqa_gq'][j][sw_ax[pidx]]
        sp[:, 414 + j] = inp['gqa_gk'][j][pidx]
        sp[:, 416 + j] = inp['gqa_gk'][j][sw_ax[pidx]]
        sp[:, 418 + j] = inp['diff_subln_g'][j]
    sh['smallp'] = sp
    lv = np.zeros((1, 512), np.float32)
    for j in range(2):
        for q, nm in enumerate(('diff_lq1', 'diff_lk1', 'diff_lq2', 'diff_lk2')):
            lv[0, (j * 4 + q) * 64:(j * 4 + q + 1) * 64] = inp[nm][j]
    sh['lvec'] = lv
    return sh


def _prep_core(inp, r, tabs_s, tabs_p, icnt_s, icnt_p):
    m = {}
    if r < 4:
        x = inp['x_sample'][r]
        cseg = np.stack([inp['c_sample'][r]] * 4)
        m['tabs'] = tabs_s
        m['icnt'] = icnt_s
        m['maskb'] = np.zeros((128, 16), np.float32)
        m['flag'] = np.ones((128, 1), np.float32)
    else:
        q = r - 4
        x = inp['x_prompt'][4 * q:4 * q + 4].reshape(NTOK, D)
        cseg = inp['c_prompt'][4 * q:4 * q + 4]
        m['tabs'] = tabs_p
        m['icnt'] = icnt_p
        mb = np.full((4, 4), -30000.0, np.float32)
        mb[np.arange(4), np.arange(4)] = 0.0
        m['maskb'] = np.ascontiguousarray(np.broadcast_to(mb.reshape(1, 16), (128, 16)))
        m['flag'] = np.zeros((128, 1), np.float32)
    m['xT'] = np.ascontiguousarray(x.T).reshape(8, 128, NTOK)
    m['c4T'] = np.ascontiguousarray(cseg.reshape(4, 8, 128).transpose(2, 1, 0)).reshape(128, 32)
    return m


def _icnt(S):
    t = np.arange(NTOK) % S
    out = np.empty((4, NTOK), np.float32)
    for g, w in enumerate((2, 4, 8, 16)):
        lo = np.clip(t - w // 2, 0, S)
        hi = np.clip(t + w - w // 2, 0, S)
        out[g] = (np.float32(1.0) / (hi - lo).astype(np.float32)).astype(np.float32)
    return out


def build_program():
    nc = bass.Bass("TRN2", target_bir_lowering=False)

    def din(name, shape, dt=F32):
        return nc.dram_tensor(name, list(shape), dt, kind="ExternalInput").ap()

    def dscr(name, shape, dt):
        return nc.dram_tensor(name, list(shape), dt).ap()

    xT_in = din("xT", [8, 128, NTOK])
    c4T_in = din("c4T", [128, 32])
    tabs_in = din("tabs", [6, 128, NTOK])
    maskb_in = din("maskb", [128, 16])
    flag_in = din("flag", [128, 1])
    icnt_in = din("icnt", [4, NTOK])
    smallp_in = din("smallp", [128, NS])
    lvec_in = din("lvec", [1, 512])
    wada_in = din("wada", [32, 128, 9216])
    ffn_in_f = din("ffn_in", [8, 11, 128, 4096])
    ffn_out_f = din("ffn_out", [8, 4, 128, 5632])
    ev_in_f = din("ev_in", [2, 128, 9728])
    ev_uq_f = din("ev_uq", [2, 128, 3072])
    ev_ukv_f = din("ev_ukv", [2, 128, 2048])
    ev_out_f = din("ev_out", [2, 2, 128, 4096])
    poolw_f = din("poolw", [128, 1024])
    od_in_f = din("od_in", [2, 128, 26624])
    od_v_f = din("od_v", [2, 128, 5120])
    od_out_f = din("od_out", [2, 2, 128, 4096])
    yT_out = nc.dram_tensor("yT", [8, 128, NTOK], F32, kind="ExternalOutput").ap()

    ffn_in_b = dscr("ffn_in_b", [8, 11, 128, 4096], BF16)
    ffn_out_b = dscr("ffn_out_b", [8, 4, 128, 5632], BF16)
    ev_in_b = dscr("ev_in_b", [2, 128, 9728], BF16)
    ev_uq_b = dscr("ev_uq_b", [2, 128, 3072], BF16)
    ev_ukv_b = dscr("ev_ukv_b", [2, 128, 2048], BF16)
    ev_out_b = dscr("ev_out_b", [2, 2, 128, 4096], BF16)
    od_in_b = dscr("od_in_b", [2, 128, 26624], BF16)
    od_v_b = dscr("od_v_b", [2, 128, 5120], BF16)
    od_out_b = dscr("od_out_b", [2, 2, 128, 4096], BF16)
    xs_d = dscr("xs_d", [8, 128, NTOK], F32)
    qT_d = dscr("qT_d", [16, 128, NTOK], BF16)
    kT_d = dscr("kT_d", [16, 128, NTOK], BF16)
    kpe_d = dscr("kpe_d", [32, NTOK], BF16)
    v64_d = dscr("v64_d", [8, 128, 64, 64], BF16)
    v128_d = dscr("v128_d", [4, 128, 64, 128], BF16)
    oT_d = dscr("oT_d", [8, 128, NTOK], BF16)
    pz_d = dscr("pz_d", [4, 128, NTOK], F32)

    ARENA_BYTES = 204 * 1024
    with contextlib.ExitStack() as es:
        sems = [es.enter_context(nc.semaphore(f"s{i}")) for i in range(60)]
        P = Prog(nc, sems)

        cur_es = [es]
        uid = [0]
        cursor = [0]

        def alloc(nbytes):
            return None

        def vf32(_off, n):
            uid[0] += 1
            return cur_es[0].enter_context(nc.sbuf_tensor(f"f{uid[0]}", [128, n], F32))[:, :]

        def vbf(_off, n):
            uid[0] += 1
            return cur_es[0].enter_context(nc.sbuf_tensor(f"b{uid[0]}", [128, n], BF16))[:, :]

        PSP = None
        PSB = None

        def alloc_psum():
            nonlocal PSP, PSB
            PSP = []
            for k in range(4):
                uid[0] += 1
                PSP.append(cur_es[0].enter_context(nc.psum_tensor(f"ps{uid[0]}", [128, 1024], F32))[:, :])
            PSB = [Buf(f"ps{b}") for b in range(8)]

        def bank(b):
            return PSP[b // 2][:, (b % 2) * 512:(b % 2) * 512 + 512]

        ones_bf = vbf(alloc(256), 128)
        blk_bf = vbf(alloc(256), 128)
        ones_f = vf32(alloc(512), 128)
        smallp = vf32(alloc(NS * 4), NS)
        c4T = vf32(alloc(128), 32)
        scT = vf32(alloc(128), 32)
        modT = [vf32(alloc(1152), 288) for _ in range(4)]
        Aall = [[vf32(alloc(128), 32) for _ in range(3)] for _ in range(4)]
        Gall = [[vf32(alloc(128), 32) for _ in range(3)] for _ in range(4)]
        maskb = vf32(alloc(64), 16)
        flag = vf32(alloc(32), 1)
        lvt = vf32(alloc(2048), 512)
        lamc = vf32(alloc(64), 16)
        sublng = vf32(alloc(32), 2)
        epsc = vf32(alloc(32), 1)
        poolw = vbf(alloc(2048), 1024)
        B_const = Buf("const")
        B_mod = Buf("mod")

        cst = P.chan("const")

        def cload(dst, src):
            P.add('sp', lambda e: e.dma_start(out=dst, in_=src), writes=[B_const], chan=cst)

        cload(smallp, smallp_in[:, :])
        cload(c4T, c4T_in[:, :])
        cload(maskb, maskb_in[:, :])
        cload(flag, flag_in[:, :])
        cload(lvt, lvec_in[:, :].partition_broadcast(128))
        cvt0 = P.chan("cvt_pw")
        P.add('pool', lambda e: e.dma_start(out=poolw, in_=poolw_f[:, :]), writes=[B_const], chan=cvt0)
        P.add('dve', lambda e: e.memset(ones_bf, 1.0), writes=[B_const])
        P.add('dve', lambda e: e.memset(ones_f, 1.0), writes=[B_const])
        P.add('dve', lambda e: e.memset(epsc, EPS), writes=[B_const])
        P.add('dve', lambda e: e.memset(blk_bf, 0.0), writes=[B_const])
        P.add('dve', lambda e: e.memset(blk_bf[0:64, 0:64], 1.0), writes=[B_const])
        P.add('dve', lambda e: e.memset(blk_bf[64:128, 64:128], 1.0), writes=[B_const])

        WD = [Buf(f"wd{i}") for i in range(4)]
        for i in range(4):
            ch = P.chan(f"cvt{i}")
            ch.nobar = True
            j = i // 2

            def cv(dst, src, ch=ch, i=i):
                P.add('pool', lambda e: e.dma_start(out=dst, in_=src), writes=[WD[i]], chan=ch)

            for b in range(11):
                cv(ffn_in_b[i * 2, b], ffn_in_f[i * 2, b])
            for c2 in range(4):
                cv(ffn_out_b[i * 2, c2], ffn_out_f[i * 2, c2])
            if i % 2 == 0:
                cv(ev_in_b[j], ev_in_f[j])
                cv(ev_uq_b[j], ev_uq_f[j])
                cv(ev_ukv_b[j], ev_ukv_f[j])
                for b in range(2):
                    cv(ev_out_b[j, b], ev_out_f[j, b])
            else:
                for b in range(7):
                    n = 4096 if b < 6 else 2048
                    cv(od_in_b[j][:, b * 4096:b * 4096 + n], od_in_f[j][:, b * 4096:b * 4096 + n])
                cv(od_v_b[j], od_v_f[j])
                for b in range(2):
                    cv(od_out_b[j, b], od_out_f[j, b])
            for b in range(11):
                cv(ffn_in_b[i * 2 + 1, b], ffn_in_f[i * 2 + 1, b])
            for c2 in range(4):
                cv(ffn_out_b[i * 2 + 1, c2], ffn_out_f[i * 2 + 1, c2])
            WD[i].last_w = ('dma', ch, ch.count)
            WD[i].readers = []

        pro_es = contextlib.ExitStack()
        cur_es[0] = pro_es
        alloc_psum()
        B_const.last_w = ('dma', cst, cst.count)
        P.wait_refs('dve', [('dma', cvt0, cvt0.count)])
        P.add('act', lambda e: e.activation(out=scT, in_=c4T, func=AF.Silu), reads=[B_const], writes=[B_mod])
        ada_off = [alloc(8 * 1152 * 4) for _ in range(2)]
        ada_buf = [vf32(o, 9216) for o in ada_off]
        ada_B = [Buf("ada0"), Buf("ada1")]
        ada_ch = [P.chan("ada0"), P.chan("ada1")]
        scT3 = scT.rearrange("p (k s) -> p k s", s=4)
        nb = 0
        for i in range(4):
            for b in range(8):
                sl = nb % 2
                wb = ada_buf[sl].rearrange("p (k n) -> p k n", n=1152)
                P.add('sp', lambda e, sl=sl, i=i, b=b: e.dma_start(out=ada_buf[sl], in_=wada_in[i * 8 + b]),
                      writes=[ada_B[sl]], chan=ada_ch[sl])
                pb = nb % 2
                for jj in range(9):
                    for kc in range(8):
                        P.add('pe', lambda e, wb=wb, jj=jj, kc=kc, pb=pb: e.matmul(
                            bank(pb)[:, jj * 4:jj * 4 + 4], wb[:, kc, jj * 128:(jj + 1) * 128], scT3[:, kc, :],
                            start=(kc == 0), stop=(kc == 7)),
                            reads=[ada_B[sl], B_mod], writes=[PSB[pb]])
                for jj in range(9):
                    jcol = b * 9 + jj
                    P.add('dve', lambda e, i=i, jj=jj, jcol=jcol, pb=pb: e.tensor_scalar(
                        out=modT[i][:, jcol * 4:jcol * 4 + 4], in0=bank(pb)[:, jj * 4:jj * 4 + 4],
                        scalar1=smallp[:, i * 72 + jcol:i * 72 + jcol + 1], scalar2=None, op0=ALU.add),
                        reads=[PSB[pb], B_const], writes=[B_mod])
                nb += 1
        for i in range(4):
            for n in range(3):
                for c in range(8):
                    js = (3 * n + 1) * 8 + c
                    P.add('dve', lambda e, i=i, n=n, c=c, js=js: e.tensor_scalar(
                        out=Aall[i][n][:, c * 4:c * 4 + 4], in0=modT[i][:, js * 4:js * 4 + 4],
                        scalar1=1.0, scalar2=smallp[:, 288 + (i * 3 + n) * 8 + c:288 + (i * 3 + n) * 8 + c + 1],
                        op0=ALU.add, op1=ALU.mult), reads=[B_mod, B_const], writes=[B_mod])
                jg = (3 * n + 2) * 8
                P.add('dve', lambda e, i=i, n=n, jg=jg: e.tensor_scalar(
                    out=Gall[i][n], in0=modT[i][:, jg * 4:jg * 4 + 32],
                    scalar1=(1.0 if n == 1 else 0.5), scalar2=None, op0=ALU.mult),
                    reads=[B_mod], writes=[B_mod])
        lam_tmp = vf32(alloc(256), 64)
        for j in range(2):
            li = 2 * j + 1
            lam_init = 0.8 - 0.6 * float(np.exp(-0.3 * li))
            for q in range(2):
                a = lvt[:, (j * 4 + 2 * q) * 64:(j * 4 + 2 * q + 1) * 64]
                b_ = lvt[:, (j * 4 + 2 * q + 1) * 64:(j * 4 + 2 * q + 2) * 64]
                P.add('dve', lambda e, a=a, b_=b_: e.tensor_tensor(out=lam_tmp, in0=a, in1=b_, op=ALU.mult),
                      reads=[B_const, B_mod], writes=[B_mod])
                P.add('dve', lambda e, j=j, q=q: e.reduce_sum(out=lamc[:, j * 4 + 2 + q:j * 4 + 3 + q], in_=lam_tmp, axis=AX.X),
                      reads=[B_mod], writes=[B_mod])
            P.add('act', lambda e, j=j: e.activation(out=lamc[:, j * 4 + 2:j * 4 + 4], in_=lamc[:, j * 4 + 2:j * 4 + 4], func=AF.Exp),
                  reads=[B_mod], writes=[B_mod])
            P.add('dve', lambda e, j=j, lam_init=lam_init: e.scalar_tensor_tensor(
                out=lamc[:, j * 4 + 1:j * 4 + 2], in0=lamc[:, j * 4 + 3:j * 4 + 4], scalar=-lam_init,
                in1=lamc[:, j * 4 + 2:j * 4 + 3], op0=ALU.add, op1=ALU.subtract), reads=[B_mod], writes=[B_mod])
            P.add('dve', lambda e, j=j, lam_init=lam_init: e.tensor_scalar(
                out=sublng[:, j:j + 1], in0=smallp[:, 418 + j:419 + j], scalar1=(1.0 - lam_init), scalar2=None, op0=ALU.mult),
                reads=[B_mod, B_const], writes=[B_mod])
        P.barrier()
        P.emit_pending(nc)
        pro_es.close()

        XSETS = XCHS = XSCHS = tmp = XTall = XT = XB = HTall = HT = HB = AT = AB = SQ = SQB = NTMP = TMP = TMPB = tmp_i = RT = RTB = RSTD = RSTDB = NRA = RA = RAB = RACH = ra_i = NRB = RBv = RBB = RBCH = rb_i = OTall = OT = OTB = OTCH = TAB = TABB = TABCH = NST = STG = STGB = STGCH = stg_i = VST = VSTB = VSTCH = PZS = PZSB = PZSCH = CQN = CQNB = XCH = XSCH = TOKEN_END = PSP = PSB = None

        def alloc_token():
            nonlocal XSETS, XCHS, XSCHS, tmp, XTall, XT, XB, HTall, HT, HB, AT, AB, SQ, SQB, NTMP, TMP, TMPB, tmp_i, RT, RTB, RSTD, RSTDB, NRA, RA, RAB, RACH, ra_i, NRB, RBv, RBB, RBCH, rb_i, OTall, OT, OTB, OTCH, TAB, TABB, TABCH, NST, STG, STGB, STGCH, stg_i, VST, VSTB, VSTCH, PZS, PZSB, PZSCH, CQN, CQNB, XCH, XSCH, TOKEN_END, PSP, PSB
            alloc_psum()
            XSETS = []
            for q_ in range(2):
                xa_ = vf32(None, 8 * T)
                XSETS.append((xa_, [xa_[:, c * T:(c + 1) * T] for c in range(8)], [Buf(f"x{q_}_{c}") for c in range(8)]))
            XTall, XT, XB = XSETS[0]
            XCHS = [P.chan("xld0"), P.chan("xld1")]
            XSCHS = [P.chan("xst0"), P.chan("xst1")]
            HTall = vbf(None, 8 * T)
            HT = [HTall[:, c * T:(c + 1) * T] for c in range(8)]
            HB = [Buf(f"h{c}") for c in range(8)]
            AT = [vbf(None, T) for f in range(22)]
            AB = [Buf(f"a{f}") for f in range(22)]
            SQ = [vbf(None, T) for c in range(8)]
            SQB = [Buf(f"sq{c}") for c in range(8)]
            NTMP = 4
            TMP = [vf32(alloc(T * 4), T) for _ in range(NTMP)]
            TMPB = [Buf(f"tmp{k}") for k in range(NTMP)]
            tmp_i = [0]

            def tmp():
                k = tmp_i[0] % NTMP
                tmp_i[0] += 1
                return TMP[k], TMPB[k]

            RT = vf32(alloc(T * 4), T)
            RTB = Buf("rt")
            RSTD = vf32(alloc(T * 4), T)
            RSTDB = Buf("rstd")
            NRA = 4
            RA = [vbf(alloc(4096 * 2), 4096) for _ in range(NRA)]
            RAB = [Buf(f"ra{k}") for k in range(NRA)]
            RACH = [P.chan(f"ra{k}") for k in range(NRA)]
            ra_i = [0]
            NRB = 2
            RBv = [vbf(alloc(5632 * 2), 5632) for _ in range(NRB)]
            RBB = [Buf(f"rb{k}") for k in range(NRB)]
            RBCH = [P.chan(f"rb{k}") for k in range(NRB)]
            rb_i = [0]
            OTall = vbf(None, 8 * T)
            OT = [OTall[:, c * T:(c + 1) * T] for c in range(8)]
            OTB = Buf("ot")
            OTCH = P.chan("ot")
            TAB = [vf32(alloc(T * 4), T) for _ in range(4)]
            TABB = Buf("tab")
            TABCH = P.chan("tab")
            NST = 4
            STG = [vbf(alloc(T * 2), T) for _ in range(NST)]
            STGB = [Buf(f"stg{k}") for k in range(NST)]
            STGCH = [P.chan(f"stg{k}") for k in range(NST)]
            stg_i = [0]
            VST = vbf(alloc(4 * 640 * 2), 4 * 640)
            VSTB = Buf("vst")
            VSTCH = P.chan("vst")
            PZS = vf32(alloc(4 * T * 4), 4 * T)
            PZSB = Buf("pzs")
            PZSCH = P.chan("pzs")
            CQN = [vbf(alloc(T * 2), T) for _ in range(3)]
            CQNB = [Buf(f"cqn{k}") for k in range(3)]
            XCH = P.chan("xld")
            XSCH = P.chan("xst")
            TOKEN_END = cursor[0]


        def stg():
            k = stg_i[0] % NST
            stg_i[0] += 1
            return STG[k], STGB[k], STGCH[k]

        def ringA(src, n, wd):
            k = ra_i[0] % NRA
            ra_i[0] += 1
            dst = RA[k][:, 0:n]
            P.add('sp', lambda e: e.dma_start(out=dst, in_=src), reads=[wd], writes=[RAB[k]], chan=RACH[k])
            return RA[k], RAB[k]

        def ringB(src, wd):
            k = rb_i[0] % NRB
            rb_i[0] += 1
            dst = RBv[k]
            P.add('sp', lambda e: e.dma_start(out=dst, in_=src), reads=[wd], writes=[RBB[k]], chan=RBCH[k])
            return RBv[k], RBB[k]

        def mm(bk, lhsT, rhs, start, stop, reads, rows=None):
            out = bank(bk) if rows is None else bank(bk)[rows[0]:rows[1], :]
            P.add('pe', lambda e: e.matmul(out, lhsT, rhs, start=start, stop=stop), reads=reads, writes=[PSB[bk]])

        def sqrt_recip(bk, inv_n, rows=(0, 128)):
            r0, r1 = rows
            P.add('act', lambda e: e.activation(out=RT[r0:r1, :], in_=bank(bk)[r0:r1, :], func=AF.Sqrt,
                                                bias=epsc[r0:r1, :], scale=inv_n),
                  reads=[PSB[bk], B_const], writes=[RTB])
            P.add('dve', lambda e: e.reciprocal(out=RSTD[r0:r1, :], in_=RT[r0:r1, :]), reads=[RTB], writes=[RSTDB])

        def norm_main(Acols, Bcols, s):
            for c in range(8):
                P.add('act', lambda e, c=c, XT=XT: e.activation(out=SQ[c], in_=XT[c], func=AF.Square),
                      reads=[XB[c]], writes=[SQB[c]])
            for c in range(8):
                mm(6, ones_bf, SQ[c], c == 0, c == 7, [SQB[c], B_const])
            sqrt_recip(6, 1.0 / D)
            for c in range(8):
                tv, tb = tmp()
                P.add('dve', lambda e, c=c, tv=tv, XT=XT: e.scalar_tensor_tensor(
                    out=tv, in0=XT[c], scalar=Acols[:, c * 4 + s:c * 4 + s + 1], in1=RSTD, op0=ALU.mult, op1=ALU.mult),
                    reads=[XB[c], RSTDB, B_mod], writes=[tb])
                if Bcols is None:
                    continue
                P.add('act', lambda e, c=c, tv=tv: e.activation(out=HT[c], in_=tv, func=AF.Identity,
                                                                bias=Bcols[:, c * 4 + s:c * 4 + s + 1], scale=1.0),
                      reads=[tb, B_mod], writes=[HB[c]])

        def ffn(i, k, s):
            G = Gall[i][0 if k == 0 else 2]
            pp = 0
            for b in range(11):
                wv, wb = ringA(ffn_in_b[i * 2 + k, b], 4096, WD[i])
                w3 = wv.rearrange("p (k n) -> p k n", n=512)
                for j in range(2):
                    f = 2 * b + j
                    bg, bu = (0, 1) if pp % 2 == 0 else (2, 3)
                    pp += 1
                    for kc in range(8):
                        mm(bg, w3[:, kc, j * 128:(j + 1) * 128], HT[kc], kc == 0, kc == 7, [wb, HB[kc]])
                    for kc in range(8):
                        mm(bu, w3[:, kc, 256 + j * 128:256 + (j + 1) * 128], HT[kc], kc == 0, kc == 7, [wb, HB[kc]])
                    tv, tb = tmp()
                    P.add('act', lambda e, bg=bg, tv=tv: e.activation(out=tv, in_=bank(bg), func=AF.Silu),
                          reads=[PSB[bg]], writes=[tb])
                    P.add('dve', lambda e, bu=bu, tv=tv, f=f: e.tensor_tensor(out=AT[f], in0=bank(bu), in1=tv, op=ALU.mult),
                          reads=[PSB[bu], tb], writes=[AB[f]])
            for c2 in range(4):
                wv, wb = ringB(ffn_out_b[i * 2 + k, c2], WD[i])
                w3 = wv.rearrange("p (f n) -> p f n", n=256)
                for cc in range(2):
                    c = 2 * c2 + cc
                    by = 4 + (c % 2)
                    for f in range(22):
                        mm(by, w3[:, f, cc * 128:(cc + 1) * 128], AT[f], f == 0, f == 21, [wb, AB[f]])
                    P.add('dve', lambda e, c=c, by=by, XT=XT: e.scalar_tensor_tensor(
                        out=XT[c], in0=bank(by), scalar=G[:, c * 4 + s:c * 4 + s + 1], in1=XT[c], op0=ALU.mult, op1=ALU.add),
                        reads=[PSB[by], B_mod], writes=[XB[c]])

        def outproj(i, t, s):
            j = i // 2
            wsrc = ev_out_b if i % 2 == 0 else od_out_b
            G = Gall[i][1]
            for b in range(2):
                wv, wb = ringA(wsrc[j, b], 4096, WD[i])
                w3 = wv.rearrange("p (k n) -> p k n", n=512)
                for cc in range(4):
                    c = 4 * b + cc
                    by = 4 + (c % 2)
                    for kc in range(8):
                        mm(by, w3[:, kc, cc * 128:(cc + 1) * 128], OT[kc], kc == 0, kc == 7, [wb, OTB])
                    P.add('dve', lambda e, c=c, by=by, XT=XT: e.scalar_tensor_tensor(
                        out=XT[c], in0=bank(by), scalar=G[:, c * 4 + s:c * 4 + s + 1], in1=XT[c], op0=ALU.mult, op1=ALU.add),
                        reads=[PSB[by], B_mod], writes=[XB[c]])

        def store_rows(sv, sb, sch, pieces, t):
            for (r0, r1, dap) in pieces:
                P.add('sp', lambda e, r0=r0, r1=r1, dap=dap: e.dma_start(out=dap[:, t * T:(t + 1) * T], in_=sv[r0:r1, :]),
                      reads=[sb], chan=sch)

        def load_tabs(t, which):
            for q, w in enumerate(which):
                P.add('sp', lambda e, q=q, w=w: e.dma_start(out=TAB[q], in_=tabs_in[w][:, t * T:(t + 1) * T]),
                      writes=[TABB], chan=TABCH)

        def rope_combine(bd, bs, rows, cosT, sinT, out_ap, out_b, gcols=None, rstd=False):
            r0, r1 = rows
            t1, b1 = tmp()
            t2, b2 = tmp()
            if gcols is None:
                P.add('dve', lambda e: e.tensor_tensor(out=t1[r0:r1, :], in0=bank(bd)[r0:r1, :], in1=cosT[r0:r1, :], op=ALU.mult),
                      reads=[PSB[bd], TABB], writes=[b1])
                P.add('dve', lambda e: e.tensor_tensor(out=t2[r0:r1, :], in0=bank(bs)[r0:r1, :], in1=sinT[r0:r1, :], op=ALU.mult),
                      reads=[PSB[bs], TABB], writes=[b2])
            else:
                g, gs = gcols
                P.add('dve', lambda e: e.scalar_tensor_tensor(out=t1[r0:r1, :], in0=bank(bd)[r0:r1, :], scalar=g[r0:r1, :],
                                                              in1=cosT[r0:r1, :], op0=ALU.mult, op1=ALU.mult),
                      reads=[PSB[bd], TABB, B_const], writes=[b1])
                P.add('dve', lambda e: e.scalar_tensor_tensor(out=t2[r0:r1, :], in0=bank(bs)[r0:r1, :], scalar=gs[r0:r1, :],
                                                              in1=sinT[r0:r1, :], op0=ALU.mult, op1=ALU.mult),
                      reads=[PSB[bs], TABB, B_const], writes=[b2])
            if not rstd:
                P.add('dve', lambda e: e.tensor_tensor(out=out_ap[r0:r1, :], in0=t1[r0:r1, :], in1=t2[r0:r1, :], op=ALU.add),
                      reads=[b1, b2], writes=[out_b])
            else:
                P.add('dve', lambda e: e.tensor_tensor(out=t1[r0:r1, :], in0=t1[r0:r1, :], in1=t2[r0:r1, :], op=ALU.add),
                      reads=[b2], writes=[b1])
                P.add('dve', lambda e: e.tensor_tensor(out=out_ap[r0:r1, :], in0=t1[r0:r1, :], in1=RSTD[r0:r1, :], op=ALU.mult),
                      reads=[b1, RSTDB], writes=[out_b])

        def sub_norm(banks, gbase, nfeat):
            n = len(banks)
            for k, bk in enumerate(banks):
                P.add('act', lambda e, k=k, bk=bk: e.activation(out=SQ[k], in_=bank(bk), func=AF.Square),
                      reads=[PSB[bk]], writes=[SQB[k]])
            for k in range(n):
                mm(6, ones_bf, SQ[k], k == 0, k == n - 1, [SQB[k], B_const])
            sqrt_recip(6, 1.0 / nfeat)
            for k, bk in enumerate(banks):
                P.add('dve', lambda e, k=k, bk=bk: e.scalar_tensor_tensor(
                    out=CQN[k], in0=bank(bk), scalar=smallp[:, gbase + k:gbase + k + 1], in1=RSTD, op0=ALU.mult, op1=ALU.mult),
                    reads=[PSB[bk], RSTDB, B_const], writes=[CQNB[k]])

        def inproj_even(i, t):
            j = i // 2
            load_tabs(t, (0, 1))
            W = []
            offs = [(0, 4096), (4096, 4096), (8192, 1536)]
            ncol = [512, 512, 192]

            def getw(b):
                wv, wb = ringA(ev_in_b[j][:, offs[b][0]:offs[b][0] + offs[b][1]], offs[b][1], WD[i])
                return wv[:, 0:offs[b][1]].rearrange("p (k n) -> p k n", n=ncol[b]), wb

            def proj(bk, w3, wb, c0, c1, rows=None):
                for kc in range(8):
                    mm(bk, w3[:, kc, c0:c1], HT[kc], kc == 0, kc == 7, [wb, HB[kc]], rows=rows)

            w3, wb = getw(0)
            for c in range(3):
                proj(c, w3, wb, c * 128, (c + 1) * 128)
            proj(3, w3, wb, 384, 512)
            w3, wb = getw(1)
            proj(4, w3, wb, 0, 128)
            pzb = [5, 7, 5, 7]
            for g in range(3):
                proj(pzb[g], w3, wb, 128 + g * 128, 256 + g * 128)
                P.add('act', lambda e, g=g: e.activation(out=PZS[:, g * T:(g + 1) * T], in_=bank(pzb[g]), func=AF.Copy),
                      reads=[PSB[pzb[g]]], writes=[PZSB])
            w3, wb = getw(2)
            proj(7, w3, wb, 0, 128)
            P.add('act', lambda e: e.activation(out=PZS[:, 3 * T:4 * T], in_=bank(7), func=AF.Copy),
                  reads=[PSB[7]], writes=[PZSB])
            P.add('sp', lambda e: e.dma_start(out=pz_d[:, :, t * T:(t + 1) * T].rearrange("g p n -> p g n"),
                                              in_=PZS.rearrange("p (g n) -> p g n", n=T)), reads=[PZSB], chan=PZSCH)
            proj(5, w3, wb, 128, 160, rows=(0, 32))
            proj(7, w3, wb, 160, 192, rows=(0, 32))
            sv, sb, sch = stg()
            rope_combine(5, 7, (0, 32), TAB[0], TAB[1], sv, sb)
            store_rows(sv, sb, sch, [(0, 32, kpe_d)], t)
            sub_norm([0, 1, 2], 392 + j * 3, 384)
            wv, wb = ringA(ev_uq_b[j], 3072, WD[i])
            w3 = wv[:, 0:3072].rearrange("p (k n) -> p k n", n=1024)
            for cn in range(4):
                bk = cn % 2
                for kc in range(3):
                    mm(bk, w3[:, kc, cn * 128:(cn + 1) * 128], CQN[kc], kc == 0, kc == 2, [wb, CQNB[kc]])
                sv, sb, sch = stg()
                P.add('act', lambda e, bk=bk, sv=sv: e.activation(out=sv, in_=bank(bk), func=AF.Copy),
                      reads=[PSB[bk]], writes=[sb])
                store_rows(sv, sb, sch, [(0, 64, qT_d[2 * cn][0:64]), (64, 128, qT_d[2 * cn + 1][0:64])], t)
            for cr in range(2):
                for kc in range(3):
                    mm(0, w3[:, kc, 512 + cr * 128:512 + (cr + 1) * 128], CQN[kc], kc == 0, kc == 2, [wb, CQNB[kc]])
                for kc in range(3):
                    mm(1, w3[:, kc, 768 + cr * 128:768 + (cr + 1) * 128], CQN[kc], kc == 0, kc == 2, [wb, CQNB[kc]])
                sv, sb, sch = stg()
                rope_combine(0, 1, (0, 128), TAB[0], TAB[1], sv, sb)
                store_rows(sv, sb, sch, [(hh * 32, hh * 32 + 32, qT_d[4 * cr + hh][64:96]) for hh in range(4)], t)
            sub_norm([3, 4], 398 + j * 2, 256)
            wv, wb = ringA(ev_ukv_b[j], 2048, WD[i])
            w3 = wv[:, 0:2048].rearrange("p (k n) -> p k n", n=1024)
            for ck in range(4):
                bk = ck % 2
                for kc in range(2):
                    mm(bk, w3[:, kc, ck * 128:(ck + 1) * 128], CQN[kc], kc == 0, kc == 1, [wb, CQNB[kc]])
                sv, sb, sch = stg()
                P.add('act', lambda e, bk=bk, sv=sv: e.activation(out=sv, in_=bank(bk), func=AF.Copy),
                      reads=[PSB[bk]], writes=[sb])
                store_rows(sv, sb, sch, [(0, 64, kT_d[2 * ck][0:64]), (64, 128, kT_d[2 * ck + 1][0:64])], t)
            vst3 = VST.rearrange("p (b n) -> p b n", n=640)
            for tb_ in range(4):
                bk = 2 + tb_ % 2
                for kc in range(2):
                    mm(bk, CQN[kc][:, tb_ * 128:(tb_ + 1) * 128], w3[:, kc, 512:1024], kc == 0, kc == 1, [wb, CQNB[kc]])
                P.add('act', lambda e, bk=bk, tb_=tb_: e.activation(out=vst3[:, tb_, 0:512], in_=bank(bk), func=AF.Copy),
                      reads=[PSB[bk]], writes=[VSTB])
            for h in range(8):
                P.add('sp', lambda e, h=h: e.dma_start(out=v64_d[h][:, t * 4:(t + 1) * 4, :], in_=vst3[:, :, h * 64:(h + 1) * 64]),
                      reads=[VSTB], chan=VSTCH)

        def inproj_odd(i, t):
            j = i // 2
            load_tabs(t, (2, 3, 4, 5))
            gq = (smallp[:, 410 + j:411 + j], smallp[:, 412 + j:413 + j])
            gk = (smallp[:, 414 + j:415 + j], smallp[:, 416 + j:417 + j])
            seq = [('qc', c) for c in range(4)] + [('kc', 0)] + [('qd', c) for c in range(4)] + [('kd', c) for c in range(4)]
            cur = {'b': -1, 'w3': None, 'wb': None}

            def wcols(ci):
                b = ci // 4
                if b != cur['b']:
                    n = 4096 if b < 6 else 2048
                    wv, wb = ringA(od_in_b[j][:, b * 4096:b * 4096 + n], n, WD[i])
                    cur['b'] = b
                    cur['w3'] = wv[:, 0:n].rearrange("p (k n) -> p k n", n=n // 8)
                    cur['wb'] = wb
                return cur['w3'], cur['wb'], (ci % 4) * 128

            pp = 0
            for si, (kind, idx) in enumerate(seq):
                bd, bs = (0, 1) if pp % 2 == 0 else (2, 3)
                pp += 1
                for q, bk in enumerate((bd, bs)):
                    w3, wb, c0 = wcols(2 * si + q)
                    for kc in range(8):
                        mm(bk, w3[:, kc, c0:c0 + 128], HT[kc], kc == 0, kc == 7, [wb, HB[kc]])
                sv, sb, sch = stg()
                if kind in ('qc', 'kc'):
                    P.add('act', lambda e, bd=bd: e.activation(out=SQ[0], in_=bank(bd), func=AF.Square),
                          reads=[PSB[bd]], writes=[SQB[0]])
                    mm(6, blk_bf, SQ[0], True, True, [SQB[0], B_const])
                    sqrt_recip(6, 1.0 / 64)
                    rope_combine(bd, bs, (0, 128), TAB[0], TAB[1], sv, sb, gcols=(gq if kind == 'qc' else gk), rstd=True)
                    if kind == 'qc':
                        pieces = [(0, 64, qT_d[2 * idx][0:64]), (64, 128, qT_d[2 * idx + 1][0:64])]
                    else:
                        pieces = [(0, 64, kT_d[0][0:64]), (64, 128, kT_d[1][0:64])]
                else:
                    rope_combine(bd, bs, (0, 128), TAB[2], TAB[3], sv, sb)
                    if kind == 'qd':
                        pieces = [(0, 64, qT_d[8 + 2 * idx][0:64]), (64, 128, qT_d[8 + 2 * idx + 1][0:64])]
                    else:
                        pieces = [(0, 64, kT_d[2 + 2 * idx][0:64]), (64, 128, kT_d[2 + 2 * idx + 1][0:64])]
                store_rows(sv, sb, sch, pieces, t)
            wv, wb = ringA(od_v_b[j][:, 0:4096], 4096, WD[i])
            w3 = wv.rearrange("p (k n) -> p k n", n=512)
            wv2, wb2 = ringA(od_v_b[j][:, 4096:5120], 1024, WD[i])
            w32 = wv2[:, 0:1024].rearrange("p (k n) -> p k n", n=128)
            vst3 = VST.rearrange("p (b n) -> p b n", n=640)
            for tb_ in range(4):
                bk = tb_ % 2
                for kc in range(8):
                    mm(bk, HT[kc][:, tb_ * 128:(tb_ + 1) * 128], w3[:, kc, :], kc == 0, kc == 7, [wb, HB[kc]])
                P.add('act', lambda e, bk=bk, tb_=tb_: e.activation(out=vst3[:, tb_, 0:512], in_=bank(bk), func=AF.Copy),
                      reads=[PSB[bk]], writes=[VSTB])
                bk2 = 2 + tb_ % 2
                for kc in range(8):
                    P.add('pe', lambda e, kc=kc, bk2=bk2, tb_=tb_: e.matmul(
                        bank(bk2)[:, 0:128], HT[kc][:, tb_ * 128:(tb_ + 1) * 128], w32[:, kc, :], start=(kc == 0), stop=(kc == 7)),
                        reads=[wb2, HB[kc]], writes=[PSB[bk2]])
                P.add('act', lambda e, bk2=bk2, tb_=tb_: e.activation(out=vst3[:, tb_, 512:640], in_=bank(bk2)[:, 0:128], func=AF.Copy),
                      reads=[PSB[bk2]], writes=[VSTB])
            for h in range(4):
                P.add('sp', lambda e, h=h: e.dma_start(out=v128_d[h][:, t * 4:(t + 1) * 4, :], in_=vst3[:, :, h * 128:(h + 1) * 128]),
                      reads=[VSTB], chan=VSTCH)
            for h in range(2):
                P.add('sp', lambda e, h=h: e.dma_start(out=v64_d[h][:, t * 4:(t + 1) * 4, :], in_=vst3[:, :, 512 + h * 64:512 + (h + 1) * 64]),
                      reads=[VSTB], chan=VSTCH)

        def token_phase(ph):
            nonlocal XTall, XT, XB
            src = xT_in if ph == 0 else xs_d

            def xload(tt):
                xa, _, xb = XSETS[tt % 2]
                P.add('sp', lambda e: e.dma_start(
                    out=xa.rearrange("p (c n) -> p c n", n=T), in_=src[:, :, tt * T:(tt + 1) * T].rearrange("c p n -> p c n")),
                    writes=xb, chan=XCHS[tt % 2])

            for t in range(NT):
                P.cut()
                s = t // 4
                if t == 0:
                    xload(0)
                if t + 1 < NT:
                    xload(t + 1)
                XTall, XT, XB = XSETS[t % 2]
                XSCH_t = XSCHS[t % 2]
                if ph > 0:
                    i = ph - 1
                    P.add('sp', lambda e, t=t: e.dma_start(
                        out=OTall.rearrange("p (c n) -> p c n", n=T), in_=oT_d[:, :, t * T:(t + 1) * T].rearrange("c p n -> p c n")),
                        writes=[OTB], chan=OTCH)
                    outproj(i, t, s)
                    norm_main(Aall[i][2], modT[i][:, 6 * 32:7 * 32], s)
                    ffn(i, 1, s)
                if ph < 4:
                    i = ph
                    norm_main(Aall[i][0], modT[i][:, 0:32], s)
                    ffn(i, 0, s)
                    norm_main(Aall[i][1], modT[i][:, 3 * 32:4 * 32], s)
                    if i % 2 == 0:
                        inproj_even(i, t)
                    else:
                        inproj_odd(i, t)
                    P.add('sp', lambda e, t=t, xa_cur=XTall: e.dma_start(
                        out=xs_d[:, :, t * T:(t + 1) * T].rearrange("c p n -> p c n"), in_=xa_cur.rearrange("p (c n) -> p c n", n=T)),
                        reads=XB, chan=XSCH_t)
                else:
                    for c in range(8):
                        P.add('act', lambda e, c=c, XT=XT: e.activation(out=SQ[c], in_=XT[c], func=AF.Square),
                              reads=[XB[c]], writes=[SQB[c]])
                    for c in range(8):
                        mm(6, ones_bf, SQ[c], c == 0, c == 7, [SQB[c], B_const])
                    sqrt_recip(6, 1.0 / D)
                    for c in range(8):
                        P.add('dve', lambda e, c=c, XT=XT: e.scalar_tensor_tensor(
                            out=XT[c], in0=XT[c], scalar=smallp[:, 384 + c:385 + c], in1=RSTD, op0=ALU.mult, op1=ALU.mult),
                            reads=[RSTDB, B_const], writes=[XB[c]])
                    P.add('sp', lambda e, t=t, xa_cur=XTall: e.dma_start(
                        out=yT_out[:, :, t * T:(t + 1) * T].rearrange("c p n -> p c n"), in_=xa_cur.rearrange("p (c n) -> p c n", n=T)),
                        reads=XB, chan=XSCH_t)

        ACC = ACCB = KT = QT = VA = VBt = KTB = QTB = VAB = VBB = KCH = QCH = VCH = NPT = PT = PTB = REC = RECB = NOS = OST = OSTB = OSTCH = ost_i = O1 = O2 = O1B = O2B = ASQ = ASQB = ART = ARTB = ARS = ARSB = ATT_END = PSP = PSB = None

        def alloc_attn():
            nonlocal ACC, ACCB, KT, QT, VA, VBt, KTB, QTB, VAB, VBB, KCH, QCH, VCH, NPT, PT, PTB, REC, RECB, NOS, OST, OSTB, OSTCH, ost_i, O1, O2, O1B, O2B, ASQ, ASQB, ART, ARTB, ARS, ARSB, ATT_END, PSP, PSB
            alloc_psum()
            KT = [vbf(alloc(NTOK * 2), NTOK) for _ in range(2)]
            QT = [vbf(alloc(NTOK * 2), NTOK) for _ in range(2)]
            VA = [vbf(alloc(64 * 128 * 2), 64 * 128) for _ in range(2)]
            VBt = [vbf(alloc(64 * 128 * 2), 64 * 128) for _ in range(2)]
            KTB = [Buf("kt0"), Buf("kt1")]
            QTB = [Buf("qt0"), Buf("qt1")]
            VAB = [Buf("va0"), Buf("va1")]
            VBB = [Buf("vb0"), Buf("vb1")]
            KCH = [P.chan("k0"), P.chan("k1")]
            QCH = [P.chan("q0"), P.chan("q1")]
            VCH = [P.chan("v0"), P.chan("v1")]
            NPT = 3
            PT = [vbf(alloc(1024 * 2), 1024) for _ in range(NPT)]
            PTB = [Buf(f"pt{k}") for k in range(NPT)]
            REC = [vf32(alloc(T * 4), T) for _ in range(2)]
            RECB = [Buf("rec0"), Buf("rec1")]
            NOS = 3
            OST = [vbf(alloc(T * 2), T) for _ in range(NOS)]
            OSTB = [Buf(f"ost{k}") for k in range(NOS)]
            OSTCH = [P.chan(f"ost{k}") for k in range(NOS)]
            ost_i = [0]
            O1 = vf32(alloc(T * 4), T)
            O2 = vf32(alloc(T * 4), T)
            O1B = Buf("o1")
            O2B = Buf("o2")
            ASQ = vbf(alloc(T * 2), T)
            ASQB = Buf("asq")
            ART = vf32(alloc(T * 4), T)
            ARTB = Buf("art")
            ARS = vf32(alloc(T * 4), T)
            ARSB = Buf("ars")
            ACC = [vf32(alloc(1024 * 4), 1024) for _ in range(4)]
            ACCB = [Buf(f"acc{k}") for k in range(4)]
            ATT_END = cursor[0]

        PZW = PZ = PW1 = PW2 = ICN = PD = PZB = PW1B = PW2B = ICNB = PDB = PZCH = ICNCH = HLCH = HRCH = POST = POSTB = POSTCH = POOL_END = PSP = PSB = None

        def alloc_pool():
            nonlocal PZW, PZ, PW1, PW2, ICN, PD, PZB, PW1B, PW2B, ICNB, PDB, PZCH, ICNCH, HLCH, HRCH, POST, POSTB, POSTCH, POOL_END, PSP, PSB
            alloc_psum()
            PZW = 2048 + 16
            PZ = vf32(alloc(PZW * 4), PZW)
            PW1 = vf32(alloc(PZW * 4), PZW)
            PW2 = vf32(alloc(PZW * 4), PZW)
            ICN = vf32(alloc(2048 * 4), 2048)
            PD = vbf(alloc(2048 * 2), 2048)
            PZB, PW1B, PW2B, ICNB, PDB = Buf("pz"), Buf("pw1"), Buf("pw2"), Buf("icn"), Buf("pd")
            PZCH = P.chan("pzl")
            ICNCH = P.chan("icn")
            HLCH = P.chan("hl")
            HRCH = P.chan("hr")
            POST = [vbf(alloc(T * 2), T) for _ in range(2)]
            POSTB = [Buf("post0"), Buf("post1")]
            POSTCH = [P.chan("post0"), P.chan("post1")]
            POOL_END = cursor[0]


        def ost():
            k = ost_i[0] % NOS
            ost_i[0] += 1
            return OST[k], OSTB[k], OSTCH[k]

        set_i = [0]

        def load_qk(qsrcs, ksrcs):
            sl = set_i[0] % 2
            for (r0, r1, ap) in qsrcs:
                P.add('sp', lambda e, r0=r0, r1=r1, ap=ap: e.dma_start(out=QT[sl][r0:r1, :], in_=ap), writes=[QTB[sl]], chan=QCH[sl])
            for (r0, r1, ap) in ksrcs:
                P.add('sp', lambda e, r0=r0, r1=r1, ap=ap: e.dma_start(out=KT[sl][r0:r1, :], in_=ap), writes=[KTB[sl]], chan=KCH[sl])
            return sl

        NSS = 3

        def attn_units(sl, dk, scale, pv, obanks_for_qg, finalize, vreads):
            units = [(qg, kp) for qg in range(16) for kp in range(32)]
            pt_i = [0]

            def qk(u):
                qg, kp = units[u]
                sb = (u % NSS) * 2
                for kk in range(2):
                    kb = 2 * kp + kk
                    r0, r1 = (64, 128) if (dk == 64 and kk == 1) else (0, dk)
                    mm(sb + kk, KT[sl][r0:r1, kb * 128:(kb + 1) * 128], QT[sl][r0:r1, qg * T:(qg + 1) * T], True, True,
                       [KTB[sl], QTB[sl]])

            qk(0)
            qk(1)
            for u in range(len(units)):
                qg, kp = units[u]
                if u + 2 < len(units):
                    qk(u + 2)
                sb = (u % NSS) * 2
                k = pt_i[0] % NPT
                pt_i[0] += 1
                kseg = (2 * kp) // 16
                mcol = kseg * 4 + qg // 4
                P.add('act', lambda e, sb=sb, k=k, mcol=mcol: e.activation(
                    out=PT[k], in_=PSP[sb // 2], func=AF.Exp, bias=maskb[:, mcol:mcol + 1], scale=scale),
                    reads=[PSB[sb], PSB[sb + 1], B_const], writes=[PTB[k]])
                obs = obanks_for_qg(qg)
                for kk in range(2):
                    kb = 2 * kp + kk
                    for (getl, ob) in zip(pv, obs):
                        mm(ob, getl(kb), PT[k][:, kk * T:(kk + 1) * T], kb == 0, kb == 63, [PTB[k]] + vreads)
                if kp == 31:
                    finalize(qg, obs)

        va_ones_done = [False]

        def ensure_ones():
            for sl in range(2):
                va3 = VA[sl].rearrange("p (k n) -> p k n", n=128)
                vb3 = VBt[sl].rearrange("p (k n) -> p k n", n=128)
                P.add('dve', lambda e, va3=va3: e.memset(va3[:, :, 64:128], 1.0), writes=[VAB[sl]])
                P.add('dve', lambda e, vb3=vb3: e.memset(vb3[:, :, 0:64], 1.0), writes=[VBB[sl]])

        def dv64_load(spec):
            qsrcs, ksrcs, vsrc = spec[0], spec[1], spec[2]
            sl = load_qk(qsrcs, ksrcs)
            set_i[0] += 1
            va3 = VA[sl].rearrange("p (k n) -> p k n", n=128)
            P.add('sp', lambda e: e.dma_start(out=va3[:, :, 0:64], in_=vsrc), writes=[VAB[sl]], chan=VCH[sl])
            return sl

        def dv64_compute(spec, sl):
            dk, scale, out_chunk, out_row0, obase = spec[3:]
            va3 = VA[sl].rearrange("p (k n) -> p k n", n=128)

            def fin(qg, obs):
                ob = obs[0]
                r = qg % 2
                P.add('dve', lambda e: e.reciprocal(out=REC[r][64:128, :], in_=bank(ob)[64:128, :]), reads=[PSB[ob]], writes=[RECB[r]])
                sv, sb_, sch = ost()
                P.add('dve', lambda e: e.tensor_tensor(out=sv[0:64, :], in0=bank(ob)[0:64, :], in1=REC[r][64:128, :], op=ALU.mult),
                      reads=[PSB[ob], RECB[r]], writes=[sb_])
                P.add('sp', lambda e: e.dma_start(out=oT_d[out_chunk][out_row0:out_row0 + 64, qg * T:(qg + 1) * T], in_=sv[0:64, :]),
                      reads=[sb_], chan=sch)

            attn_units(sl, dk, scale, [lambda kb: va3[:, kb, :]], lambda qg: [6 + (qg + obase) % 2], fin, [VAB[sl]])

        def attn_dv64_list(specs):
            sl = dv64_load(specs[0])
            for m in range(len(specs)):
                nsl = dv64_load(specs[m + 1]) if m + 1 < len(specs) else None
                dv64_compute(specs[m], sl)
                P.cut()
                sl = nsl

        def attn_diff(j, h):
            sls = []
            for w in range(2):
                sl = load_qk([(0, 64, qT_d[8 + 2 * h + w][0:64]), (64, 128, qT_d[8 + 2 * h + w][0:64])],
                             [(0, 64, kT_d[2 + 2 * h + w][0:64]), (64, 128, kT_d[2 + 2 * h + w][0:64])])
                set_i[0] += 1
                sls.append(sl)
            vs = h % 2
            va3 = VA[vs].rearrange("p (k n) -> p k n", n=128)
            P.add('sp', lambda e: e.dma_start(out=va3, in_=v128_d[h]), writes=[VAB[vs]], chan=VCH[vs])
            for qg in range(16):
                if qg % 4 == 0:
                    P.cut()
                for w in range(2):
                    sl = sls[w]
                    ob = 6

                    def qk(kp, sl=sl):
                        sb = (kp % 3) * 2
                        for kk in range(2):
                            kb = 2 * kp + kk
                            r0 = 64 * kk
                            mm(sb + kk, KT[sl][r0:r0 + 64, kb * 128:(kb + 1) * 128], QT[sl][r0:r0 + 64, qg * T:(qg + 1) * T], True, True,
                               [KTB[sl], QTB[sl]])
                    qk(0)
                    qk(1)
                    first_rs = True
                    for kp in range(32):
                        if kp + 2 < 32:
                            qk(kp + 2)
                        sb = (kp % 3) * 2
                        k = (kp + w) % NPT
                        mcol = ((2 * kp) // 16) * 4 + qg // 4
                        P.add('act', lambda e, sb=sb, k=k, mcol=mcol: e.activation(
                            out=PT[k], in_=PSP[sb // 2], func=AF.Exp, bias=maskb[:, mcol:mcol + 1], scale=0.125),
                            reads=[PSB[sb], PSB[sb + 1], B_const], writes=[PTB[k]])
                        ai = w
                        if kp % 3 != 2:
                            if kp == 0:
                                P.add('dve', lambda e, ai=ai, k=k: e.tensor_copy(ACC[ai], PT[k]), reads=[PTB[k]], writes=[ACCB[ai]])
                            else:
                                P.add('dve', lambda e, ai=ai, k=k: e.tensor_tensor(out=ACC[ai], in0=ACC[ai], in1=PT[k], op=ALU.add),
                                      reads=[PTB[k]], writes=[ACCB[ai]])
                        for kk in range(2):
                            kb = 2 * kp + kk
                            mm(ob, va3[:, kb, :], PT[k][:, kk * T:(kk + 1) * T], kb == 0, kb == 63, [PTB[k], VAB[vs]])
                        if kp % 3 == 2:
                            for kk in range(2):
                                mm(7, ones_bf, PT[k][:, kk * T:(kk + 1) * T], first_rs, False, [PTB[k], B_const])
                                first_rs = False
                    for hh in range(2):
                        mm(7, ones_f, ACC[w][:, hh * T:(hh + 1) * T], False, hh == 1, [ACCB[w], B_const])
                    ov, ovb = (O1, O1B) if w == 0 else (O2, O2B)
                    P.add('dve', lambda e, w=w: e.reciprocal(out=REC[w], in_=bank(7)), reads=[PSB[7]], writes=[RECB[w]])
                    P.add('dve', lambda e, w=w, ob=ob, ov=ov: e.tensor_tensor(out=ov, in0=bank(ob), in1=REC[w], op=ALU.mult),
                          reads=[PSB[ob], RECB[w]], writes=[ovb])
                P.add('dve', lambda e: e.scalar_tensor_tensor(out=O1, in0=O2, scalar=lamc[:, j * 4 + 1:j * 4 + 2], in1=O1, op0=ALU.mult, op1=ALU.add),
                      reads=[O2B, B_mod], writes=[O1B])
                P.add('act', lambda e: e.activation(out=ASQ, in_=O1, func=AF.Square), reads=[O1B], writes=[ASQB])
                mm(7, ones_bf, ASQ, True, True, [ASQB, B_const])
                P.add('act', lambda e: e.activation(out=ART, in_=bank(7), func=AF.Sqrt, bias=epsc, scale=1.0 / 128), reads=[PSB[7], B_const], writes=[ARTB])
                P.add('dve', lambda e: e.reciprocal(out=ARS, in_=ART), reads=[ARTB], writes=[ARSB])
                sv, sb_, sch = ost()
                P.add('dve', lambda e, sv=sv: e.scalar_tensor_tensor(out=sv, in0=O1, scalar=sublng[:, j:j + 1], in1=ARS, op0=ALU.mult, op1=ALU.mult),
                      reads=[O1B, ARSB, B_mod], writes=[sb_])
                P.add('sp', lambda e, sv=sv, qg=qg: e.dma_start(out=oT_d[4 + h][:, qg * T:(qg + 1) * T], in_=sv), reads=[sb_], chan=sch)

        def pool_branch(j):
            pw = poolw.rearrange("p (g n) -> p g n", n=128)
            post_i = 0
            for g, w in enumerate((2, 4, 8, 16)):
                P.cut()
                for seg in range(4):
                    t0 = seg * 2048
                    P.add('sp', lambda e, g=g, t0=t0: e.dma_start(out=PZ[:, 8:8 + 2048], in_=pz_d[g][:, t0:t0 + 2048]), writes=[PZB], chan=PZCH)
                    P.add('sp', lambda e, g=g, t0=t0: e.dma_start(out=ICN, in_=icnt_in[g:g + 1, t0:t0 + 2048].partition_broadcast(128)), writes=[ICNB], chan=ICNCH)
                    if seg > 0:
                        P.add('sp', lambda e, g=g, t0=t0: e.dma_start(out=PZ[:, 0:8], in_=pz_d[g][:, t0 - 8:t0]), writes=[PZB], chan=HLCH)
                        P.add('dve', lambda e: e.tensor_scalar(out=PZ[:, 0:8], in0=PZ[:, 0:8], scalar1=flag[:, 0:1], scalar2=None, op0=ALU.mult),
                              reads=[B_const], writes=[PZB])
                    else:
                        P.add('dve', lambda e: e.memset(PZ[:, 0:8], 0.0), writes=[PZB])
                    if seg < 3:
                        P.add('sp', lambda e, g=g, t0=t0: e.dma_start(out=PZ[:, 2056:2064], in_=pz_d[g][:, t0 + 2048:t0 + 2056]), writes=[PZB], chan=HRCH)
                        P.add('dve', lambda e: e.tensor_scalar(out=PZ[:, 2056:2064], in0=PZ[:, 2056:2064], scalar1=flag[:, 0:1], scalar2=None, op0=ALU.mult),
                              reads=[B_const], writes=[PZB])
                    else:
                        P.add('dve', lambda e: e.memset(PZ[:, 2056:2064], 0.0), writes=[PZB])
                    src, srcb = PZ, PZB
                    step = 1
                    n = PZW
                    bufs = [(PW1, PW1B), (PW2, PW2B)]
                    bi = 0
                    while step < w:
                        dst, dstb = bufs[bi % 2]
                        bi += 1
                        n2 = n - step
                        P.add('dve', lambda e, src=src, dst=dst, n2=n2, step=step: e.tensor_tensor(
                            out=dst[:, 0:n2], in0=src[:, 0:n2], in1=src[:, step:step + n2], op=ALU.add),
                            reads=[srcb], writes=[dstb])
                        src, srcb = dst, dstb
                        n = n2
                        step *= 2
                    o0 = 8 - w // 2
                    dst, dstb = bufs[bi % 2]
                    P.add('dve', lambda e, src=src, dst=dst, o0=o0: e.tensor_tensor(
                        out=dst[:, 0:2048], in0=src[:, o0:o0 + 2048], in1=ICN, op=ALU.mult), reads=[srcb, ICNB], writes=[dstb])
                    P.add('dve', lambda e, dst=dst: e.tensor_tensor(out=PD, in0=dst[:, 0:2048], in1=PZ[:, 8:8 + 2048], op=ALU.subtract),
                          reads=[dstb, PZB], writes=[PDB])
                    for tq in range(4):
                        bk = tq % 2
                        mm(bk, pw[:, j * 4 + g, :], PD[:, tq * T:(tq + 1) * T], True, True, [PDB, B_const])
                        k = post_i % 2
                        post_i += 1
                        P.add('act', lambda e, bk=bk, k=k, g=g: e.activation(out=POST[k], in_=bank(bk), func=AF.Identity,
                                                                           scale=smallp[:, 402 + j * 4 + g:403 + j * 4 + g]),
                              reads=[PSB[bk], B_const], writes=[POSTB[k]])
                        P.add('sp', lambda e, k=k, g=g, t0=t0, tq=tq: e.dma_start(
                            out=oT_d[4 + g][:, t0 + tq * T:t0 + (tq + 1) * T], in_=POST[k]), reads=[POSTB[k]], chan=POSTCH[k])

        def dv64_specs_even():
            return [([(0, 96, qT_d[h][0:96])], [(0, 64, kT_d[h][0:64]), (64, 96, kpe_d[:, :])], v64_d[h],
                     96, float(96 ** -0.5), h // 2, (h % 2) * 64, h) for h in range(8)]

        def dv64_specs_odd():
            return [([(0, 64, qT_d[m][0:64]), (64, 128, qT_d[m][0:64])],
                     [(0, 64, kT_d[m // 4][0:64]), (64, 128, kT_d[m // 4][0:64])], v64_d[m // 4],
                     64, 0.125, m // 2, (m % 2) * 64, m) for m in range(8)]

        def attention_phase(i):
            j = i // 2
            if i % 2 == 0:
                with contextlib.ExitStack() as pes:
                    cur_es[0] = pes
                    alloc_pool()
                    pool_branch(j)
                    P.barrier()
                    P.emit_pending(nc)
                with contextlib.ExitStack() as pes:
                    cur_es[0] = pes
                    alloc_attn()
                    ensure_ones()
                    attn_dv64_list(dv64_specs_even())
                    P.barrier()
                    P.emit_pending(nc)
            else:
                with contextlib.ExitStack() as pes:
                    cur_es[0] = pes
                    alloc_attn()
                    ensure_ones()
                    attn_dv64_list(dv64_specs_odd())
                    P.barrier()
                    for h in range(4):
                        attn_diff(j, h)
                    P.barrier()
                    P.emit_pending(nc)

        for ph in range(5):
            with contextlib.ExitStack() as pes:
                cur_es[0] = pes
                alloc_token()
                token_phase(ph)
                P.barrier()
                P.emit_pending(nc)
            if ph < 4:
                attention_phase(ph)
    return nc


_CACHE = {}


def kernel(**inputs):
    inp = {k: np.asarray(v) for k, v in inputs.items()}
    sh = _prep_shared(inp)
    pos_s = np.arange(NTOK)
    pos_p = np.arange(NTOK) % 2048
    tabs_s = _rope_tables(pos_s, pos_s // 64, pos_s % 64)
    tabs_p = _rope_tables(pos_p, pos_p // 64, pos_p % 64)
    icnt_s = _icnt(NTOK)
    icnt_p = _icnt(2048)
    in_maps = []
    for r in range(NCORES):
        m = _prep_core(inp, r, tabs_s, tabs_p, icnt_s, icnt_p)
        m.update(sh)
        in_maps.append(m)
    if 'nc' not in _CACHE:
        _CACHE['nc'] = build_program()
    nc = _CACHE['nc']
    res = run_bass_kernel_spmd(nc, in_maps, core_ids=list(range(NCORES)))
    outs = [np.asarray(r["yT"]).reshape(D, NTOK).T for r in res.results]
    y_sample = np.stack([np.ascontiguousarray(outs[r]) for r in range(4)]).astype(np.float32)
    y_prompt = np.concatenate([outs[r].reshape(4, 2048, D) for r in range(4, 8)], axis=0).astype(np.float32)
    return (np.ascontiguousarray(y_prompt), y_sample)
```
